# Optimizing a Trainium2 kernel written in Bass

```python
import math
import jax, jax.numpy as jnp
from jax import lax
import numpy as np

D_MODEL = 2048
BATCH = 1
SEQ = 16384
DEPTH = 1

HEAD_DIM = 128
GRID_W = 64
Q_BLOCK = 128
A_Q_HEADS = 8
A_KV_HEADS = 2
A_GROUP = A_Q_HEADS // A_KV_HEADS
B_HEADS = 8
WIN_ROWS_MAX = 8
WIN_COLS = 16
A_WIDTH = A_Q_HEADS * HEAD_DIM
B_WIDTH = B_HEADS * HEAD_DIM
MIX_WIDTH = A_WIDTH + B_WIDTH
A_KV_WIDTH = A_KV_HEADS * HEAD_DIM
IN_WIDTH = A_WIDTH + 2 * A_KV_WIDTH + 3 * B_WIDTH
D_FF = 4 * D_MODEL
PLE_DIM = 256
ROPE_THETA = 10000.0
ROPE_AXIS_DIM = HEAD_DIM // 2
NORM_EPS = 1e-6

kernel_name = "hybrid_gqa_axialrope_neighbourhood_attn_layer"


def rms_norm(x, g):
    xf = x.astype(jnp.float32)
    y = xf * lax.rsqrt(jnp.mean(xf * xf, axis=-1, keepdims=True) + NORM_EPS)
    return (y * g.astype(jnp.float32)).astype(x.dtype)


def rope_half(x, ang):
    n = ang.shape[-1]
    shape = (1, ang.shape[0]) + (1,) * (x.ndim - 3) + (n,)
    cos = jnp.cos(ang).reshape(shape)
    sin = jnp.sin(ang).reshape(shape)
    xf = x.astype(jnp.float32)
    x1, x2 = xf[..., :n], xf[..., n:]
    return jnp.concatenate([x1 * cos - x2 * sin, x2 * cos + x1 * sin], axis=-1).astype(x.dtype)


def axial_rope(x, ang_row, ang_col):
    return jnp.concatenate([rope_half(x[..., :ROPE_AXIS_DIM], ang_row),
                            rope_half(x[..., ROPE_AXIS_DIM:], ang_col)], axis=-1)


def to_blocks(a):
    b, s = a.shape[0], a.shape[1]
    a = a.reshape((b, s // Q_BLOCK, Q_BLOCK) + a.shape[2:])
    return jnp.moveaxis(a, 1, 0)


def from_blocks(a):
    a = jnp.moveaxis(a, 0, 1)
    return a.reshape((a.shape[0], a.shape[1] * a.shape[2]) + a.shape[3:])


def global_gqa(q, k, v):
    scale = 1.0 / math.sqrt(HEAD_DIM)

    def block(qb):
        s = jnp.einsum('bqkgd,bskd->bkgqs', qb, k).astype(jnp.float32) * scale
        w = jax.nn.softmax(s, axis=-1).astype(v.dtype)
        return jnp.einsum('bkgqs,bskd->bqkgd', w, v)

    o = lax.map(block, to_blocks(q))
    return from_blocks(o)


def neighbourhood_attn(q, k, v, rpb, rows):
    s_len = q.shape[1]
    win_r = min(WIN_ROWS_MAX, rows)
    scale = 1.0 / math.sqrt(HEAD_DIM)
    t_blocks = jnp.arange(s_len, dtype=jnp.int32).reshape(s_len // Q_BLOCK, Q_BLOCK)
    ar_r = jnp.arange(win_r, dtype=jnp.int32)
    ar_c = jnp.arange(WIN_COLS, dtype=jnp.int32)

    def block(args):
        qb, t = args
        r = t // GRID_W
        c = t % GRID_W
        r0 = jnp.clip(r - win_r // 2, 0, rows - win_r)
        c0 = jnp.clip(c - WIN_COLS // 2, 0, GRID_W - WIN_COLS)
        kr = r0[:, None] + ar_r[None, :]
        kc = c0[:, None] + ar_c[None, :]
        idx = (kr[:, :, None] * GRID_W + kc[:, None, :]).reshape(Q_BLOCK, win_r * WIN_COLS)
        dr = kr - r[:, None] + (WIN_ROWS_MAX - 1)
        dc = kc - c[:, None] + (WIN_COLS - 1)
        bias = rpb[:, dr[:, :, None], dc[:, None, :]].reshape(B_HEADS, Q_BLOCK, win_r * WIN_COLS)
        kg = k[:, idx]
        vg = v[:, idx]
        s = jnp.einsum('bqhd,bqnhd->bhqn', qb, kg).astype(jnp.float32) * scale
        s = s + bias.astype(jnp.float32)[None]
        w = jax.nn.softmax(s, axis=-1).astype(vg.dtype)
        return jnp.einsum('bhqn,bqnhd->bqhd', w, vg)

    o = lax.map(block, (to_blocks(q), t_blocks))
    return from_blocks(o)


def setup_inputs(seed: int = 0) -> dict:
    key = jax.random.key(seed)
    ks = jax.random.split(key, 20)
    f32 = jnp.float32

    def w(k, shape, fan_in):
        return jax.random.normal(k, shape, f32) * (fan_in ** -0.5)

    def gain(k, shape):
        return 1.0 + 0.05 * jax.random.normal(k, shape, f32)

    return {
        "x": jax.random.normal(ks[0], (BATCH, SEQ, D_MODEL), f32),
        "p": jax.random.normal(ks[1], (DEPTH, BATCH, SEQ, PLE_DIM), f32),
        "pre_mix_norm": gain(ks[2], (DEPTH, D_MODEL)),
        "w_in": w(ks[3], (DEPTH, D_MODEL, IN_WIDTH), D_MODEL),
        "q_norm": gain(ks[4], (DEPTH, HEAD_DIM)),
        "k_norm": gain(ks[5], (DEPTH, HEAD_DIM)),
        "rel_pos_bias": 0.1 * jax.random.normal(ks[6], (DEPTH, B_HEADS, 2 * WIN_ROWS_MAX - 1, 2 * WIN_COLS - 1), f32),
        "w_o": w(ks[7], (DEPTH, MIX_WIDTH, D_MODEL), MIX_WIDTH),
        "post_mix_norm": gain(ks[8], (DEPTH, D_MODEL)),
        "pre_mlp_norm": gain(ks[9], (DEPTH, D_MODEL)),
        "w_up": w(ks[10], (DEPTH, D_MODEL, D_FF), D_MODEL),
        "w_down": w(ks[11], (DEPTH, D_FF, D_MODEL), D_FF),
        "post_mlp_norm": gain(ks[12], (DEPTH, D_MODEL)),
        "pre_ple_norm": gain(ks[13], (DEPTH, D_MODEL)),
        "w_ple_gate": w(ks[14], (DEPTH, D_MODEL, D_MODEL), D_MODEL),
        "w_ple_proj": w(ks[15], (DEPTH, PLE_DIM, D_MODEL), PLE_DIM),
        "post_ple_norm": gain(ks[16], (DEPTH, D_MODEL)),
    }


def reference(x, p, pre_mix_norm, w_in, q_norm, k_norm, rel_pos_bias, w_o, post_mix_norm,
              pre_mlp_norm, w_up, w_down, post_mlp_norm, pre_ple_norm, w_ple_gate, w_ple_proj,
              post_ple_norm):
    b, s_len, _ = x.shape
    rows = s_len // GRID_W
    t = jnp.arange(s_len, dtype=jnp.int32)
    row = (t // GRID_W).astype(jnp.float32)
    col = (t % GRID_W).astype(jnp.float32)
    n_freq = ROPE_AXIS_DIM // 2
    freqs = ROPE_THETA ** (-jnp.arange(n_freq, dtype=jnp.float32) / n_freq)
    ang_row = row[:, None] * freqs[None, :]
    ang_col = col[:, None] * freqs[None, :]

    h = x
    for i in range(DEPTH):
        xn = rms_norm(h, pre_mix_norm[i])
        proj = xn @ w_in[i]
        o0 = A_WIDTH
        o1 = o0 + A_KV_WIDTH
        o2 = o1 + A_KV_WIDTH
        o3 = o2 + B_WIDTH
        o4 = o3 + B_WIDTH
        qa = proj[..., :o0].reshape(b, s_len, A_KV_HEADS, A_GROUP, HEAD_DIM)
        ka = proj[..., o0:o1].reshape(b, s_len, A_KV_HEADS, HEAD_DIM)
        va = proj[..., o1:o2].reshape(b, s_len, A_KV_HEADS, HEAD_DIM)
        qb = proj[..., o2:o3].reshape(b, s_len, B_HEADS, HEAD_DIM)
        kb = proj[..., o3:o4].reshape(b, s_len, B_HEADS, HEAD_DIM)
        vb = proj[..., o4:].reshape(b, s_len, B_HEADS, HEAD_DIM)

        qa = axial_rope(rms_norm(qa, q_norm[i]), ang_row, ang_col)
        ka = axial_rope(rms_norm(ka, k_norm[i]), ang_row, ang_col)
        out_a = global_gqa(qa, ka, va).reshape(b, s_len, A_WIDTH)
        out_b = neighbourhood_attn(qb, kb, vb, rel_pos_bias[i], rows).reshape(b, s_len, B_WIDTH)

        mix = jnp.concatenate([out_a, out_b], axis=-1) @ w_o[i]
        h = h + rms_norm(mix, post_mix_norm[i])

        m = rms_norm(h, pre_mlp_norm[i]) @ w_up[i]
        m = jnp.square(jax.nn.relu(m)) @ w_down[i]
        h = h + rms_norm(m, post_mlp_norm[i])

        gate = jax.nn.sigmoid(rms_norm(h, pre_ple_norm[i]) @ w_ple_gate[i])
        e = (p[i] @ w_ple_proj[i]) * gate
        h = h + rms_norm(e, post_ple_norm[i])
    return h
```

```python
import numpy as np
import concourse.bass as bass
import concourse.mybir as mybir
from concourse.bass_utils import run_bass_kernel_spmd

F32 = mybir.dt.float32
BF16 = mybir.dt.bfloat16
AF = mybir.ActivationFunctionType
ALU = mybir.AluOpType

NCORES = 8
S = 16384
D = 2048
TOK = 2048
NT = 16
NE = 20
ETOK = NE * 128
DFF = 8192
EPS = 1e-6
SCALE = 1.0 / np.sqrt(128.0)
NEG = -30000.0
DEBUG = False


class Tk:
    __slots__ = ("w", "r")

    def __init__(self):
        self.w = {}
        self.r = {}


class Tile:
    __slots__ = ("ap", "tk")

    def __init__(self, ap):
        self.ap = ap
        self.tk = Tk()


def _tk(x):
    return x.tk if isinstance(x, Tile) else x


ENGS = ("pe", "act", "dve", "pool", "sp")


class Prog:
    def __init__(self):
        self.q = {e: [] for e in ENGS}
        self.cnt = {e: 0 for e in ENGS}
        self.waited = {e: {} for e in ENGS}
        self.ring = {"sp": 28, "pool": 12}
        self.dma_i = {"sp": 0, "pool": 0}
        self.dma_last = {}
        self.bar = {e: [] for e in ENGS}

    def _collect(self, eng, reads, writes, extra, add):
        toks = list(extra) + self.bar[eng]
        self.bar[eng] = []
        for t in reads:
            toks.extend(_tk(t).w.items())
        for t in writes:
            t = _tk(t)
            if not add:
                toks.extend(t.w.items())
            toks.extend(t.r.items())
        out = []
        wd = self.waited[eng]
        for k, v in toks:
            if k == "pe" and eng == "pe":
                continue
            if wd.get(k, 0) >= v:
                continue
            wd[k] = v
            out.append((k, v))
        return out

    def _mark(self, tok, reads, writes, add):
        k, v = tok
        for t in reads:
            t = _tk(t)
            if t.r.get(k, 0) < v:
                t.r[k] = v
        for t in writes:
            t = _tk(t)
            if add:
                if t.w.get(k, 0) < v:
                    t.w[k] = v
            else:
                t.w = {k: v}
            t.r = {}

    def op(self, eng, fn, reads=(), writes=(), signal=True, extra=(), add=False):
        ws = self._collect(eng, reads, writes, extra, add)
        if signal:
            self.cnt[eng] += 1
            tok = (eng, self.cnt[eng])
            sig = (eng, 1)
        else:
            tok = (eng, self.cnt[eng] + 1)
            sig = None
        self.q[eng].append((ws, fn, sig))
        self._mark(tok, reads, writes, add)
        return tok

    def dma(self, queue, out_ap, in_ap, reads=(), writes=(), add=False):
        i = self.dma_i[queue]
        self.dma_i[queue] += 1
        n = self.ring[queue]
        key = "d_%s_%d" % (queue, i % n)
        val = 16 * (i // n + 1)
        extra = [(key, val - 16)] if i >= n else []
        ws = self._collect(queue, reads, writes, extra, add)
        self.q[queue].append((ws, (lambda e, o=out_ap, a=in_ap: e.dma_start(out=o, in_=a)), (key, 16)))
        tok = (key, val)
        self.dma_last[key] = val
        self._mark(tok, reads, writes, add)
        return tok

    def barrier(self):
        toks = [(e, self.cnt[e]) for e in ("pe", "act", "dve", "pool") if self.cnt[e] > 0]
        toks += list(self.dma_last.items())
        for e in ENGS:
            self.bar[e] = self.bar[e] + toks


def MM(out, lhsT, rhs, start, stop):
    return lambda e: e.matmul(out, lhsT, rhs, start=start, stop=stop)


def TR(out, in_, ident):
    return lambda e: e.transpose(out, in_, ident)


def ACT(out, in_, func, **kw):
    return lambda e: e.activation(out=out, in_=in_, func=func, **kw)


def TT(out, in0, in1, op):
    return lambda e: e.tensor_tensor(out=out, in0=in0, in1=in1, op=op)


def TS(out, in0, s1, s2, op0, op1=None):
    if op1 is None:
        return lambda e: e.tensor_scalar(out=out, in0=in0, scalar1=s1, scalar2=None, op0=op0)
    return lambda e: e.tensor_scalar(out=out, in0=in0, scalar1=s1, scalar2=s2, op0=op0, op1=op1)


def STT(out, in0, scalar, in1, op0, op1):
    return lambda e: e.scalar_tensor_tensor(out=out, in0=in0, scalar=scalar, in1=in1, op0=op0, op1=op1)


def CP(out, in_):
    return lambda e: e.tensor_copy(out, in_)


def RCP(out, in_):
    return lambda e: e.reciprocal(out=out, in_=in_)


def MSET(ap, v):
    return lambda e: e.memset(ap, v)


def build_program():
    nc = bass.Bass("TRN2", target_bir_lowering=False)
    P = Prog()

    def din(name, shape, dt=F32):
        return nc.dram_tensor(name, list(shape), dt, kind="ExternalInput").ap()

    def dscr(name, shape, dt=BF16):
        kind = "ExternalOutput" if (DEBUG and name in DEBUG_OUTS) else "Internal"
        return nc.dram_tensor(name, list(shape), dt, kind=kind).ap()

    x_all = din("x_all", [S, D])
    x_ext = din("x_ext", [ETOK, D])
    pT_in = din("pT", [256, TOK])
    w_in = din("w_in", [D, 4608])
    w_o = din("w_o", [D, D])
    w_up = din("w_up", [D, DFF])
    w_dn = din("w_down", [DFF, D])
    w_g = din("w_gate", [D, D])
    w_p = din("w_proj", [256, D])
    gains = din("gains", [6, 128, D])
    gq_in = din("gq", [128, 512])
    gk_in = din("gk", [128, 256])
    cos_all = din("cos_all", [S, 128])
    sin_all = din("sin_all", [S, 128])
    cos_own = din("cos_own", [TOK, 128])
    sin_own = din("sin_own", [TOK, 128])
    biasT = din("biasT", [3, 8, 128, 8, 512])
    ident_in = din("ident", [128, 128])
    y_out = nc.dram_tensor("y", [TOK, D], F32, kind="ExternalOutput").ap()

    kaT_s = dscr("kaT_s", [2, 128, S])
    va_s = dscr("va_s", [128, 2, 128, 128])
    qaT_s = dscr("qaT_s", [8, 128, TOK])
    qbT_s = dscr("qbT_s", [8, 128, TOK])
    kbT_s = dscr("kbT_s", [8, 128, ETOK])
    vb_s = dscr("vb_s", [128, 8, NE, 128])
    mixT_s = dscr("mixT_s", [16, 128, TOK])
    h1_s = dscr("h1_s", [TOK, D], F32)
    h1nT_s = dscr("h1nT_s", [4, 128, 16, 512])
    wup_s = dscr("wup_s", [D, DFF])
    wdn_s = dscr("wdn_s", [DFF, D])
    m_s = dscr("m_s", [TOK, D], F32)

    ARENA_W = 48 * 1024
    arena_cm = nc.sbuf_tensor("arena", [128, ARENA_W], F32)
    psum_cm = nc.psum_tensor("ps", [128, 8, 512], F32)
    arena = arena_cm.__enter__()
    ps = psum_cm.__enter__()
    psb = ps[:].bitcast(BF16)
    PSB = [Tk() for _ in range(8)]

    st = {"off": 0}

    def reset(off=0):
        st["off"] = off

    def sb(shape, dt):
        n = int(np.prod(shape))
        nb = n * (2 if dt == BF16 else 4)
        nb = (nb + 63) // 64 * 64
        off = st["off"]
        st["off"] = off + nb
        assert st["off"] <= ARENA_W * 4, "sbuf arena overflow %d" % st["off"]
        v = arena[:, off // 4:(off + nb) // 4]
        if dt == BF16:
            v = v.bitcast(BF16)
        v = v[:, 0:n]
        if len(shape) == 2:
            v = v.rearrange("p (a b) -> p a b", a=shape[0])
        elif len(shape) == 3:
            v = v.rearrange("p (a b c) -> p a b c", a=shape[0], b=shape[1])
        return Tile(v)

    ident = sb([128], BF16)
    ones = sb([128], F32)
    P.dma("pool", ident.ap, ident_in[:, :], writes=[ident])
    P.op("pool", MSET(ones.ap, 1.0), writes=[ones])
    BASE = st["off"]

    WUP_TK = [Tk() for _ in range(16)]
    WDN_TK = [Tk() for _ in range(64)]

    def issue_weight_casts():
        for c in range(16):
            P.dma("pool", wup_s[c * 128:(c + 1) * 128, :], w_up[c * 128:(c + 1) * 128, :], writes=[WUP_TK[c]])
        for c in range(64):
            P.dma("pool", wdn_s[c * 128:(c + 1) * 128, :], w_dn[c * 128:(c + 1) * 128, :], writes=[WDN_TK[c]])

    def rstd_ops(ss, r, n):
        P.op("dve", TS(r.ap, ss.ap, 1.0 / n, EPS, ALU.mult, ALU.add), reads=[ss], writes=[r])
        P.op("act", ACT(r.ap, r.ap, AF.Sqrt), reads=[r], writes=[r])
        P.op("dve", RCP(r.ap, r.ap), reads=[r], writes=[r])

    def norm_to_bf16(src, gain, junk, ss, r, xg, mul_eng="pool", tmp32=None):
        P.op("act", ACT(junk.ap, src.ap, AF.Square, accum_out=ss.ap[:, 0:1]), reads=[src], writes=[junk, ss])
        rstd_ops(ss, r, D)
        if mul_eng == "pool":
            P.op("pool", TT(tmp32.ap, src.ap, gain.ap, ALU.mult), reads=[src, gain], writes=[tmp32])
            P.op("dve", TS(xg.ap, tmp32.ap, r.ap[:, 0:1], None, ALU.mult), reads=[tmp32, r], writes=[xg])
        else:
            P.op("dve", STT(xg.ap, src.ap, r.ap[:, 0:1], gain.ap, ALU.mult, ALU.mult), reads=[src, r, gain], writes=[xg])

    def transpose16(xg, dst_fn, dst_tiles, banks=(6, 7)):
        for half in range(2):
            bk = banks[half]
            for j in range(8):
                c = half * 8 + j
                P.op("pe", TR(psb[:, bk, j * 128:(j + 1) * 128], xg.ap[:, c * 128:(c + 1) * 128], ident.ap),
                     reads=[xg, ident], writes=[PSB[bk]], signal=(j == 7))
            eng = "act" if half == 0 else "dve"
            src = psb[:, bk, :].rearrange("p (c t) -> p c t", c=8)
            fn = ACT(dst_fn(half * 8, half * 8 + 8), src, AF.Copy) if eng == "act" else CP(dst_fn(half * 8, half * 8 + 8), src)
            P.op(eng, fn, reads=[PSB[bk]], writes=dst_tiles)

    def norm_rope(src_ap, src_tk, nh, gain, cs, sn, kg, t1, t2, hss, hr, junk, outb):
        W = nh * 128
        for h in range(nh):
            P.op("act", ACT(junk.ap[:, 0:128], src_ap[:, h * 128:(h + 1) * 128], AF.Square, accum_out=hss.ap[:, h:h + 1]),
                 reads=[src_tk], writes=[junk, hss], add=True)
        rstd_ops(hss, hr, 128)
        P.op("dve", TT(kg.ap[:, 0:W], src_ap, gain.ap[:, 0:W], ALU.mult), reads=[src_tk, gain], writes=[kg])
        kg3 = kg.ap[:, 0:W].rearrange("p (h d) -> p h d", h=nh)
        t13 = t1.ap[:, 0:W].rearrange("p (h d) -> p h d", h=nh)
        P.op("dve", TT(t13, kg3, cs.ap.unsqueeze(1).broadcast_to([128, nh, 128]), ALU.mult), reads=[kg, cs], writes=[t1])
        kg5 = kg.ap[:, 0:W].rearrange("p (h s f i) -> p h s f i", h=nh, s=2, f=2)
        t25 = t2.ap[:, 0:W].rearrange("p (h s f i) -> p h s f i", h=nh, s=2, f=2)
        sn4 = sn.ap.rearrange("p (s f i) -> p s f i", s=2, f=2)
        for f in range(2):
            P.op("pool", TT(t25[:, :, :, f, :], kg5[:, :, :, 1 - f, :],
                            sn4[:, :, f, :].unsqueeze(1).broadcast_to([128, nh, 2, 32]), ALU.mult),
                 reads=[kg, sn], writes=[t2], add=(f == 1))
        P.op("dve", TT(t1.ap[:, 0:W], t1.ap[:, 0:W], t2.ap[:, 0:W], ALU.add), reads=[t1, t2], writes=[t1])
        for h in range(nh):
            P.op("act", ACT(outb.ap[:, h, :], t1.ap[:, h * 128:(h + 1) * 128], AF.Copy, scale=hr.ap[:, h:h + 1]),
                 reads=[t1, hr], writes=[outb], add=(h > 0))

    def phase1():
        reset(BASE)
        wkv = sb([16, 512], BF16)
        gpm = sb([D], F32)
        gk = sb([256], F32)
        xt = [sb([D], F32) for _ in range(2)]
        xg = [sb([D], BF16) for _ in range(2)]
        xnT = [sb([16, 128], BF16) for _ in range(2)]
        junk = sb([D], BF16)
        tmp32 = sb([D], F32)
        cs = [sb([128], F32) for _ in range(2)]
        sn = [sb([128], F32) for _ in range(2)]
        ss = [sb([1], F32) for _ in range(2)]
        r = [sb([1], F32) for _ in range(2)]
        hss = [sb([2], F32) for _ in range(2)]
        hr = [sb([2], F32) for _ in range(2)]
        kg = sb([256], F32)
        t1 = sb([256], F32)
        t2 = sb([256], F32)
        kb = [sb([2, 128], BF16) for _ in range(2)]
        kst = [sb([2, 512], BF16) for _ in range(2)]
        vst = [sb([2, 4, 128], BF16) for _ in range(2)]
        KA_TK = [Tk() for _ in range(32)]
        VA_TK = [Tk() for _ in range(32)]

        P.dma("pool", wkv.ap, w_in[:, 1024:1536].rearrange("(c p) n -> p c n", p=128), writes=[wkv])
        P.dma("sp", gpm.ap, gains[0], writes=[gpm])
        P.dma("sp", gk.ap, gk_in[:, :], writes=[gk])
        issue_weight_casts()

        for i in range(128):
            b = i % 2
            g4 = i // 4
            gb = g4 % 2
            P.dma("sp", xt[b].ap, x_all[i * 128:(i + 1) * 128, :], writes=[xt[b]])
            P.dma("sp", cs[b].ap, cos_all[i * 128:(i + 1) * 128, :], writes=[cs[b]])
            P.dma("sp", sn[b].ap, sin_all[i * 128:(i + 1) * 128, :], writes=[sn[b]])
            norm_to_bf16(xt[b], gpm, junk, ss[b], r[b], xg[b], tmp32=tmp32)
            transpose16(xg[b], lambda c0, c1, b=b: xnT[b].ap[:, c0:c1, :], [xnT[b]], banks=(6, 7))
            bank = i % 2
            for c in range(16):
                P.op("pe", MM(ps[:, bank, :], xnT[b].ap[:, c, :], wkv.ap[:, c, :], c == 0, c == 15),
                     reads=[xnT[b], wkv], writes=[PSB[bank]], signal=(c == 15))
            P.op("act", ACT(vst[gb].ap[:, :, i % 4, :], ps[:, bank, 256:512].rearrange("p (k d) -> p k d", k=2), AF.Copy),
                 reads=[PSB[bank]], writes=[vst[gb]], add=(i % 4 != 0))
            norm_rope(ps[:, bank, 0:256], PSB[bank], 2, gk, cs[b], sn[b], kg, t1, t2, hss[b], hr[b], junk, kb[b])
            tb = 4 + (i % 2)
            for h in range(2):
                P.op("pe", TR(psb[:, tb, h * 128:(h + 1) * 128], kb[b].ap[:, h, :], ident.ap),
                     reads=[kb[b], ident], writes=[PSB[tb]], signal=(h == 1))
            P.op("dve", CP(kst[gb].ap[:, :, (i % 4) * 128:(i % 4 + 1) * 128], psb[:, tb, 0:256].rearrange("p (k t) -> p k t", k=2)),
                 reads=[PSB[tb]], writes=[kst[gb]], add=(i % 4 != 0))
            if i % 4 == 3:
                P.dma("sp", kaT_s[:, :, g4 * 512:(g4 + 1) * 512].rearrange("k d t -> d k t"), kst[gb].ap,
                      reads=[kst[gb]], writes=[KA_TK[g4]])
                P.dma("sp", va_s[:, :, g4 * 4:(g4 + 1) * 4, :], vst[gb].ap, reads=[vst[gb]], writes=[VA_TK[g4]])
        return KA_TK, VA_TK

    def phase2():
        reset(BASE)
        xnT = sb([16, ETOK], BF16)
        XN_TK = [Tk() for _ in range(NE)]
        gpm = sb([D], F32)
        gq = sb([512], F32)
        mark = st["off"]
        xt = [sb([D], F32) for _ in range(2)]
        xg = [sb([D], BF16) for _ in range(2)]
        junk = sb([D], BF16)
        tmp32 = sb([D], F32)
        ss = [sb([1], F32) for _ in range(2)]
        r = [sb([1], F32) for _ in range(2)]

        P.dma("sp", gpm.ap, gains[0], writes=[gpm])
        P.dma("sp", gq.ap, gq_in[:, :], writes=[gq])
        for e in range(NE):
            b = e % 2
            P.dma("sp", xt[b].ap, x_ext[e * 128:(e + 1) * 128, :], writes=[xt[b]])
            norm_to_bf16(xt[b], gpm, junk, ss[b], r[b], xg[b], tmp32=tmp32)
            transpose16(xg[b], lambda c0, c1, e=e: xnT.ap[:, c0:c1, e * 128:(e + 1) * 128], [XN_TK[e]],
                        banks=(6, 7) if e % 2 == 0 else (4, 5))
        P.barrier()
        reset(mark)
        wblk = [sb([16, 512], BF16) for _ in range(2)]
        stage = [sb([4 * ETOK], BF16) for _ in range(2)]
        junk = sb([D], BF16)
        cs = [sb([128], F32) for _ in range(2)]
        sn = [sb([128], F32) for _ in range(2)]
        hss = [sb([4], F32) for _ in range(2)]
        hr = [sb([4], F32) for _ in range(2)]
        kg = sb([512], F32)
        t1 = sb([512], F32)
        t2 = sb([512], F32)
        qb_ = [sb([4, 128], BF16) for _ in range(2)]

        blocks = [("qa", 0, 0), ("qa", 1, 512), ("qb", 0, 1536), ("qb", 1, 2048),
                  ("kb", 0, 2560), ("kb", 1, 3072), ("vb", 0, 3584), ("vb", 1, 4096)]
        bankrot = [0]

        def nbank():
            bk = bankrot[0] % 4
            bankrot[0] += 1
            return bk

        OUT_TK = {"qa": [Tk(), Tk()], "qb": [Tk(), Tk()], "kb": [Tk(), Tk()], "vb": [Tk(), Tk()]}
        for bi, (kind, half, col) in enumerate(blocks):
            wb = wblk[bi % 2]
            sg = stage[bi % 2]
            P.dma("pool", wb.ap, w_in[:, col:col + 512].rearrange("(c p) n -> p c n", p=128), writes=[wb])
            if kind == "qa":
                sgv = sg.ap[:, 0:4 * TOK].rearrange("p (h t) -> p h t", h=4)
                for t in range(NT):
                    e = t + 2
                    b = t % 2
                    P.dma("sp", cs[b].ap, cos_own[t * 128:(t + 1) * 128, :], writes=[cs[b]])
                    P.dma("sp", sn[b].ap, sin_own[t * 128:(t + 1) * 128, :], writes=[sn[b]])
                    bk = nbank()
                    for c in range(16):
                        P.op("pe", MM(ps[:, bk, :], xnT.ap[:, c, e * 128:(e + 1) * 128], wb.ap[:, c, :], c == 0, c == 15),
                             reads=[XN_TK[e], wb], writes=[PSB[bk]], signal=(c == 15))
                    norm_rope(ps[:, bk, :], PSB[bk], 4, gq, cs[b], sn[b], kg, t1, t2, hss[b], hr[b], junk, qb_[b])
                    tb = 4 + (t % 2)
                    for h in range(4):
                        P.op("pe", TR(psb[:, tb, h * 128:(h + 1) * 128], qb_[b].ap[:, h, :], ident.ap),
                             reads=[qb_[b], ident], writes=[PSB[tb]], signal=(h == 3))
                    P.op("dve", CP(sgv[:, :, t * 128:(t + 1) * 128], psb[:, tb, 0:512].rearrange("p (h t) -> p h t", h=4)),
                         reads=[PSB[tb]], writes=[sg], add=(t > 0))
                P.dma("sp", qaT_s[half * 4:(half + 1) * 4].rearrange("h d t -> d h t"), sgv, reads=[sg], writes=[OUT_TK[kind][half]])
            elif kind in ("qb", "kb"):
                ntok = TOK if kind == "qb" else ETOK
                e0 = 2 if kind == "qb" else 0
                sgv = sg.ap[:, 0:4 * ntok].rearrange("p (h t) -> p h t", h=4)
                first = True
                for j in range(4):
                    for sbk in range(ntok // 512):
                        tk0 = e0 * 128 + sbk * 512
                        bk = nbank()
                        for c in range(16):
                            P.op("pe", MM(ps[:, bk, :], wb.ap[:, c, j * 128:(j + 1) * 128], xnT.ap[:, c, tk0:tk0 + 512], c == 0, c == 15),
                                 reads=[wb] + XN_TK[tk0 // 128:tk0 // 128 + 4], writes=[PSB[bk]], signal=(c == 15))
                        eng = "act" if (sbk % 2 == 0) else "dve"
                        dst = sgv[:, j, sbk * 512:(sbk + 1) * 512]
                        fn = ACT(dst, ps[:, bk, :], AF.Copy) if eng == "act" else CP(dst, ps[:, bk, :])
                        P.op(eng, fn, reads=[PSB[bk]], writes=[sg], add=(not first))
                        first = False
                dst_s = qbT_s if kind == "qb" else kbT_s
                P.dma("sp", dst_s[half * 4:(half + 1) * 4].rearrange("h d t -> d h t"), sgv, reads=[sg], writes=[OUT_TK[kind][half]])
            else:
                sgv = sg.ap[:, 0:NE * 512].rearrange("p (h e d) -> p h e d", h=4, e=NE)
                for e in range(NE):
                    bk = nbank()
                    for c in range(16):
                        P.op("pe", MM(ps[:, bk, :], xnT.ap[:, c, e * 128:(e + 1) * 128], wb.ap[:, c, :], c == 0, c == 15),
                             reads=[XN_TK[e], wb], writes=[PSB[bk]], signal=(c == 15))
                    eng = "act" if (e % 2 == 0) else "dve"
                    dst = sgv[:, :, e, :]
                    src = ps[:, bk, :].rearrange("p (h d) -> p h d", h=4)
                    fn = ACT(dst, src, AF.Copy) if eng == "act" else CP(dst, src)
                    P.op(eng, fn, reads=[PSB[bk]], writes=[sg], add=(e > 0))
                P.dma("sp", vb_s[:, half * 4:(half + 1) * 4, :, :], sgv, reads=[sg], writes=[OUT_TK[kind][half]])
        return OUT_TK

    def attn_iter(it, kT_fn, v_fn, k_reads, q_ap, q_reads, ngroups, bufs, out_dst, out_tk, bias=None):
        pT, accD, accP, sbias, rl, ost = bufs["pT"], bufs["accD"], bufs["accP"], bufs["sbias"], bufs["rl"], bufs["ost"]
        ob = 4 + (it % 2)
        ostb = ost[it % 2]

        def qk(j):
            s0 = (j % 2) * 2
            for u in range(2):
                P.op("pe", MM(ps[:, s0 + u, :], kT_fn(2 * j + u), q_ap, True, True),
                     reads=k_reads + q_reads, writes=[PSB[s0 + u]], signal=(u == 1))

        def softmax_pv(j):
            s0 = (j % 2) * 2
            pt = pT[j % 3]
            src = ps[:, s0:s0 + 2, :]
            if bias is not None:
                sbt = sbias[j % 2]
                P.op("dve", STT(sbt.ap, src, float(SCALE), bias[0][:, 2 * j:2 * j + 2, :], ALU.mult, ALU.add),
                     reads=[PSB[s0], PSB[s0 + 1], bias[1]], writes=[sbt])
                P.op("act", ACT(pt.ap, sbt.ap, AF.Exp), reads=[sbt], writes=[pt])
            else:
                P.op("act", ACT(pt.ap, src, AF.Exp, scale=float(SCALE)), reads=[PSB[s0], PSB[s0 + 1]], writes=[pt])
            for u in range(2):
                P.op("pe", MM(ps[:, ob, :], v_fn(2 * j + u), pt.ap[:, u, :], (j == 0 and u == 0), (j == ngroups - 1 and u == 1)),
                     reads=k_reads + [pt], writes=[PSB[ob]], signal=(u == 1))
            eng, acc = ("dve", accD) if j % 2 == 0 else ("pool", accP)
            if j < 2:
                P.op(eng, CP(acc.ap, pt.ap), reads=[pt], writes=[acc])
            else:
                P.op(eng, TT(acc.ap, acc.ap, pt.ap, ALU.add), reads=[pt, acc], writes=[acc])

        qk(0)
        for j in range(ngroups):
            if j + 1 < ngroups:
                qk(j + 1)
            softmax_pv(j)
        P.op("dve", TT(accD.ap, accD.ap, accP.ap, ALU.add), reads=[accD, accP], writes=[accD])
        P.op("dve", TT(accD.ap[:, 0, :], accD.ap[:, 0, :], accD.ap[:, 1, :], ALU.add), reads=[accD], writes=[accD])
        P.op("pe", MM(ps[:, 6, :], ones.ap, accD.ap[:, 0, :], True, True), reads=[accD, ones], writes=[PSB[6]])
        P.op("dve", RCP(rl.ap, ps[:, 6, :]), reads=[PSB[6]], writes=[rl])
        P.op("dve", TT(ostb.ap, ps[:, ob, :], rl.ap, ALU.mult), reads=[PSB[ob], rl], writes=[ostb])
        P.dma("sp", out_dst, ostb.ap, reads=[ostb], writes=[out_tk], add=True)

    def attn_bufs():
        return {
            "pT": [sb([2, 512], BF16) for _ in range(3)],
            "accD": sb([2, 512], F32), "accP": sb([2, 512], F32),
            "sbias": [sb([2, 512], F32) for _ in range(2)],
            "rl": sb([512], F32),
            "ost": [sb([512], BF16) for _ in range(2)],
        }

    MIX_TK = [Tk() for _ in range(16)]

    def phase3(KA_TK, VA_TK, QA_TK):
        reset(BASE)
        kT = sb([S], BF16)
        vv = sb([128, 128], BF16)
        qT = sb([4, TOK], BF16)
        bufs = attn_bufs()
        it = 0
        for kv in range(2):
            for c4 in range(4):
                P.dma("sp", kT.ap[:, c4 * 4096:(c4 + 1) * 4096], kaT_s[kv, :, c4 * 4096:(c4 + 1) * 4096],
                      reads=KA_TK[c4 * 8:(c4 + 1) * 8], writes=[kT], add=(c4 > 0))
                P.dma("sp", vv.ap[:, c4 * 32:(c4 + 1) * 32, :], va_s[:, kv, c4 * 32:(c4 + 1) * 32, :],
                      reads=VA_TK[c4 * 8:(c4 + 1) * 8], writes=[vv], add=(c4 > 0))
            P.dma("sp", qT.ap, qaT_s[kv * 4:(kv + 1) * 4].rearrange("h d t -> d h t"), reads=[QA_TK[kv]], writes=[qT])
            for g in range(4):
                h = kv * 4 + g
                for qb in range(4):
                    attn_iter(it, lambda kt: kT.ap[:, kt * 128:(kt + 1) * 128], lambda kt: vv.ap[:, kt, :], [kT, vv],
                              qT.ap[:, g, qb * 512:(qb + 1) * 512], [qT], 64, bufs,
                              mixT_s[h, :, qb * 512:(qb + 1) * 512], MIX_TK[h])
                    it += 1

    def phase4(OUT_TK):
        reset(BASE)
        kT = [sb([ETOK], BF16) for _ in range(2)]
        vv = [sb([NE, 128], BF16) for _ in range(2)]
        qT = [sb([TOK], BF16) for _ in range(2)]
        bt = [sb([8, 512], F32) for _ in range(2)]
        bufs = attn_bufs()
        it = 0
        for h in range(8):
            hb = h % 2
            P.dma("sp", kT[hb].ap, kbT_s[h], reads=[OUT_TK["kb"][h // 4]], writes=[kT[hb]])
            P.dma("sp", vv[hb].ap, vb_s[:, h, :, :], reads=[OUT_TK["vb"][h // 4]], writes=[vv[hb]])
            P.dma("sp", qT[hb].ap, qbT_s[h], reads=[OUT_TK["qb"][h // 4]], writes=[qT[hb]])
            for qb in range(4):
                typ = 0 if qb == 0 else (2 if qb == 3 else 1)
                btt = bt[it % 2]
                P.dma("sp", btt.ap, biasT[typ, h], writes=[btt])
                attn_iter(it, lambda kt, hb=hb, qb=qb: kT[hb].ap[:, (4 * qb + kt) * 128:(4 * qb + kt + 1) * 128],
                          lambda kt, hb=hb, qb=qb: vv[hb].ap[:, 4 * qb + kt, :], [kT[hb], vv[hb]],
                          qT[hb].ap[:, qb * 512:(qb + 1) * 512], [qT[hb]], 4, bufs,
                          mixT_s[8 + h, :, qb * 512:(qb + 1) * 512], MIX_TK[8 + h], bias=(btt.ap, btt))
                it += 1

    H1_TK = [Tk() for _ in range(NT)]
    H1N_TK = [Tk() for _ in range(4)]

    def phase5():
        reset(BASE)
        wo = sb([16, D], BF16)
        mixT = [sb([16, 512], BF16) for _ in range(1)]
        gpo = sb([D], F32)
        gpl = sb([D], F32)
        xt = [sb([D], F32) for _ in range(2)]
        tmp = sb([D], F32)
        h1 = [sb([D], F32) for _ in range(2)]
        xg = sb([D], BF16)
        junk = sb([D], BF16)
        hst = [sb([16, 512], BF16) for _ in range(2)]
        ss = [sb([1], F32) for _ in range(2)]
        r = [sb([1], F32) for _ in range(2)]
        ss1 = [sb([1], F32) for _ in range(2)]
        r1 = [sb([1], F32) for _ in range(2)]
        for n in range(4):
            P.dma("pool", wo.ap[:, :, n * 512:(n + 1) * 512], w_o[:, n * 512:(n + 1) * 512].rearrange("(c p) n -> p c n", p=128),
                  writes=[wo], add=(n > 0))
        P.dma("sp", gpo.ap, gains[1], writes=[gpo])
        P.dma("sp", gpl.ap, gains[2], writes=[gpl])
        for t in range(NT):
            sbk, tt = t // 4, t % 4
            mx = mixT[0]
            b = t % 2
            if tt == 0:
                P.dma("sp", mx.ap, mixT_s[:, :, sbk * 512:(sbk + 1) * 512].rearrange("c d t -> d c t"), reads=MIX_TK, writes=[mx])
            P.dma("sp", xt[b].ap, x_ext[(t + 2) * 128:(t + 3) * 128, :], writes=[xt[b]])
            b0 = 0 if t % 2 == 0 else 2
            for n in range(4):
                for c in range(16):
                    P.op("pe", MM(ps[:, n, :], mx.ap[:, c, tt * 128:(tt + 1) * 128], wo.ap[:, c, n * 512:(n + 1) * 512], c == 0, c == 15),
                         reads=[mx, wo], writes=[PSB[n]], signal=(c == 15))
            pall = ps[:, 0:4, :]
            P.op("act", ACT(junk.ap.rearrange("p (a b) -> p a b", a=4), pall, AF.Square, accum_out=ss[b].ap[:, 0:1]),
                 reads=PSB[0:4], writes=[junk, ss[b]])
            rstd_ops(ss[b], r[b], D)
            P.op("dve", STT(tmp.ap.rearrange("p (a b) -> p a b", a=4), pall, r[b].ap[:, 0:1], gpo.ap.rearrange("p (a b) -> p a b", a=4), ALU.mult, ALU.mult),
                 reads=PSB[0:4] + [r[b], gpo], writes=[tmp])
            P.op("pool", TT(h1[b].ap, tmp.ap, xt[b].ap, ALU.add), reads=[tmp, xt[b]], writes=[h1[b]])
            P.dma("sp", h1_s[t * 128:(t + 1) * 128, :], h1[b].ap, reads=[h1[b]], writes=[H1_TK[t]])
            norm_to_bf16(h1[b], gpl, junk, ss1[b], r1[b], xg, mul_eng="dve")
            hs = hst[sbk % 2]
            transpose16(xg, lambda c0, c1, hs=hs, tt=tt: hs.ap[:, c0:c1, tt * 128:(tt + 1) * 128], [hs], banks=(6, 7) if t % 2 == 0 else (4, 5))
            if tt == 3:
                P.dma("sp", h1nT_s[sbk], hs.ap, reads=[hs], writes=[H1N_TK[sbk]])

    M_TK = [Tk() for _ in range(4)]

    def phase6():
        reset(BASE)
        aT = sb([64, 512], BF16)
        AT_TK = [Tk() for _ in range(64)]
        hnT = [sb([16, 512], BF16) for _ in range(2)]
        wu = [sb([16, 512], BF16) for _ in range(2)]
        wd = [sb([4, 1024], BF16) for _ in range(3)]
        rt = [sb([512], F32) for _ in range(2)]
        mst = [sb([1024], F32) for _ in range(2)]
        k_up = 0
        k_dn = 0
        k_ms = 0
        for sbk in range(4):
            hn = hnT[sbk % 2]
            P.dma("sp", hn.ap, h1nT_s[sbk], reads=[H1N_TK[sbk]], writes=[hn])
            for fg in range(16):
                w = wu[k_up % 2]
                k_up += 1
                P.dma("sp", w.ap, wup_s[:, fg * 512:(fg + 1) * 512].rearrange("(c p) n -> p c n", p=128), reads=WUP_TK, writes=[w])
                for j in range(4):
                    fc = fg * 4 + j
                    bk = fc % 4
                    for c in range(16):
                        P.op("pe", MM(ps[:, bk, :], w.ap[:, c, j * 128:(j + 1) * 128], hn.ap[:, c, :], c == 0, c == 15),
                             reads=[w, hn], writes=[PSB[bk]], signal=(c == 15))
                    rtt = rt[fc % 2]
                    P.op("act", ACT(rtt.ap, ps[:, bk, :], AF.Relu), reads=[PSB[bk]], writes=[rtt])
                    P.op("dve" if fc % 2 == 0 else "pool", TT(aT.ap[:, fc, :], rtt.ap, rtt.ap, ALU.mult), reads=[rtt], writes=[AT_TK[fc]])
            for hf in range(2):
                for fgd in range(16):
                    w = wd[k_dn % 3]
                    k_dn += 1
                    P.dma("sp", w.ap, wdn_s[fgd * 512:(fgd + 1) * 512, hf * 1024:(hf + 1) * 1024].rearrange("(j p) n -> p j n", p=128),
                          reads=WDN_TK[fgd * 4:(fgd + 1) * 4], writes=[w])
                    for j in range(4):
                        fc = fgd * 4 + j
                        for tt in range(4):
                            for n in range(2):
                                bk = tt * 2 + n
                                last = (fc == 63)
                                P.op("pe", MM(ps[:, bk, :], aT.ap[:, fc, tt * 128:(tt + 1) * 128], w.ap[:, j, n * 512:(n + 1) * 512], fc == 0, last),
                                     reads=[AT_TK[fc], w], writes=[PSB[bk]], signal=(last or (j == 3 and tt == 3 and n == 1)))
                for tt in range(4):
                    ms = mst[k_ms % 2]
                    k_ms += 1
                    src = ps[:, tt * 2:tt * 2 + 2, :]
                    dst = ms.ap.rearrange("p (a b) -> p a b", a=2)
                    fn = ACT(dst, src, AF.Copy) if tt % 2 == 0 else CP(dst, src)
                    P.op("act" if tt % 2 == 0 else "dve", fn, reads=[PSB[tt * 2], PSB[tt * 2 + 1]], writes=[ms])
                    t = sbk * 4 + tt
                    P.dma("sp", m_s[t * 128:(t + 1) * 128, hf * 1024:(hf + 1) * 1024], ms.ap, reads=[ms], writes=[M_TK[sbk]], add=True)

    def phase7():
        reset(BASE)
        wg = sb([16, D], BF16)
        wp = sb([2, D], BF16)
        g3 = sb([D], F32)
        g4 = sb([D], F32)
        g5 = sb([D], F32)
        mt = [sb([D], F32) for _ in range(2)]
        h1 = [sb([D], F32) for _ in range(2)]
        pt = [sb([2, 128], BF16) for _ in range(2)]
        tmp = sb([D], F32)
        xg = sb([D], BF16)
        junk = sb([D], BF16)
        hnT = [sb([16, 128], BF16) for _ in range(2)]
        sg = [sb([1024], F32) for _ in range(2)]
        et = sb([D], F32)
        yt = [sb([D], F32) for _ in range(2)]
        ssA = [sb([1], F32) for _ in range(2)]
        rA = [sb([1], F32) for _ in range(2)]
        ssB = [sb([1], F32) for _ in range(2)]
        rB = [sb([1], F32) for _ in range(2)]
        ssC = [sb([1], F32) for _ in range(2)]
        rC = [sb([1], F32) for _ in range(2)]
        for n in range(4):
            P.dma("pool", wg.ap[:, :, n * 512:(n + 1) * 512], w_g[:, n * 512:(n + 1) * 512].rearrange("(c p) n -> p c n", p=128),
                  writes=[wg], add=(n > 0))
        P.dma("pool", wp.ap, w_p.rearrange("(c p) n -> p c n", p=128), writes=[wp])
        P.dma("sp", g3.ap, gains[3], writes=[g3])
        P.dma("sp", g4.ap, gains[4], writes=[g4])
        P.dma("sp", g5.ap, gains[5], writes=[g5])
        Y_TK = []
        for t in range(NT):
            b = t % 2
            P.dma("sp", mt[b].ap, m_s[t * 128:(t + 1) * 128, :], reads=[M_TK[t // 4]], writes=[mt[b]])
            P.dma("sp", h1[b].ap, h1_s[t * 128:(t + 1) * 128, :], reads=[H1_TK[t]], writes=[h1[b]])
            P.dma("pool", pt[b].ap, pT_in[:, t * 128:(t + 1) * 128].rearrange("(c p) t -> p c t", p=128), writes=[pt[b]])
            P.op("act", ACT(junk.ap, mt[b].ap, AF.Square, accum_out=ssA[b].ap[:, 0:1]), reads=[mt[b]], writes=[junk, ssA[b]])
            rstd_ops(ssA[b], rA[b], D)
            P.op("dve", STT(tmp.ap, mt[b].ap, rA[b].ap[:, 0:1], g3.ap, ALU.mult, ALU.mult), reads=[mt[b], rA[b], g3], writes=[tmp])
            P.op("pool", TT(h1[b].ap, tmp.ap, h1[b].ap, ALU.add), reads=[tmp, h1[b]], writes=[h1[b]])
            norm_to_bf16(h1[b], g4, junk, ssB[b], rB[b], xg, mul_eng="dve")
            transpose16(xg, lambda c0, c1, b=b: hnT[b].ap[:, c0:c1, :], [hnT[b]], banks=(6, 7))
            for hf in range(2):
                gb = (hf * 2) % 4
                for n in range(2):
                    col = hf * 1024 + n * 512
                    for c in range(16):
                        P.op("pe", MM(ps[:, gb + n, :], hnT[b].ap[:, c, :], wg.ap[:, c, col:col + 512], c == 0, c == 15),
                             reads=[hnT[b], wg], writes=[PSB[gb + n]], signal=(c == 15))
                for n in range(2):
                    col = hf * 1024 + n * 512
                    for c in range(2):
                        P.op("pe", MM(ps[:, 4 + n, :], pt[b].ap[:, c, :], wp.ap[:, c, col:col + 512], c == 0, c == 1),
                             reads=[pt[b], wp], writes=[PSB[4 + n]], signal=(c == 1))
                sgt = sg[hf]
                P.op("act", ACT(sgt.ap.rearrange("p (a b) -> p a b", a=2), ps[:, gb:gb + 2, :], AF.Sigmoid),
                     reads=[PSB[gb], PSB[gb + 1]], writes=[sgt])
                P.op("dve", TT(et.ap[:, hf * 1024:(hf + 1) * 1024].rearrange("p (a b) -> p a b", a=2), ps[:, 4:6, :],
                               sgt.ap.rearrange("p (a b) -> p a b", a=2), ALU.mult),
                     reads=[PSB[4], PSB[5], sgt], writes=[et], add=(hf == 1))
            P.op("act", ACT(junk.ap, et.ap, AF.Square, accum_out=ssC[b].ap[:, 0:1]), reads=[et], writes=[junk, ssC[b]])
            rstd_ops(ssC[b], rC[b], D)
            P.op("dve", STT(tmp.ap, et.ap, rC[b].ap[:, 0:1], g5.ap, ALU.mult, ALU.mult), reads=[et, rC[b], g5], writes=[tmp])
            P.op("pool", TT(yt[b].ap, tmp.ap, h1[b].ap, ALU.add), reads=[tmp, h1[b]], writes=[yt[b]])
            ytk = Tk()
            P.dma("sp", y_out[t * 128:(t + 1) * 128, :], yt[b].ap, reads=[yt[b]], writes=[ytk])
            Y_TK.append(ytk)
        return Y_TK

    KA_TK, VA_TK = phase1()
    P.barrier()
    OUT_TK = phase2()
    P.barrier()
    phase3(KA_TK, VA_TK, OUT_TK["qa"])
    P.barrier()
    phase4(OUT_TK)
    P.barrier()
    phase5()
    P.barrier()
    phase6()
    P.barrier()
    Y_TK = phase7()
    P.barrier()
    P.op("sp", lambda e: e.nop(), signal=False)
    for e in ("pe", "act", "dve", "pool"):
        P.bar[e] = []

    sem_names = ["pe", "act", "dve", "pool"] + ["d_sp_%d" % i for i in range(P.ring["sp"])] + ["d_pool_%d" % i for i in range(P.ring["pool"])]
    sem_cms = [nc.semaphore(n) for n in sem_names]
    sems = {n: cm.__enter__() for n, cm in zip(sem_names, sem_cms)}

    def replay(name, e):
        for ws, fn, sig in P.q[name]:
            for k, v in ws:
                e.wait_ge(sems[k], v)
            ins = fn(e)
            if sig is not None:
                ins.then_inc(sems[sig[0]], sig[1])

    with nc.Block() as block:
        @block.tensor
        def _(e):
            replay("pe", e)

        @block.scalar
        def _(e):
            replay("act", e)

        @block.vector
        def _(e):
            replay("dve", e)

        @block.gpsimd
        def _(e):
            replay("pool", e)

        @block.sync
        def _(e):
            replay("sp", e)

    for cm in reversed(sem_cms):
        cm.__exit__(None, None, None)
    psum_cm.__exit__(None, None, None)
    arena_cm.__exit__(None, None, None)
    return nc


DEBUG_OUTS = ()


def _rope_tables():
    t = np.arange(S)
    row = (t // 64).astype(np.float32)
    col = (t % 64).astype(np.float32)
    freqs = (np.float32(10000.0) ** (-np.arange(32, dtype=np.float32) / np.float32(32))).astype(np.float32)
    ar = row[:, None] * freqs[None, :]
    ac = col[:, None] * freqs[None, :]
    cos = np.concatenate([np.cos(ar), np.cos(ar), np.cos(ac), np.cos(ac)], axis=1).astype(np.float32)
    sin = np.concatenate([-np.sin(ar), np.sin(ar), -np.sin(ac), np.sin(ac)], axis=1).astype(np.float32)
    return cos, sin


def _bias_tiles(rpb, core):
    out = np.full((3, 8, 8 * 128, 512), NEG, dtype=np.float32)
    for typ, qb in ((0, 0), (1, 1), (2, 3)):
        R = 32 * core + 8 * qb
        q = np.arange(512)
        r = R + q // 64
        c = q % 64
        r0 = np.clip(r - 4, 0, 256 - 8)
        c0 = np.clip(c - 8, 0, 64 - 16)
        k = np.arange(1024)
        kr = (R - 4) + k // 64
        kc = k % 64
        inr = (kr[:, None] >= r0[None, :]) & (kr[:, None] < r0[None, :] + 8)
        inc = (kc[:, None] >= c0[None, :]) & (kc[:, None] < c0[None, :] + 16)
        ok = inr & inc
        dr = np.clip(kr[:, None] - r[None, :] + 7, 0, 14)
        dc = np.clip(kc[:, None] - c[None, :] + 15, 0, 30)
        for h in range(8):
            g = rpb[h][dr, dc]
            out[typ, h] = np.where(ok, g, np.float32(NEG))
    return np.ascontiguousarray(out.reshape(3, 8, 8, 128, 512).transpose(0, 1, 3, 2, 4))


def _prepare_inputs(x, p, pre_mix_norm, w_in, q_norm, k_norm, rel_pos_bias, w_o, post_mix_norm, pre_mlp_norm,
                    w_up, w_down, post_mlp_norm, pre_ple_norm, w_ple_gate, w_ple_proj, post_ple_norm):
    f = lambda a: np.ascontiguousarray(np.asarray(a, dtype=np.float32))
    x2 = f(x)[0]
    p2 = f(p)[0, 0]
    gains = np.stack([np.broadcast_to(f(g)[0][None, :], (128, D)) for g in
                      (pre_mix_norm, post_mix_norm, pre_mlp_norm, post_mlp_norm, pre_ple_norm, post_ple_norm)])
    gains = np.ascontiguousarray(gains)
    gq = np.ascontiguousarray(np.broadcast_to(np.tile(f(q_norm)[0], 4)[None, :], (128, 512)))
    gk = np.ascontiguousarray(np.broadcast_to(np.tile(f(k_norm)[0], 2)[None, :], (128, 256)))
    cos, sin = _rope_tables()
    rpb = f(rel_pos_bias)[0]
    ident = np.eye(128, dtype=np.float32)
    shared = {
        "x_all": x2, "w_in": f(w_in)[0], "w_o": f(w_o)[0], "w_up": f(w_up)[0], "w_down": f(w_down)[0],
        "w_gate": f(w_ple_gate)[0], "w_proj": f(w_ple_proj)[0], "gains": gains, "gq": gq, "gk": gk,
        "cos_all": cos, "sin_all": sin, "ident": ident,
    }
    in_maps = []
    xpad = np.concatenate([np.zeros((256, D), np.float32), x2, np.zeros((256, D), np.float32)], axis=0)
    for c in range(NCORES):
        m = dict(shared)
        m["x_ext"] = np.ascontiguousarray(xpad[c * TOK:c * TOK + ETOK])
        m["pT"] = np.ascontiguousarray(p2[c * TOK:(c + 1) * TOK].T)
        m["cos_own"] = np.ascontiguousarray(cos[c * TOK:(c + 1) * TOK])
        m["sin_own"] = np.ascontiguousarray(sin[c * TOK:(c + 1) * TOK])
        m["biasT"] = _bias_tiles(rpb, c)
        in_maps.append(m)
    return in_maps


_NC_CACHE = {}


def kernel(**inputs):
    in_maps = _prepare_inputs(**inputs)
    if "nc" not in _NC_CACHE:
        _NC_CACHE["nc"] = build_program()
    nc = _NC_CACHE["nc"]
    res = run_bass_kernel_spmd(nc, in_maps, core_ids=list(range(NCORES)))
    out = np.concatenate([np.asarray(r["y"], dtype=np.float32) for r in res.results], axis=0)
    if DEBUG:
        kernel.last_results = res.results
    return out.reshape(1, S, D)
```

```python
import numpy as np
import concourse.bass as bass
import concourse.mybir as mybir
from concourse.bass_utils import run_bass_kernel_spmd

F32 = mybir.dt.float32
BF16 = mybir.dt.bfloat16
AF = mybir.ActivationFunctionType
ALU = mybir.AluOpType

NCORES = 8
S = 16384
D = 2048
TOK = 2048
NT = 16
NE = 20
ETOK = NE * 128
DFF = 8192
EPS = 1e-6
SCALE = 1.0 / np.sqrt(128.0)
NEG = -30000.0
DEBUG = False


class Tk:
    __slots__ = ("w", "r")

    def __init__(self):
        self.w = {}
        self.r = {}


class Tile:
    __slots__ = ("ap", "tk")

    def __init__(self, ap):
        self.ap = ap
        self.tk = Tk()


def _tk(x):
    return x.tk if isinstance(x, Tile) else x


ENGS = ("pe", "act", "dve", "pool", "sp")


class Prog:
    def __init__(self):
        self.q = {e: [] for e in ENGS}
        self.cnt = {e: 0 for e in ENGS}
        self.waited = {e: {} for e in ENGS}
        self.ring = {"sp": 28, "pool": 12}
        self.dma_i = {"sp": 0, "pool": 0}
        self.dma_last = {}
        self.bar = {e: [] for e in ENGS}

    def _collect(self, eng, reads, writes, extra, add):
        toks = list(extra) + self.bar[eng]
        self.bar[eng] = []
        for t in reads:
            toks.extend(_tk(t).w.items())
        for t in writes:
            t = _tk(t)
            if not add:
                toks.extend(t.w.items())
            toks.extend(t.r.items())
        out = []
        wd = self.waited[eng]
        for k, v in toks:
            if k == "pe" and eng == "pe":
                continue
            if wd.get(k, 0) >= v:
                continue
            wd[k] = v
            out.append((k, v))
        return out

    def _mark(self, tok, reads, writes, add):
        k, v = tok
        for t in reads:
            t = _tk(t)
            if t.r.get(k, 0) < v:
                t.r[k] = v
        for t in writes:
            t = _tk(t)
            if add:
                if t.w.get(k, 0) < v:
                    t.w[k] = v
            else:
                t.w = {k: v}
            t.r = {}

    def op(self, eng, fn, reads=(), writes=(), signal=True, extra=(), add=False):
        ws = self._collect(eng, reads, writes, extra, add)
        if signal:
            self.cnt[eng] += 1
            tok = (eng, self.cnt[eng])
            sig = (eng, 1)
        else:
            tok = (eng, self.cnt[eng] + 1)
            sig = None
        self.q[eng].append((ws, fn, sig))
        self._mark(tok, reads, writes, add)
        return tok

    def dma(self, queue, out_ap, in_ap, reads=(), writes=(), add=False):
        i = self.dma_i[queue]
        self.dma_i[queue] += 1
        n = self.ring[queue]
        key = "d_%s_%d" % (queue, i % n)
        val = 16 * (i // n + 1)
        extra = [(key, val - 16)] if i >= n else []
        ws = self._collect(queue, reads, writes, extra, add)
        self.q[queue].append((ws, (lambda e, o=out_ap, a=in_ap: e.dma_start(out=o, in_=a)), (key, 16)))
        tok = (key, val)
        self.dma_last[key] = val
        self._mark(tok, reads, writes, add)
        return tok

    def barrier(self):
        toks = [(e, self.cnt[e]) for e in ("pe", "act", "dve", "pool") if self.cnt[e] > 0]
        toks += list(self.dma_last.items())
        for e in ENGS:
            self.bar[e] = self.bar[e] + toks


def MM(out, lhsT, rhs, start, stop):
    return lambda e: e.matmul(out, lhsT, rhs, start=start, stop=stop)


def TR(out, in_, ident):
    return lambda e: e.transpose(out, in_, ident)


def ACT(out, in_, func, **kw):
    return lambda e: e.activation(out=out, in_=in_, func=func, **kw)


def TT(out, in0, in1, op):
    return lambda e: e.tensor_tensor(out=out, in0=in0, in1=in1, op=op)


def TS(out, in0, s1, s2, op0, op1=None):
    if op1 is None:
        return lambda e: e.tensor_scalar(out=out, in0=in0, scalar1=s1, scalar2=None, op0=op0)
    return lambda e: e.tensor_scalar(out=out, in0=in0, scalar1=s1, scalar2=s2, op0=op0, op1=op1)


def STT(out, in0, scalar, in1, op0, op1):
    return lambda e: e.scalar_tensor_tensor(out=out, in0=in0, scalar=scalar, in1=in1, op0=op0, op1=op1)


def CP(out, in_):
    return lambda e: e.tensor_copy(out, in_)


def RCP(out, in_):
    return lambda e: e.reciprocal(out=out, in_=in_)


def MSET(ap, v):
    return lambda e: e.memset(ap, v)


def build_program():
    nc = bass.Bass("TRN2", target_bir_lowering=False)
    P = Prog()

    def din(name, shape, dt=F32):
        return nc.dram_tensor(name, list(shape), dt, kind="ExternalInput").ap()

    def dscr(name, shape, dt=BF16):
        kind = "ExternalOutput" if (DEBUG and name in DEBUG_OUTS) else "Internal"
        return nc.dram_tensor(name, list(shape), dt, kind=kind).ap()

    x_all = din("x_all", [S, D])
    x_ext = din("x_ext", [ETOK, D])
    pT_in = din("pT", [256, TOK])
    w_in = din("w_in", [D, 4608])
    w_o = din("w_o", [D, D])
    w_up = din("w_up", [D, DFF])
    w_dn = din("w_down", [DFF, D])
    w_g = din("w_gate", [D, D])
    w_p = din("w_proj", [256, D])
    gains = din("gains", [6, 128, D])
    gainsT = din("gainsT", [6, 128, 16])
    gq_in = din("gq", [128, 512])
    gk_in = din("gk", [128, 256])
    cos_all = din("cos_all", [S, 128])
    sin_all = din("sin_all", [S, 128])
    cos_own = din("cos_own", [TOK, 128])
    sin_own = din("sin_own", [TOK, 128])
    biasT = din("biasT", [3, 8, 128, 8, 512])
    ident_in = din("ident", [128, 128])
    y_out = nc.dram_tensor("y", [TOK, D], F32, kind="ExternalOutput").ap()

    kaT_s = dscr("kaT_s", [2, 128, S])
    va_s = dscr("va_s", [128, 2, 128, 128])
    qaT_s = dscr("qaT_s", [8, 128, TOK])
    qbT_s = dscr("qbT_s", [8, 128, TOK])
    kbT_s = dscr("kbT_s", [8, 128, ETOK])
    vb_s = dscr("vb_s", [128, 8, NE, 128])
    mixT_s = dscr("mixT_s", [16, 128, TOK])
    h1_s = dscr("h1_s", [TOK, D], F32)
    h1nT_s = dscr("h1nT_s", [4, 128, 16, 512])
    wup_s = dscr("wup_s", [D, DFF])
    wdn_s = dscr("wdn_s", [DFF, D])
    m_s = dscr("m_s", [TOK, D], F32)

    ARENA_W = 48 * 1024
    arena_cm = nc.sbuf_tensor("arena", [128, ARENA_W], F32)
    psum_cm = nc.psum_tensor("ps", [128, 8, 512], F32)
    arena = arena_cm.__enter__()
    ps = psum_cm.__enter__()
    psb = ps[:].bitcast(BF16)
    PSB = [Tk() for _ in range(8)]

    st = {"off": 0}

    def reset(off=0):
        st["off"] = off

    def sb(shape, dt):
        n = int(np.prod(shape))
        nb = n * (2 if dt == BF16 else 4)
        nb = (nb + 63) // 64 * 64
        off = st["off"]
        st["off"] = off + nb
        assert st["off"] <= ARENA_W * 4, "sbuf arena overflow %d" % st["off"]
        v = arena[:, off // 4:(off + nb) // 4]
        if dt == BF16:
            v = v.bitcast(BF16)
        v = v[:, 0:n]
        if len(shape) == 2:
            v = v.rearrange("p (a b) -> p a b", a=shape[0])
        elif len(shape) == 3:
            v = v.rearrange("p (a b c) -> p a b c", a=shape[0], b=shape[1])
        return Tile(v)

    ident = sb([128], BF16)
    ones = sb([128], F32)
    onesb = sb([128], BF16)
    P.dma("pool", ident.ap, ident_in[:, :], writes=[ident])
    P.op("pool", MSET(ones.ap, 1.0), writes=[ones])
    P.op("pool", MSET(onesb.ap, 1.0), writes=[onesb])
    BASE = st["off"]

    WUP_TK = [Tk() for _ in range(16)]
    WDN_TK = [Tk() for _ in range(64)]

    def issue_weight_casts():
        for c in range(16):
            P.dma("pool", wup_s[c * 128:(c + 1) * 128, :], w_up[c * 128:(c + 1) * 128, :], writes=[WUP_TK[c]])
        for c in range(64):
            P.dma("pool", wdn_s[c * 128:(c + 1) * 128, :], w_dn[c * 128:(c + 1) * 128, :], writes=[WDN_TK[c]])

    def rstd_ops(ss, r, n):
        P.op("dve", TS(r.ap, ss.ap, 1.0 / n, EPS, ALU.mult, ALU.add), reads=[ss], writes=[r])
        P.op("act", ACT(r.ap, r.ap, AF.Sqrt), reads=[r], writes=[r])
        P.op("dve", RCP(r.ap, r.ap), reads=[r], writes=[r])

    def norm_to_bf16(src, gain, junk, ss, r, xg, mode="stt"):
        P.op("act", ACT(junk.ap, src.ap, AF.Square, accum_out=ss.ap[:, 0:1]), reads=[src], writes=[ss])
        rstd_ops(ss, r, D)
        if mode == "fold":
            P.op("dve", TS(xg.ap, src.ap, r.ap[:, 0:1], None, ALU.mult), reads=[src, r], writes=[xg])
        else:
            P.op("dve", STT(xg.ap, src.ap, r.ap[:, 0:1], gain.ap, ALU.mult, ALU.mult), reads=[src, r, gain], writes=[xg])

    def fold_gain(wt, gT, nchunks=16):
        for c in range(nchunks):
            P.op("dve", TS(wt.ap[:, c, :], wt.ap[:, c, :], gT.ap[:, c:c + 1], None, ALU.mult), reads=[wt, gT], writes=[wt])

    def transpose16(xg, dst_fn, dst_tiles, banks=(6, 7)):
        for half in range(2):
            bk = banks[half]
            for j in range(8):
                c = half * 8 + j
                P.op("pe", TR(psb[:, bk, j * 128:(j + 1) * 128], xg.ap[:, c * 128:(c + 1) * 128], ident.ap),
                     reads=[xg, ident], writes=[PSB[bk]], signal=(j == 7))
            eng = "act" if half == 0 else "dve"
            src = psb[:, bk, :].rearrange("p (c t) -> p c t", c=8)
            fn = ACT(dst_fn(half * 8, half * 8 + 8), src, AF.Copy) if eng == "act" else CP(dst_fn(half * 8, half * 8 + 8), src)
            P.op(eng, fn, reads=[PSB[bk]], writes=dst_tiles)

    def norm_rope(src_ap, src_tk, nh, gain, cs, sn, kg, t1, t2, hss, hr, junk, outb, rs=None):
        W = nh * 128
        rr = [rs] if rs is not None else []
        for h in range(nh):
            kw = {"scale": rs.ap[:, 0:1]} if rs is not None else {}
            P.op("act", ACT(junk.ap[:, 0:128], src_ap[:, h * 128:(h + 1) * 128], AF.Square, accum_out=hss.ap[:, h:h + 1], **kw),
                 reads=[src_tk] + rr, writes=[hss], add=True)
        rstd_ops(hss, hr, 128)
        if rs is not None:
            P.op("dve", STT(kg.ap[:, 0:W], src_ap, rs.ap[:, 0:1], gain.ap[:, 0:W], ALU.mult, ALU.mult), reads=[src_tk, gain, rs], writes=[kg])
        else:
            P.op("dve", TT(kg.ap[:, 0:W], src_ap, gain.ap[:, 0:W], ALU.mult), reads=[src_tk, gain], writes=[kg])
        kg3 = kg.ap[:, 0:W].rearrange("p (h d) -> p h d", h=nh)
        t13 = t1.ap[:, 0:W].rearrange("p (h d) -> p h d", h=nh)
        P.op("dve", TT(t13, kg3, cs.ap.unsqueeze(1).broadcast_to([128, nh, 128]), ALU.mult), reads=[kg, cs], writes=[t1])
        kg5 = kg.ap[:, 0:W].rearrange("p (h s f i) -> p h s f i", h=nh, s=2, f=2)
        t25 = t2.ap[:, 0:W].rearrange("p (h s f i) -> p h s f i", h=nh, s=2, f=2)
        sn4 = sn.ap.rearrange("p (s f i) -> p s f i", s=2, f=2)
        for f in range(2):
            P.op("dve", TT(t25[:, :, :, f, :], kg5[:, :, :, 1 - f, :],
                           sn4[:, :, f, :].unsqueeze(1).broadcast_to([128, nh, 2, 32]), ALU.mult),
                 reads=[kg, sn], writes=[t2], add=(f == 1))
        P.op("dve", TT(t1.ap[:, 0:W], t1.ap[:, 0:W], t2.ap[:, 0:W], ALU.add), reads=[t1, t2], writes=[t1])
        for h in range(nh):
            P.op("act", ACT(outb.ap[:, h, :], t1.ap[:, h * 128:(h + 1) * 128], AF.Copy, scale=hr.ap[:, h:h + 1]),
                 reads=[t1, hr], writes=[outb], add=(h > 0))

    def phase1():
        reset(BASE)
        wkv = sb([16, 512], BF16)
        gT = sb([16], F32)
        gk = sb([256], F32)
        xt = [sb([D], F32) for _ in range(2)]
        xg = [sb([D], BF16) for _ in range(2)]
        xnT = [sb([16, 128], BF16) for _ in range(2)]
        junk = sb([D], BF16)
        cs = [sb([128], F32) for _ in range(2)]
        sn = [sb([128], F32) for _ in range(2)]
        ss = [sb([1], F32) for _ in range(2)]
        r = [sb([1], F32) for _ in range(2)]
        hss = [sb([2], F32) for _ in range(2)]
        hr = [sb([2], F32) for _ in range(2)]
        kg = sb([256], F32)
        t1 = sb([256], F32)
        t2 = sb([256], F32)
        kb = [sb([2, 128], BF16) for _ in range(2)]
        kst = [sb([2, 512], BF16) for _ in range(2)]
        vst = [sb([2, 4, 128], BF16) for _ in range(2)]
        KA_TK = [Tk() for _ in range(32)]
        VA_TK = [Tk() for _ in range(32)]

        P.dma("pool", wkv.ap, w_in[:, 1024:1536].rearrange("(c p) n -> p c n", p=128), writes=[wkv])
        P.dma("sp", gT.ap, gainsT[0], writes=[gT])
        P.dma("sp", gk.ap, gk_in[:, :], writes=[gk])
        fold_gain(wkv, gT)

        def front(i):
            b = i % 2
            P.dma("sp", xt[b].ap, x_all[i * 128:(i + 1) * 128, :], writes=[xt[b]])
            P.dma("sp", cs[b].ap, cos_all[i * 128:(i + 1) * 128, :], writes=[cs[b]])
            P.dma("sp", sn[b].ap, sin_all[i * 128:(i + 1) * 128, :], writes=[sn[b]])
            P.op("dve", CP(xg[b].ap, xt[b].ap), reads=[xt[b]], writes=[xg[b]])
            P.op("act", ACT(junk.ap, xt[b].ap, AF.Square, accum_out=ss[b].ap[:, 0:1]), reads=[xt[b]], writes=[ss[b]])
            rstd_ops(ss[b], r[b], D)
            transpose16(xg[b], lambda c0, c1, b=b: xnT[b].ap[:, c0:c1, :], [xnT[b]], banks=(6, 7))
            bank = i % 2
            for c in range(16):
                P.op("pe", MM(ps[:, bank, :], xnT[b].ap[:, c, :], wkv.ap[:, c, :], c == 0, c == 15),
                     reads=[xnT[b], wkv], writes=[PSB[bank]], signal=(c == 15))

        def back(i):
            b = i % 2
            g4 = i // 4
            gb = g4 % 2
            bank = i % 2
            P.op("act", ACT(vst[gb].ap[:, :, i % 4, :], ps[:, bank, 256:512].rearrange("p (k d) -> p k d", k=2), AF.Copy, scale=r[b].ap[:, 0:1]),
                 reads=[PSB[bank], r[b]], writes=[vst[gb]], add=(i % 4 != 0))
            norm_rope(ps[:, bank, 0:256], PSB[bank], 2, gk, cs[b], sn[b], kg, t1, t2, hss[b], hr[b], junk, kb[b], rs=r[b])
            tb = 4 + (i % 2)
            for h in range(2):
                P.op("pe", TR(psb[:, tb, h * 128:(h + 1) * 128], kb[b].ap[:, h, :], ident.ap),
                     reads=[kb[b], ident], writes=[PSB[tb]], signal=(h == 1))
            P.op("dve", CP(kst[gb].ap[:, :, (i % 4) * 128:(i % 4 + 1) * 128], psb[:, tb, 0:256].rearrange("p (k t) -> p k t", k=2)),
                 reads=[PSB[tb]], writes=[kst[gb]], add=(i % 4 != 0))
            if i % 4 == 3:
                P.dma("sp", kaT_s[:, :, g4 * 512:(g4 + 1) * 512].rearrange("k d t -> d k t"), kst[gb].ap,
                      reads=[kst[gb]], writes=[KA_TK[g4]])
                P.dma("sp", va_s[:, :, g4 * 4:(g4 + 1) * 4, :], vst[gb].ap, reads=[vst[gb]], writes=[VA_TK[g4]])

        for i in range(129):
            if i < 128:
                front(i)
            if i >= 1:
                back(i - 1)
        return KA_TK, VA_TK

    def phase2():
        reset(BASE)
        xnT = sb([16, ETOK], BF16)
        XN_TK = [Tk() for _ in range(NE)]
        gT = sb([16], F32)
        gq = sb([512], F32)
        mark = st["off"]
        xt = [sb([D], F32) for _ in range(2)]
        xg = [sb([D], BF16) for _ in range(2)]
        junk = sb([D], BF16)
        ss = [sb([1], F32) for _ in range(2)]
        r = [sb([1], F32) for _ in range(2)]

        P.dma("sp", gT.ap, gainsT[0], writes=[gT])
        P.dma("sp", gq.ap, gq_in[:, :], writes=[gq])
        for e in range(NE):
            b = e % 2
            P.dma("sp", xt[b].ap, x_ext[e * 128:(e + 1) * 128, :], writes=[xt[b]])
            norm_to_bf16(xt[b], None, junk, ss[b], r[b], xg[b], mode="fold")
            transpose16(xg[b], lambda c0, c1, e=e: xnT.ap[:, c0:c1, e * 128:(e + 1) * 128], [XN_TK[e]],
                        banks=(6, 7) if e % 2 == 0 else (4, 5))
        P.barrier()
        reset(mark)
        wblk = [sb([16, 512], BF16) for _ in range(2)]
        stage = [sb([4 * ETOK], BF16) for _ in range(2)]
        junk = sb([D], BF16)
        cs = [sb([128], F32) for _ in range(2)]
        sn = [sb([128], F32) for _ in range(2)]
        hss = [sb([4], F32) for _ in range(2)]
        hr = [sb([4], F32) for _ in range(2)]
        kg = sb([512], F32)
        t1 = sb([512], F32)
        t2 = sb([512], F32)
        qb_ = [sb([4, 128], BF16) for _ in range(2)]

        blocks = [("qa", 0, 0), ("qa", 1, 512), ("qb", 0, 1536), ("qb", 1, 2048),
                  ("kb", 0, 2560), ("kb", 1, 3072), ("vb", 0, 3584), ("vb", 1, 4096)]
        bankrot = [0]

        def nbank():
            bk = bankrot[0] % 4
            bankrot[0] += 1
            return bk

        OUT_TK = {"qa": [Tk(), Tk()], "qb": [Tk(), Tk()], "kb": [Tk(), Tk()], "vb": [Tk(), Tk()]}
        for bi, (kind, half, col) in enumerate(blocks):
            wb = wblk[bi % 2]
            sg = stage[bi % 2]
            P.dma("pool", wb.ap, w_in[:, col:col + 512].rearrange("(c p) n -> p c n", p=128), writes=[wb])
            fold_gain(wb, gT)
            if kind == "qa":
                sgv = sg.ap[:, 0:4 * TOK].rearrange("p (h t) -> p h t", h=4)
                for t in range(NT):
                    e = t + 2
                    b = t % 2
                    P.dma("sp", cs[b].ap, cos_own[t * 128:(t + 1) * 128, :], writes=[cs[b]])
                    P.dma("sp", sn[b].ap, sin_own[t * 128:(t + 1) * 128, :], writes=[sn[b]])
                    bk = nbank()
                    for c in range(16):
                        P.op("pe", MM(ps[:, bk, :], xnT.ap[:, c, e * 128:(e + 1) * 128], wb.ap[:, c, :], c == 0, c == 15),
                             reads=[XN_TK[e], wb], writes=[PSB[bk]], signal=(c == 15))
                    norm_rope(ps[:, bk, :], PSB[bk], 4, gq, cs[b], sn[b], kg, t1, t2, hss[b], hr[b], junk, qb_[b])
                    tb = 4 + (t % 2)
                    for h in range(4):
                        P.op("pe", TR(psb[:, tb, h * 128:(h + 1) * 128], qb_[b].ap[:, h, :], ident.ap),
                             reads=[qb_[b], ident], writes=[PSB[tb]], signal=(h == 3))
                    P.op("dve", CP(sgv[:, :, t * 128:(t + 1) * 128], psb[:, tb, 0:512].rearrange("p (h t) -> p h t", h=4)),
                         reads=[PSB[tb]], writes=[sg], add=(t > 0))
                P.dma("sp", qaT_s[half * 4:(half + 1) * 4].rearrange("h d t -> d h t"), sgv, reads=[sg], writes=[OUT_TK[kind][half]])
            elif kind in ("qb", "kb"):
                ntok = TOK if kind == "qb" else ETOK
                e0 = 2 if kind == "qb" else 0
                sgv = sg.ap[:, 0:4 * ntok].rearrange("p (h t) -> p h t", h=4)
                first = True
                for j in range(4):
                    for sbk in range(ntok // 512):
                        tk0 = e0 * 128 + sbk * 512
                        bk = nbank()
                        for c in range(16):
                            P.op("pe", MM(ps[:, bk, :], wb.ap[:, c, j * 128:(j + 1) * 128], xnT.ap[:, c, tk0:tk0 + 512], c == 0, c == 15),
                                 reads=[wb] + XN_TK[tk0 // 128:tk0 // 128 + 4], writes=[PSB[bk]], signal=(c == 15))
                        eng = "act" if (sbk % 2 == 0) else "dve"
                        dst = sgv[:, j, sbk * 512:(sbk + 1) * 512]
                        fn = ACT(dst, ps[:, bk, :], AF.Copy) if eng == "act" else CP(dst, ps[:, bk, :])
                        P.op(eng, fn, reads=[PSB[bk]], writes=[sg], add=(not first))
                        first = False
                dst_s = qbT_s if kind == "qb" else kbT_s
                P.dma("sp", dst_s[half * 4:(half + 1) * 4].rearrange("h d t -> d h t"), sgv, reads=[sg], writes=[OUT_TK[kind][half]])
            else:
                sgv = sg.ap[:, 0:NE * 512].rearrange("p (h e d) -> p h e d", h=4, e=NE)
                for e in range(NE):
                    bk = nbank()
                    for c in range(16):
                        P.op("pe", MM(ps[:, bk, :], xnT.ap[:, c, e * 128:(e + 1) * 128], wb.ap[:, c, :], c == 0, c == 15),
                             reads=[XN_TK[e], wb], writes=[PSB[bk]], signal=(c == 15))
                    eng = "act" if (e % 2 == 0) else "dve"
                    dst = sgv[:, :, e, :]
                    src = ps[:, bk, :].rearrange("p (h d) -> p h d", h=4)
                    fn = ACT(dst, src, AF.Copy) if eng == "act" else CP(dst, src)
                    P.op(eng, fn, reads=[PSB[bk]], writes=[sg], add=(e > 0))
                P.dma("sp", vb_s[:, half * 4:(half + 1) * 4, :, :], sgv, reads=[sg], writes=[OUT_TK[kind][half]])
        return OUT_TK

    def attn_iter(it, kT_fn, v_fn, k_reads, q_ap, q_reads, ngroups, bufs, out_dst, out_tk, bias=None):
        pT, accD, accP, sbias, rl, ost = bufs["pT"], bufs["accD"], bufs["accP"], bufs["sbias"], bufs["rl"], bufs["ost"]
        ob = 4 + (it % 2)
        ostb = ost[it % 2]

        def qk(j):
            s0 = (j % 2) * 2
            for u in range(2):
                P.op("pe", MM(ps[:, s0 + u, :], kT_fn(2 * j + u), q_ap, True, True),
                     reads=k_reads + q_reads, writes=[PSB[s0 + u]], signal=(u == 1))

        def softmax_pv(j):
            s0 = (j % 2) * 2
            pt = pT[j % 3]
            src = ps[:, s0:s0 + 2, :]
            if bias is not None:
                sbt = sbias[j % 2]
                P.op("dve", STT(sbt.ap, src, float(SCALE), bias[0][:, 2 * j:2 * j + 2, :], ALU.mult, ALU.add),
                     reads=[PSB[s0], PSB[s0 + 1], bias[1]], writes=[sbt])
                P.op("act", ACT(pt.ap, sbt.ap, AF.Exp), reads=[sbt], writes=[pt])
            else:
                P.op("act", ACT(pt.ap, src, AF.Exp, scale=float(SCALE)), reads=[PSB[s0], PSB[s0 + 1]], writes=[pt])
            for u in range(2):
                P.op("pe", MM(ps[:, ob, :], v_fn(2 * j + u), pt.ap[:, u, :], (j == 0 and u == 0), (j == ngroups - 1 and u == 1)),
                     reads=k_reads + [pt], writes=[PSB[ob]], signal=(u == 1))
            if j % 3 == 2:
                for u in range(2):
                    P.op("pe", MM(ps[:, 6, :], onesb.ap, pt.ap[:, u, :], (j == 2 and u == 0), False),
                         reads=[pt, onesb], writes=[PSB[6]], signal=(u == 1))
            elif j == 0:
                P.op("dve", CP(accD.ap, pt.ap), reads=[pt], writes=[accD])
            else:
                P.op("dve", TT(accD.ap, accD.ap, pt.ap, ALU.add), reads=[pt, accD], writes=[accD])

        qk(0)
        for j in range(ngroups):
            if j + 1 < ngroups:
                qk(j + 1)
            softmax_pv(j)
        P.op("dve", TT(accD.ap[:, 0, :], accD.ap[:, 0, :], accD.ap[:, 1, :], ALU.add), reads=[accD], writes=[accD])
        P.op("pe", MM(ps[:, 6, :], ones.ap, accD.ap[:, 0, :], False, True), reads=[accD, ones], writes=[PSB[6]])
        P.op("dve", RCP(rl.ap, ps[:, 6, :]), reads=[PSB[6]], writes=[rl])
        P.op("dve", TT(ostb.ap, ps[:, ob, :], rl.ap, ALU.mult), reads=[PSB[ob], rl], writes=[ostb])
        P.dma("sp", out_dst, ostb.ap, reads=[ostb], writes=[out_tk], add=True)

    def attn_bufs():
        return {
            "pT": [sb([2, 512], BF16) for _ in range(3)],
            "accD": sb([2, 512], F32), "accP": sb([2, 512], F32),
            "sbias": [sb([2, 512], F32) for _ in range(2)],
            "rl": sb([512], F32),
            "ost": [sb([512], BF16) for _ in range(2)],
        }

    MIX_TK = [Tk() for _ in range(16)]

    def phase3(KA_TK, VA_TK, QA_TK):
        reset(BASE)
        issue_weight_casts()
        kT = sb([S], BF16)
        vv = sb([128, 128], BF16)
        qT = sb([4, TOK], BF16)
        bufs = attn_bufs()
        it = 0
        for kv in range(2):
            for c4 in range(4):
                P.dma("sp", kT.ap[:, c4 * 4096:(c4 + 1) * 4096], kaT_s[kv, :, c4 * 4096:(c4 + 1) * 4096],
                      reads=KA_TK[c4 * 8:(c4 + 1) * 8], writes=[kT], add=(c4 > 0))
                P.dma("sp", vv.ap[:, c4 * 32:(c4 + 1) * 32, :], va_s[:, kv, c4 * 32:(c4 + 1) * 32, :],
                      reads=VA_TK[c4 * 8:(c4 + 1) * 8], writes=[vv], add=(c4 > 0))
            P.dma("sp", qT.ap, qaT_s[kv * 4:(kv + 1) * 4].rearrange("h d t -> d h t"), reads=[QA_TK[kv]], writes=[qT])
            for g in range(4):
                h = kv * 4 + g
                for qb in range(4):
                    attn_iter(it, lambda kt: kT.ap[:, kt * 128:(kt + 1) * 128], lambda kt: vv.ap[:, kt, :], [kT, vv],
                              qT.ap[:, g, qb * 512:(qb + 1) * 512], [qT], 64, bufs,
                              mixT_s[h, :, qb * 512:(qb + 1) * 512], MIX_TK[h])
                    it += 1

    def phase4(OUT_TK):
        reset(BASE)
        kT = [sb([ETOK], BF16) for _ in range(2)]
        vv = [sb([NE, 128], BF16) for _ in range(2)]
        qT = [sb([TOK], BF16) for _ in range(2)]
        bt = [sb([8, 512], F32) for _ in range(2)]
        bufs = attn_bufs()
        it = 0
        for h in range(8):
            hb = h % 2
            P.dma("sp", kT[hb].ap, kbT_s[h], reads=[OUT_TK["kb"][h // 4]], writes=[kT[hb]])
            P.dma("sp", vv[hb].ap, vb_s[:, h, :, :], reads=[OUT_TK["vb"][h // 4]], writes=[vv[hb]])
            P.dma("sp", qT[hb].ap, qbT_s[h], reads=[OUT_TK["qb"][h // 4]], writes=[qT[hb]])
            for qb in range(4):
                typ = 0 if qb == 0 else (2 if qb == 3 else 1)
                btt = bt[it % 2]
                P.dma("sp", btt.ap, biasT[typ, h], writes=[btt])
                attn_iter(it, lambda kt, hb=hb, qb=qb: kT[hb].ap[:, (4 * qb + kt) * 128:(4 * qb + kt + 1) * 128],
                          lambda kt, hb=hb, qb=qb: vv[hb].ap[:, 4 * qb + kt, :], [kT[hb], vv[hb]],
                          qT[hb].ap[:, qb * 512:(qb + 1) * 512], [qT[hb]], 4, bufs,
                          mixT_s[8 + h, :, qb * 512:(qb + 1) * 512], MIX_TK[8 + h], bias=(btt.ap, btt))
                it += 1

    H1_TK = [Tk() for _ in range(NT)]
    H1N_TK = [Tk() for _ in range(4)]

    def phase5():
        reset(BASE)
        wo = sb([16, D], BF16)
        mixT = [sb([16, 512], BF16) for _ in range(1)]
        gpo = sb([D], F32)
        gpl = sb([D], F32)
        xt = [sb([D], F32) for _ in range(2)]
        tmp = sb([D], F32)
        h1 = [sb([D], F32) for _ in range(2)]
        xg = sb([D], BF16)
        junk = sb([D], BF16)
        hst = [sb([16, 512], BF16) for _ in range(2)]
        ss = [sb([1], F32) for _ in range(2)]
        r = [sb([1], F32) for _ in range(2)]
        ss1 = [sb([1], F32) for _ in range(2)]
        r1 = [sb([1], F32) for _ in range(2)]
        for n in range(4):
            P.dma("pool", wo.ap[:, :, n * 512:(n + 1) * 512], w_o[:, n * 512:(n + 1) * 512].rearrange("(c p) n -> p c n", p=128),
                  writes=[wo], add=(n > 0))
        P.dma("sp", gpo.ap, gains[1], writes=[gpo])
        P.dma("sp", gpl.ap, gains[2], writes=[gpl])
        for t in range(NT):
            sbk, tt = t // 4, t % 4
            mx = mixT[0]
            b = t % 2
            if tt == 0:
                P.dma("sp", mx.ap, mixT_s[:, :, sbk * 512:(sbk + 1) * 512].rearrange("c d t -> d c t"), reads=MIX_TK, writes=[mx])
            P.dma("sp", xt[b].ap, x_ext[(t + 2) * 128:(t + 3) * 128, :], writes=[xt[b]])
            b0 = 0 if t % 2 == 0 else 2
            for n in range(4):
                for c in range(16):
                    P.op("pe", MM(ps[:, n, :], mx.ap[:, c, tt * 128:(tt + 1) * 128], wo.ap[:, c, n * 512:(n + 1) * 512], c == 0, c == 15),
                         reads=[mx, wo], writes=[PSB[n]], signal=(c == 15))
            pall = ps[:, 0:4, :]
            P.op("act", ACT(junk.ap.rearrange("p (a b) -> p a b", a=4), pall, AF.Square, accum_out=ss[b].ap[:, 0:1]),
                 reads=PSB[0:4], writes=[ss[b]])
            rstd_ops(ss[b], r[b], D)
            P.op("dve", STT(tmp.ap.rearrange("p (a b) -> p a b", a=4), pall, r[b].ap[:, 0:1], gpo.ap.rearrange("p (a b) -> p a b", a=4), ALU.mult, ALU.mult),
                 reads=PSB[0:4] + [r[b], gpo], writes=[tmp])
            P.op("dve", TT(h1[b].ap, tmp.ap, xt[b].ap, ALU.add), reads=[tmp, xt[b]], writes=[h1[b]])
            P.dma("sp", h1_s[t * 128:(t + 1) * 128, :], h1[b].ap, reads=[h1[b]], writes=[H1_TK[t]])
            norm_to_bf16(h1[b], gpl, junk, ss1[b], r1[b], xg)
            hs = hst[sbk % 2]
            transpose16(xg, lambda c0, c1, hs=hs, tt=tt: hs.ap[:, c0:c1, tt * 128:(tt + 1) * 128], [hs], banks=(6, 7) if t % 2 == 0 else (4, 5))
            if tt == 3:
                P.dma("sp", h1nT_s[sbk], hs.ap, reads=[hs], writes=[H1N_TK[sbk]])

    M_TK = [Tk() for _ in range(4)]

    def phase6():
        reset(BASE)
        aT = sb([64, 512], BF16)
        AT_TK = [Tk() for _ in range(64)]
        hnT = [sb([16, 512], BF16) for _ in range(2)]
        wu = [sb([16, 512], BF16) for _ in range(2)]
        wd = [sb([4, 1024], BF16) for _ in range(3)]
        rt = [sb([512], F32) for _ in range(2)]
        mst = [sb([1024], F32) for _ in range(2)]
        k_up = 0
        k_dn = 0
        k_ms = 0
        for sbk in range(4):
            hn = hnT[sbk % 2]
            P.dma("sp", hn.ap, h1nT_s[sbk], reads=[H1N_TK[sbk]], writes=[hn])
            for fg in range(16):
                w = wu[k_up % 2]
                k_up += 1
                P.dma("sp", w.ap, wup_s[:, fg * 512:(fg + 1) * 512].rearrange("(c p) n -> p c n", p=128), reads=WUP_TK, writes=[w])
                for j in range(4):
                    fc = fg * 4 + j
                    bk = fc % 4
                    for c in range(16):
                        P.op("pe", MM(ps[:, bk, :], w.ap[:, c, j * 128:(j + 1) * 128], hn.ap[:, c, :], c == 0, c == 15),
                             reads=[w, hn], writes=[PSB[bk]], signal=(c == 15))
                    rtt = rt[fc % 2]
                    P.op("act", ACT(rtt.ap, ps[:, bk, :], AF.Relu), reads=[PSB[bk]], writes=[rtt])
                    P.op("dve", TT(aT.ap[:, fc, :], rtt.ap, rtt.ap, ALU.mult), reads=[rtt], writes=[AT_TK[fc]])
            for hf in range(2):
                for fgd in range(16):
                    w = wd[k_dn % 3]
                    k_dn += 1
                    P.dma("sp", w.ap, wdn_s[fgd * 512:(fgd + 1) * 512, hf * 1024:(hf + 1) * 1024].rearrange("(j p) n -> p j n", p=128),
                          reads=WDN_TK[fgd * 4:(fgd + 1) * 4], writes=[w])
                    for j in range(4):
                        fc = fgd * 4 + j
                        for tt in range(4):
                            for n in range(2):
                                bk = tt * 2 + n
                                last = (fc == 63)
                                P.op("pe", MM(ps[:, bk, :], aT.ap[:, fc, tt * 128:(tt + 1) * 128], w.ap[:, j, n * 512:(n + 1) * 512], fc == 0, last),
                                     reads=[AT_TK[fc], w], writes=[PSB[bk]], signal=(last or (j == 3 and tt == 3 and n == 1)))
                for tt in range(4):
                    ms = mst[k_ms % 2]
                    k_ms += 1
                    src = ps[:, tt * 2:tt * 2 + 2, :]
                    dst = ms.ap.rearrange("p (a b) -> p a b", a=2)
                    fn = ACT(dst, src, AF.Copy) if tt % 2 == 0 else CP(dst, src)
                    P.op("act" if tt % 2 == 0 else "dve", fn, reads=[PSB[tt * 2], PSB[tt * 2 + 1]], writes=[ms])
                    t = sbk * 4 + tt
                    P.dma("sp", m_s[t * 128:(t + 1) * 128, hf * 1024:(hf + 1) * 1024], ms.ap, reads=[ms], writes=[M_TK[sbk]], add=True)

    def phase7():
        reset(BASE)
        wg = sb([16, D], BF16)
        wp = sb([2, D], BF16)
        g3 = sb([D], F32)
        g4 = sb([D], F32)
        g5 = sb([D], F32)
        mt = [sb([D], F32) for _ in range(2)]
        h1 = [sb([D], F32) for _ in range(2)]
        pt = [sb([2, 128], BF16) for _ in range(2)]
        tmp = sb([D], F32)
        xg = sb([D], BF16)
        junk = sb([D], BF16)
        hnT = [sb([16, 128], BF16) for _ in range(2)]
        sg = [sb([1024], F32) for _ in range(2)]
        et = sb([D], F32)
        yt = [sb([D], F32) for _ in range(2)]
        ssA = [sb([1], F32) for _ in range(2)]
        rA = [sb([1], F32) for _ in range(2)]
        ssB = [sb([1], F32) for _ in range(2)]
        rB = [sb([1], F32) for _ in range(2)]
        ssC = [sb([1], F32) for _ in range(2)]
        rC = [sb([1], F32) for _ in range(2)]
        for n in range(4):
            P.dma("pool", wg.ap[:, :, n * 512:(n + 1) * 512], w_g[:, n * 512:(n + 1) * 512].rearrange("(c p) n -> p c n", p=128),
                  writes=[wg], add=(n > 0))
        P.dma("pool", wp.ap, w_p.rearrange("(c p) n -> p c n", p=128), writes=[wp])
        P.dma("sp", g3.ap, gains[3], writes=[g3])
        P.dma("sp", g4.ap, gains[4], writes=[g4])
        P.dma("sp", g5.ap, gains[5], writes=[g5])
        Y_TK = []
        for t in range(NT):
            b = t % 2
            P.dma("sp", mt[b].ap, m_s[t * 128:(t + 1) * 128, :], reads=[M_TK[t // 4]], writes=[mt[b]])
            P.dma("sp", h1[b].ap, h1_s[t * 128:(t + 1) * 128, :], reads=[H1_TK[t]], writes=[h1[b]])
            P.dma("pool", pt[b].ap, pT_in[:, t * 128:(t + 1) * 128].rearrange("(c p) t -> p c t", p=128), writes=[pt[b]])
            P.op("act", ACT(junk.ap, mt[b].ap, AF.Square, accum_out=ssA[b].ap[:, 0:1]), reads=[mt[b]], writes=[ssA[b]])
            rstd_ops(ssA[b], rA[b], D)
            P.op("dve", STT(tmp.ap, mt[b].ap, rA[b].ap[:, 0:1], g3.ap, ALU.mult, ALU.mult), reads=[mt[b], rA[b], g3], writes=[tmp])
            P.op("dve", TT(h1[b].ap, tmp.ap, h1[b].ap, ALU.add), reads=[tmp, h1[b]], writes=[h1[b]])
            norm_to_bf16(h1[b], g4, junk, ssB[b], rB[b], xg)
            transpose16(xg, lambda c0, c1, b=b: hnT[b].ap[:, c0:c1, :], [hnT[b]], banks=(6, 7))
            for hf in range(2):
                gb = (hf * 2) % 4
                for n in range(2):
                    col = hf * 1024 + n * 512
                    for c in range(16):
                        P.op("pe", MM(ps[:, gb + n, :], hnT[b].ap[:, c, :], wg.ap[:, c, col:col + 512], c == 0, c == 15),
                             reads=[hnT[b], wg], writes=[PSB[gb + n]], signal=(c == 15))
                for n in range(2):
                    col = hf * 1024 + n * 512
                    for c in range(2):
                        P.op("pe", MM(ps[:, 4 + n, :], pt[b].ap[:, c, :], wp.ap[:, c, col:col + 512], c == 0, c == 1),
                             reads=[pt[b], wp], writes=[PSB[4 + n]], signal=(c == 1))
                sgt = sg[hf]
                P.op("act", ACT(sgt.ap.rearrange("p (a b) -> p a b", a=2), ps[:, gb:gb + 2, :], AF.Sigmoid),
                     reads=[PSB[gb], PSB[gb + 1]], writes=[sgt])
                P.op("dve", TT(et.ap[:, hf * 1024:(hf + 1) * 1024].rearrange("p (a b) -> p a b", a=2), ps[:, 4:6, :],
                               sgt.ap.rearrange("p (a b) -> p a b", a=2), ALU.mult),
                     reads=[PSB[4], PSB[5], sgt], writes=[et], add=(hf == 1))
            P.op("act", ACT(junk.ap, et.ap, AF.Square, accum_out=ssC[b].ap[:, 0:1]), reads=[et], writes=[ssC[b]])
            rstd_ops(ssC[b], rC[b], D)
            P.op("dve", STT(tmp.ap, et.ap, rC[b].ap[:, 0:1], g5.ap, ALU.mult, ALU.mult), reads=[et, rC[b], g5], writes=[tmp])
            P.op("dve", TT(yt[b].ap, tmp.ap, h1[b].ap, ALU.add), reads=[tmp, h1[b]], writes=[yt[b]])
            ytk = Tk()
            P.dma("sp", y_out[t * 128:(t + 1) * 128, :], yt[b].ap, reads=[yt[b]], writes=[ytk])
            Y_TK.append(ytk)
        return Y_TK

    KA_TK, VA_TK = phase1()
    P.barrier()
    OUT_TK = phase2()
    P.barrier()
    phase3(KA_TK, VA_TK, OUT_TK["qa"])
    P.barrier()
    phase4(OUT_TK)
    P.barrier()
    phase5()
    P.barrier()
    phase6()
    P.barrier()
    Y_TK = phase7()
    P.barrier()
    P.op("sp", lambda e: e.nop(), signal=False)
    for e in ("pe", "act", "dve", "pool"):
        P.bar[e] = []

    sem_names = ["pe", "act", "dve", "pool"] + ["d_sp_%d" % i for i in range(P.ring["sp"])] + ["d_pool_%d" % i for i in range(P.ring["pool"])]
    sem_cms = [nc.semaphore(n) for n in sem_names]
    sems = {n: cm.__enter__() for n, cm in zip(sem_names, sem_cms)}

    def replay(name, e):
        for ws, fn, sig in P.q[name]:
            for k, v in ws:
                e.wait_ge(sems[k], v)
            ins = fn(e)
            if sig is not None:
                ins.then_inc(sems[sig[0]], sig[1])

    with nc.Block() as block:
        @block.tensor
        def _(e):
            replay("pe", e)

        @block.scalar
        def _(e):
            replay("act", e)

        @block.vector
        def _(e):
            replay("dve", e)

        @block.gpsimd
        def _(e):
            replay("pool", e)

        @block.sync
        def _(e):
            replay("sp", e)

    for cm in reversed(sem_cms):
        cm.__exit__(None, None, None)
    psum_cm.__exit__(None, None, None)
    arena_cm.__exit__(None, None, None)
    return nc


DEBUG_OUTS = ()


def _rope_tables():
    t = np.arange(S)
    row = (t // 64).astype(np.float32)
    col = (t % 64).astype(np.float32)
    freqs = (np.float32(10000.0) ** (-np.arange(32, dtype=np.float32) / np.float32(32))).astype(np.float32)
    ar = row[:, None] * freqs[None, :]
    ac = col[:, None] * freqs[None, :]
    cos = np.concatenate([np.cos(ar), np.cos(ar), np.cos(ac), np.cos(ac)], axis=1).astype(np.float32)
    sin = np.concatenate([-np.sin(ar), np.sin(ar), -np.sin(ac), np.sin(ac)], axis=1).astype(np.float32)
    return cos, sin


def _bias_tiles(rpb, core):
    out = np.full((3, 8, 8 * 128, 512), NEG, dtype=np.float32)
    for typ, qb in ((0, 0), (1, 1), (2, 3)):
        R = 32 * core + 8 * qb
        q = np.arange(512)
        r = R + q // 64
        c = q % 64
        r0 = np.clip(r - 4, 0, 256 - 8)
        c0 = np.clip(c - 8, 0, 64 - 16)
        k = np.arange(1024)
        kr = (R - 4) + k // 64
        kc = k % 64
        inr = (kr[:, None] >= r0[None, :]) & (kr[:, None] < r0[None, :] + 8)
        inc = (kc[:, None] >= c0[None, :]) & (kc[:, None] < c0[None, :] + 16)
        ok = inr & inc
        dr = np.clip(kr[:, None] - r[None, :] + 7, 0, 14)
        dc = np.clip(kc[:, None] - c[None, :] + 15, 0, 30)
        for h in range(8):
            g = rpb[h][dr, dc]
            out[typ, h] = np.where(ok, g, np.float32(NEG))
    return np.ascontiguousarray(out.reshape(3, 8, 8, 128, 512).transpose(0, 1, 3, 2, 4))


def _prepare_inputs(x, p, pre_mix_norm, w_in, q_norm, k_norm, rel_pos_bias, w_o, post_mix_norm, pre_mlp_norm,
                    w_up, w_down, post_mlp_norm, pre_ple_norm, w_ple_gate, w_ple_proj, post_ple_norm):
    f = lambda a: np.ascontiguousarray(np.asarray(a, dtype=np.float32))
    x2 = f(x)[0]
    p2 = f(p)[0, 0]
    gains = np.stack([np.broadcast_to(f(g)[0][None, :], (128, D)) for g in
                      (pre_mix_norm, post_mix_norm, pre_mlp_norm, post_mlp_norm, pre_ple_norm, post_ple_norm)])
    gains = np.ascontiguousarray(gains)
    gainsT = np.ascontiguousarray(np.stack([f(g)[0].reshape(16, 128).T for g in
                                            (pre_mix_norm, post_mix_norm, pre_mlp_norm, post_mlp_norm, pre_ple_norm, post_ple_norm)]))
    gq = np.ascontiguousarray(np.broadcast_to(np.tile(f(q_norm)[0], 4)[None, :], (128, 512)))
    gk = np.ascontiguousarray(np.broadcast_to(np.tile(f(k_norm)[0], 2)[None, :], (128, 256)))
    cos, sin = _rope_tables()
    rpb = f(rel_pos_bias)[0]
    ident = np.eye(128, dtype=np.float32)
    shared = {
        "x_all": x2, "w_in": f(w_in)[0], "w_o": f(w_o)[0], "w_up": f(w_up)[0], "w_down": f(w_down)[0],
        "w_gate": f(w_ple_gate)[0], "w_proj": f(w_ple_proj)[0], "gains": gains, "gainsT": gainsT, "gq": gq, "gk": gk,
        "cos_all": cos, "sin_all": sin, "ident": ident,
    }
    in_maps = []
    xpad = np.concatenate([np.zeros((256, D), np.float32), x2, np.zeros((256, D), np.float32)], axis=0)
    for c in range(NCORES):
        m = dict(shared)
        m["x_ext"] = np.ascontiguousarray(xpad[c * TOK:c * TOK + ETOK])
        m["pT"] = np.ascontiguousarray(p2[c * TOK:(c + 1) * TOK].T)
        m["cos_own"] = np.ascontiguousarray(cos[c * TOK:(c + 1) * TOK])
        m["sin_own"] = np.ascontiguousarray(sin[c * TOK:(c + 1) * TOK])
        m["biasT"] = _bias_tiles(rpb, c)
        in_maps.append(m)
    return in_maps


_NC_CACHE = {}


def kernel(**inputs):
    in_maps = _prepare_inputs(**inputs)
    if "nc" not in _NC_CACHE:
        _NC_CACHE["nc"] = build_program()
    nc = _NC_CACHE["nc"]
    res = run_bass_kernel_spmd(nc, in_maps, core_ids=list(range(NCORES)))
    out = np.concatenate([np.asarray(r["y"], dtype=np.float32) for r in res.results], axis=0)
    if DEBUG:
        kernel.last_results = res.results
    return out.reshape(1, S, D)
```

```python
import numpy as np
import concourse.bass as bass
import concourse.mybir as mybir
from concourse.bass_utils import run_bass_kernel_spmd

F32 = mybir.dt.float32
BF16 = mybir.dt.bfloat16
AF = mybir.ActivationFunctionType
ALU = mybir.AluOpType

NCORES = 8
S = 16384
D = 2048
TOK = 2048
NT = 16
NE = 20
ETOK = NE * 128
DFF = 8192
EPS = 1e-6
SCALE = 1.0 / np.sqrt(128.0)
NEG = -30000.0
DEBUG = False


class Tk:
    __slots__ = ("w", "r")

    def __init__(self):
        self.w = {}
        self.r = {}


class Tile:
    __slots__ = ("ap", "tk")

    def __init__(self, ap):
        self.ap = ap
        self.tk = Tk()


def _tk(x):
    return x.tk if isinstance(x, Tile) else x


ENGS = ("pe", "act", "dve", "pool", "sp")


class Prog:
    def __init__(self):
        self.q = {e: [] for e in ENGS}
        self.cnt = {e: 0 for e in ENGS}
        self.waited = {e: {} for e in ENGS}
        self.ring = {"sp": 28, "pool": 12}
        self.dma_i = {"sp": 0, "pool": 0}
        self.dma_last = {}
        self.bar = {e: [] for e in ENGS}

    def _collect(self, eng, reads, writes, extra, add):
        toks = list(extra) + self.bar[eng]
        self.bar[eng] = []
        for t in reads:
            toks.extend(_tk(t).w.items())
        for t in writes:
            t = _tk(t)
            if not add:
                toks.extend(t.w.items())
            toks.extend(t.r.items())
        out = []
        wd = self.waited[eng]
        for k, v in toks:
            if k == "pe" and eng == "pe":
                continue
            if wd.get(k, 0) >= v:
                continue
            wd[k] = v
            out.append((k, v))
        return out

    def _mark(self, tok, reads, writes, add):
        k, v = tok
        for t in reads:
            t = _tk(t)
            if t.r.get(k, 0) < v:
                t.r[k] = v
        for t in writes:
            t = _tk(t)
            if add:
                if t.w.get(k, 0) < v:
                    t.w[k] = v
            else:
                t.w = {k: v}
            t.r = {}

    def op(self, eng, fn, reads=(), writes=(), signal=True, extra=(), add=False):
        ws = self._collect(eng, reads, writes, extra, add)
        if signal:
            self.cnt[eng] += 1
            tok = (eng, self.cnt[eng])
            sig = (eng, 1)
        else:
            tok = (eng, self.cnt[eng] + 1)
            sig = None
        self.q[eng].append((ws, fn, sig))
        self._mark(tok, reads, writes, add)
        return tok

    def dma(self, queue, out_ap, in_ap, reads=(), writes=(), add=False, extra=()):
        i = self.dma_i[queue]
        self.dma_i[queue] += 1
        n = self.ring[queue]
        key = "d_%s_%d" % (queue, i % n)
        val = 16 * (i // n + 1)
        extra = list(extra) + ([(key, val - 16)] if i >= n else [])
        ws = self._collect(queue, reads, writes, extra, add)
        self.q[queue].append((ws, (lambda e, o=out_ap, a=in_ap: e.dma_start(out=o, in_=a)), (key, 16)))
        tok = (key, val)
        self.dma_last[key] = val
        self._mark(tok, reads, writes, add)
        return tok

    def barrier(self):
        toks = [(e, self.cnt[e]) for e in ("pe", "act", "dve", "pool") if self.cnt[e] > 0]
        toks += list(self.dma_last.items())
        for e in ENGS:
            self.bar[e] = self.bar[e] + toks


def MM(out, lhsT, rhs, start, stop):
    return lambda e: e.matmul(out, lhsT, rhs, start=start, stop=stop)


def TR(out, in_, ident):
    return lambda e: e.transpose(out, in_, ident)


def ACT(out, in_, func, **kw):
    return lambda e: e.activation(out=out, in_=in_, func=func, **kw)


def TT(out, in0, in1, op):
    return lambda e: e.tensor_tensor(out=out, in0=in0, in1=in1, op=op)


def TS(out, in0, s1, s2, op0, op1=None):
    if op1 is None:
        return lambda e: e.tensor_scalar(out=out, in0=in0, scalar1=s1, scalar2=None, op0=op0)
    return lambda e: e.tensor_scalar(out=out, in0=in0, scalar1=s1, scalar2=s2, op0=op0, op1=op1)


def STT(out, in0, scalar, in1, op0, op1):
    return lambda e: e.scalar_tensor_tensor(out=out, in0=in0, scalar=scalar, in1=in1, op0=op0, op1=op1)


def CP(out, in_):
    return lambda e: e.tensor_copy(out, in_)


def RCP(out, in_):
    return lambda e: e.reciprocal(out=out, in_=in_)


def MSET(ap, v):
    return lambda e: e.memset(ap, v)


def build_program():
    nc = bass.Bass("TRN2", target_bir_lowering=False)
    P = Prog()

    def din(name, shape, dt=F32):
        return nc.dram_tensor(name, list(shape), dt, kind="ExternalInput").ap()

    def dscr(name, shape, dt=BF16):
        kind = "ExternalOutput" if (DEBUG and name in DEBUG_OUTS) else "Internal"
        return nc.dram_tensor(name, list(shape), dt, kind=kind).ap()

    x_all = din("x_all", [S, D])
    x_ext = din("x_ext", [ETOK, D])
    pT_in = din("pT", [256, TOK])
    w_in = din("w_in", [D, 4608])
    w_o = din("w_o", [D, D])
    w_up = din("w_up", [D, DFF])
    w_dn = din("w_down", [DFF, D])
    w_g = din("w_gate", [D, D])
    w_p = din("w_proj", [256, D])
    gains = din("gains", [6, 128, D])
    gainsT = din("gainsT", [6, 128, 16])
    gq_in = din("gq", [128, 512])
    gk_in = din("gk", [128, 256])
    cos_all = din("cos_all", [S, 128])
    sin_all = din("sin_all", [S, 128])
    cos_own = din("cos_own", [TOK, 128])
    sin_own = din("sin_own", [TOK, 128])
    biasT = din("biasT", [3, 8, 128, 8, 512])
    ident_in = din("ident", [128, 128])
    y_out = nc.dram_tensor("y", [TOK, D], F32, kind="ExternalOutput").ap()

    kaT_s = dscr("kaT_s", [2, 128, S])
    va_s = dscr("va_s", [128, 2, 128, 128])
    qaT_s = dscr("qaT_s", [8, 128, TOK])
    qbT_s = dscr("qbT_s", [8, 128, TOK])
    kbT_s = dscr("kbT_s", [8, 128, ETOK])
    vb_s = dscr("vb_s", [128, 8, NE, 128])
    mixT_s = dscr("mixT_s", [16, 128, TOK])
    h1_s = dscr("h1_s", [TOK, D], F32)
    h1nT_s = dscr("h1nT_s", [4, 128, 16, 512])
    wup_s = dscr("wup_s", [D, DFF])
    wdn_s = dscr("wdn_s", [DFF, D])
    m_s = dscr("m_s", [TOK, D], F32)

    ARENA_W = 48 * 1024
    arena_cm = nc.sbuf_tensor("arena", [128, ARENA_W], F32)
    psum_cm = nc.psum_tensor("ps", [128, 8, 512], F32)
    arena = arena_cm.__enter__()
    ps = psum_cm.__enter__()
    psb = ps[:].bitcast(BF16)
    PSB = [Tk() for _ in range(8)]

    st = {"off": 0}

    def reset(off=0):
        st["off"] = off

    def sb(shape, dt):
        n = int(np.prod(shape))
        nb = n * (2 if dt == BF16 else 4)
        nb = (nb + 63) // 64 * 64
        off = st["off"]
        st["off"] = off + nb
        assert st["off"] <= ARENA_W * 4, "sbuf arena overflow %d" % st["off"]
        v = arena[:, off // 4:(off + nb) // 4]
        if dt == BF16:
            v = v.bitcast(BF16)
        v = v[:, 0:n]
        if len(shape) == 2:
            v = v.rearrange("p (a b) -> p a b", a=shape[0])
        elif len(shape) == 3:
            v = v.rearrange("p (a b c) -> p a b c", a=shape[0], b=shape[1])
        return Tile(v)

    ident = sb([128], BF16)
    ones = sb([128], F32)
    onesb = sb([128], BF16)
    P.dma("pool", ident.ap, ident_in[:, :], writes=[ident])
    P.op("pool", MSET(ones.ap, 1.0), writes=[ones])
    P.op("pool", MSET(onesb.ap, 1.0), writes=[onesb])
    BASE = st["off"]

    WUP_TK = [Tk() for _ in range(16)]
    WDN_TK = [Tk() for _ in range(64)]

    cast_list = [(wup_s[c * 128:(c + 1) * 128, :], w_up[c * 128:(c + 1) * 128, :], WUP_TK[c]) for c in range(16)] + \
                [(wdn_s[c * 128:(c + 1) * 128, :], w_dn[c * 128:(c + 1) * 128, :], WDN_TK[c]) for c in range(64)]

    def issue_weight_casts(n, extra=()):
        for _ in range(n):
            if cast_list:
                o, i_, tk = cast_list.pop(0)
                P.dma("pool", o, i_, writes=[tk], extra=extra)

    def rstd_ops(ss, r, n):
        P.op("dve", TS(r.ap, ss.ap, 1.0 / n, EPS, ALU.mult, ALU.add), reads=[ss], writes=[r])
        P.op("act", ACT(r.ap, r.ap, AF.Sqrt), reads=[r], writes=[r])
        P.op("dve", RCP(r.ap, r.ap), reads=[r], writes=[r])

    def norm_to_bf16(src, gain, junk, ss, r, xg, mode="stt"):
        P.op("act", ACT(junk.ap, src.ap, AF.Square, accum_out=ss.ap[:, 0:1]), reads=[src], writes=[ss])
        rstd_ops(ss, r, D)
        if mode == "fold":
            P.op("dve", TS(xg.ap, src.ap, r.ap[:, 0:1], None, ALU.mult), reads=[src, r], writes=[xg])
        else:
            P.op("dve", STT(xg.ap, src.ap, r.ap[:, 0:1], gain.ap, ALU.mult, ALU.mult), reads=[src, r, gain], writes=[xg])

    def fold_gain(wt, gT, nchunks=16):
        for c in range(nchunks):
            P.op("dve", TS(wt.ap[:, c, :], wt.ap[:, c, :], gT.ap[:, c:c + 1], None, ALU.mult), reads=[wt, gT], writes=[wt])

    def transpose16(xg, dst_fn, dst_tiles, banks=(6, 7)):
        for half in range(2):
            bk = banks[half]
            for j in range(8):
                c = half * 8 + j
                P.op("pe", TR(psb[:, bk, j * 128:(j + 1) * 128], xg.ap[:, c * 128:(c + 1) * 128], ident.ap),
                     reads=[xg, ident], writes=[PSB[bk]], signal=(j == 7))
            eng = "act" if half == 0 else "dve"
            src = psb[:, bk, :].rearrange("p (c t) -> p c t", c=8)
            fn = ACT(dst_fn(half * 8, half * 8 + 8), src, AF.Copy) if eng == "act" else CP(dst_fn(half * 8, half * 8 + 8), src)
            P.op(eng, fn, reads=[PSB[bk]], writes=dst_tiles)

    def norm_rope(src_ap, src_tk, nh, gain, cs, sn, kg, t1, t2, hss, hr, junk, outb, rs=None):
        W = nh * 128
        rr = [rs] if rs is not None else []
        for h in range(nh):
            kw = {"scale": rs.ap[:, 0:1]} if rs is not None else {}
            P.op("act", ACT(junk.ap[:, 0:128], src_ap[:, h * 128:(h + 1) * 128], AF.Square, accum_out=hss.ap[:, h:h + 1], **kw),
                 reads=[src_tk] + rr, writes=[hss], add=True)
        rstd_ops(hss, hr, 128)
        if rs is not None:
            P.op("dve", STT(kg.ap[:, 0:W], src_ap, rs.ap[:, 0:1], gain.ap[:, 0:W], ALU.mult, ALU.mult), reads=[src_tk, gain, rs], writes=[kg])
        else:
            P.op("dve", TT(kg.ap[:, 0:W], src_ap, gain.ap[:, 0:W], ALU.mult), reads=[src_tk, gain], writes=[kg])
        kg3 = kg.ap[:, 0:W].rearrange("p (h d) -> p h d", h=nh)
        t13 = t1.ap[:, 0:W].rearrange("p (h d) -> p h d", h=nh)
        P.op("dve", TT(t13, kg3, cs.ap.unsqueeze(1).broadcast_to([128, nh, 128]), ALU.mult), reads=[kg, cs], writes=[t1])
        kg5 = kg.ap[:, 0:W].rearrange("p (h s f i) -> p h s f i", h=nh, s=2, f=2)
        t25 = t2.ap[:, 0:W].rearrange("p (h s f i) -> p h s f i", h=nh, s=2, f=2)
        sn4 = sn.ap.rearrange("p (s f i) -> p s f i", s=2, f=2)
        for f in range(2):
            P.op("dve", TT(t25[:, :, :, f, :], kg5[:, :, :, 1 - f, :],
                           sn4[:, :, f, :].unsqueeze(1).broadcast_to([128, nh, 2, 32]), ALU.mult),
                 reads=[kg, sn], writes=[t2], add=(f == 1))
        P.op("dve", TT(t1.ap[:, 0:W], t1.ap[:, 0:W], t2.ap[:, 0:W], ALU.add), reads=[t1, t2], writes=[t1])
        for h in range(nh):
            P.op("act", ACT(outb.ap[:, h, :], t1.ap[:, h * 128:(h + 1) * 128], AF.Copy, scale=hr.ap[:, h:h + 1]),
                 reads=[t1, hr], writes=[outb], add=(h > 0))

    def phase1():
        reset(BASE)
        wkv = sb([16, 512], BF16)
        gT = sb([16], F32)
        gk = sb([256], F32)
        xt = [sb([D], F32) for _ in range(2)]
        xg = [sb([D], BF16) for _ in range(2)]
        xnT = [sb([16, 128], BF16) for _ in range(2)]
        junk = sb([D], BF16)
        cs = [sb([128], F32) for _ in range(2)]
        sn = [sb([128], F32) for _ in range(2)]
        ss = [sb([1], F32) for _ in range(2)]
        r = [sb([1], F32) for _ in range(2)]
        hss = [sb([2], F32) for _ in range(2)]
        hr = [sb([2], F32) for _ in range(2)]
        kg = sb([256], F32)
        t1 = sb([256], F32)
        t2 = sb([256], F32)
        kb = [sb([2, 128], BF16) for _ in range(2)]
        kst = [sb([2, 512], BF16) for _ in range(2)]
        vst = [sb([2, 4, 128], BF16) for _ in range(2)]
        KA_TK = [Tk() for _ in range(32)]
        VA_TK = [Tk() for _ in range(32)]

        P.dma("pool", wkv.ap, w_in[:, 1024:1536].rearrange("(c p) n -> p c n", p=128), writes=[wkv])
        P.dma("sp", gT.ap, gainsT[0], writes=[gT])
        P.dma("sp", gk.ap, gk_in[:, :], writes=[gk])
        fold_gain(wkv, gT)

        def front(i):
            b = i % 2
            P.dma("sp", xt[b].ap, x_all[i * 128:(i + 1) * 128, :], writes=[xt[b]])
            P.dma("sp", cs[b].ap, cos_all[i * 128:(i + 1) * 128, :], writes=[cs[b]])
            P.dma("sp", sn[b].ap, sin_all[i * 128:(i + 1) * 128, :], writes=[sn[b]])
            P.op("dve", CP(xg[b].ap, xt[b].ap), reads=[xt[b]], writes=[xg[b]])
            P.op("act", ACT(junk.ap, xt[b].ap, AF.Square, accum_out=ss[b].ap[:, 0:1]), reads=[xt[b]], writes=[ss[b]])
            rstd_ops(ss[b], r[b], D)
            transpose16(xg[b], lambda c0, c1, b=b: xnT[b].ap[:, c0:c1, :], [xnT[b]], banks=(6, 7))
            bank = i % 2
            for c in range(16):
                P.op("pe", MM(ps[:, bank, :], xnT[b].ap[:, c, :], wkv.ap[:, c, :], c == 0, c == 15),
                     reads=[xnT[b], wkv], writes=[PSB[bank]], signal=(c == 15))

        def back(i):
            b = i % 2
            g4 = i // 4
            gb = g4 % 2
            bank = i % 2
            P.op("act", ACT(vst[gb].ap[:, :, i % 4, :], ps[:, bank, 256:512].rearrange("p (k d) -> p k d", k=2), AF.Copy, scale=r[b].ap[:, 0:1]),
                 reads=[PSB[bank], r[b]], writes=[vst[gb]], add=(i % 4 != 0))
            norm_rope(ps[:, bank, 0:256], PSB[bank], 2, gk, cs[b], sn[b], kg, t1, t2, hss[b], hr[b], junk, kb[b], rs=r[b])
            tb = 4 + (i % 2)
            for h in range(2):
                P.op("pe", TR(psb[:, tb, h * 128:(h + 1) * 128], kb[b].ap[:, h, :], ident.ap),
                     reads=[kb[b], ident], writes=[PSB[tb]], signal=(h == 1))
            P.op("dve", CP(kst[gb].ap[:, :, (i % 4) * 128:(i % 4 + 1) * 128], psb[:, tb, 0:256].rearrange("p (k t) -> p k t", k=2)),
                 reads=[PSB[tb]], writes=[kst[gb]], add=(i % 4 != 0))
            if i % 4 == 3:
                P.dma("sp", kaT_s[:, :, g4 * 512:(g4 + 1) * 512].rearrange("k d t -> d k t"), kst[gb].ap,
                      reads=[kst[gb]], writes=[KA_TK[g4]])
                P.dma("sp", va_s[:, :, g4 * 4:(g4 + 1) * 4, :], vst[gb].ap, reads=[vst[gb]], writes=[VA_TK[g4]])

        for i in range(129):
            if i < 128:
                front(i)
            if i >= 1:
                back(i - 1)
        return KA_TK, VA_TK

    def phase2():
        reset(BASE)
        xnT = sb([16, ETOK], BF16)
        XN_TK = [Tk() for _ in range(NE)]
        gT = sb([16], F32)
        gq = sb([512], F32)
        mark = st["off"]
        xt = [sb([D], F32) for _ in range(2)]
        xg = [sb([D], BF16) for _ in range(2)]
        junk = sb([D], BF16)
        ss = [sb([1], F32) for _ in range(2)]
        r = [sb([1], F32) for _ in range(2)]

        P.dma("sp", gT.ap, gainsT[0], writes=[gT])
        P.dma("sp", gq.ap, gq_in[:, :], writes=[gq])
        for e in range(NE):
            b = e % 2
            P.dma("sp", xt[b].ap, x_ext[e * 128:(e + 1) * 128, :], writes=[xt[b]])
            norm_to_bf16(xt[b], None, junk, ss[b], r[b], xg[b], mode="fold")
            transpose16(xg[b], lambda c0, c1, e=e: xnT.ap[:, c0:c1, e * 128:(e + 1) * 128], [XN_TK[e]],
                        banks=(6, 7) if e % 2 == 0 else (4, 5))
        P.barrier()
        reset(mark)
        wblk = [sb([16, 512], BF16) for _ in range(2)]
        stage = [sb([4 * ETOK], BF16) for _ in range(2)]
        junk = sb([D], BF16)
        cs = [sb([128], F32) for _ in range(2)]
        sn = [sb([128], F32) for _ in range(2)]
        hss = [sb([4], F32) for _ in range(2)]
        hr = [sb([4], F32) for _ in range(2)]
        kg = sb([512], F32)
        t1 = sb([512], F32)
        t2 = sb([512], F32)
        qb_ = [sb([4, 128], BF16) for _ in range(2)]

        blocks = [("qa", 0, 0), ("qa", 1, 512), ("qb", 0, 1536), ("qb", 1, 2048),
                  ("kb", 0, 2560), ("kb", 1, 3072), ("vb", 0, 3584), ("vb", 1, 4096)]
        bankrot = [0]

        def nbank():
            bk = bankrot[0] % 4
            bankrot[0] += 1
            return bk

        OUT_TK = {"qa": [Tk(), Tk()], "qb": [Tk(), Tk()], "kb": [Tk(), Tk()], "vb": [Tk(), Tk()]}
        for bi, (kind, half, col) in enumerate(blocks):
            wb = wblk[bi % 2]
            sg = stage[bi % 2]
            P.dma("pool", wb.ap, w_in[:, col:col + 512].rearrange("(c p) n -> p c n", p=128), writes=[wb])
            fold_gain(wb, gT)
            if kind == "qa":
                sgv = sg.ap[:, 0:4 * TOK].rearrange("p (h t) -> p h t", h=4)
                for t in range(NT):
                    e = t + 2
                    b = t % 2
                    P.dma("sp", cs[b].ap, cos_own[t * 128:(t + 1) * 128, :], writes=[cs[b]])
                    P.dma("sp", sn[b].ap, sin_own[t * 128:(t + 1) * 128, :], writes=[sn[b]])
                    bk = nbank()
                    for c in range(16):
                        P.op("pe", MM(ps[:, bk, :], xnT.ap[:, c, e * 128:(e + 1) * 128], wb.ap[:, c, :], c == 0, c == 15),
                             reads=[XN_TK[e], wb], writes=[PSB[bk]], signal=(c == 15))
                    norm_rope(ps[:, bk, :], PSB[bk], 4, gq, cs[b], sn[b], kg, t1, t2, hss[b], hr[b], junk, qb_[b])
                    tb = 4 + (t % 2)
                    for h in range(4):
                        P.op("pe", TR(psb[:, tb, h * 128:(h + 1) * 128], qb_[b].ap[:, h, :], ident.ap),
                             reads=[qb_[b], ident], writes=[PSB[tb]], signal=(h == 3))
                    P.op("dve", CP(sgv[:, :, t * 128:(t + 1) * 128], psb[:, tb, 0:512].rearrange("p (h t) -> p h t", h=4)),
                         reads=[PSB[tb]], writes=[sg], add=(t > 0))
                P.dma("sp", qaT_s[half * 4:(half + 1) * 4].rearrange("h d t -> d h t"), sgv, reads=[sg], writes=[OUT_TK[kind][half]])
            elif kind in ("qb", "kb"):
                ntok = TOK if kind == "qb" else ETOK
                e0 = 2 if kind == "qb" else 0
                sgv = sg.ap[:, 0:4 * ntok].rearrange("p (h t) -> p h t", h=4)
                first = True
                for j in range(4):
                    for sbk in range(ntok // 512):
                        tk0 = e0 * 128 + sbk * 512
                        bk = nbank()
                        for c in range(16):
                            P.op("pe", MM(ps[:, bk, :], wb.ap[:, c, j * 128:(j + 1) * 128], xnT.ap[:, c, tk0:tk0 + 512], c == 0, c == 15),
                                 reads=[wb] + XN_TK[tk0 // 128:tk0 // 128 + 4], writes=[PSB[bk]], signal=(c == 15))
                        eng = "act" if (sbk % 2 == 0) else "dve"
                        dst = sgv[:, j, sbk * 512:(sbk + 1) * 512]
                        fn = ACT(dst, ps[:, bk, :], AF.Copy) if eng == "act" else CP(dst, ps[:, bk, :])
                        P.op(eng, fn, reads=[PSB[bk]], writes=[sg], add=(not first))
                        first = False
                dst_s = qbT_s if kind == "qb" else kbT_s
                P.dma("sp", dst_s[half * 4:(half + 1) * 4].rearrange("h d t -> d h t"), sgv, reads=[sg], writes=[OUT_TK[kind][half]])
            else:
                sgv = sg.ap[:, 0:NE * 512].rearrange("p (h e d) -> p h e d", h=4, e=NE)
                for e in range(NE):
                    bk = nbank()
                    for c in range(16):
                        P.op("pe", MM(ps[:, bk, :], xnT.ap[:, c, e * 128:(e + 1) * 128], wb.ap[:, c, :], c == 0, c == 15),
                             reads=[XN_TK[e], wb], writes=[PSB[bk]], signal=(c == 15))
                    eng = "act" if (e % 2 == 0) else "dve"
                    dst = sgv[:, :, e, :]
                    src = ps[:, bk, :].rearrange("p (h d) -> p h d", h=4)
                    fn = ACT(dst, src, AF.Copy) if eng == "act" else CP(dst, src)
                    P.op(eng, fn, reads=[PSB[bk]], writes=[sg], add=(e > 0))
                P.dma("sp", vb_s[:, half * 4:(half + 1) * 4, :, :], sgv, reads=[sg], writes=[OUT_TK[kind][half]])
        return OUT_TK

    def attn_iter(it, kT_fn, v_fn, k_reads, q_ap, q_reads, ngroups, bufs, out_dst, out_tk, bias=None):
        pT, accD, accP, sbias, rl, ost = bufs["pT"], bufs["accD"], bufs["accP"], bufs["sbias"], bufs["rl"], bufs["ost"]
        ob = 4 + (it % 2)
        ostb = ost[it % 2]

        def qk(j):
            s0 = (j % 2) * 2
            for u in range(2):
                P.op("pe", MM(ps[:, s0 + u, :], kT_fn(2 * j + u), q_ap, True, True),
                     reads=k_reads + q_reads, writes=[PSB[s0 + u]], signal=(u == 1))

        def softmax_pv(j):
            s0 = (j % 2) * 2
            pt = pT[j % 3]
            src = ps[:, s0:s0 + 2, :]
            if bias is not None:
                sbt = sbias[j % 2]
                P.op("dve", STT(sbt.ap, src, float(SCALE), bias[0][:, 2 * j:2 * j + 2, :], ALU.mult, ALU.add),
                     reads=[PSB[s0], PSB[s0 + 1], bias[1]], writes=[sbt])
                P.op("act", ACT(pt.ap, sbt.ap, AF.Exp), reads=[sbt], writes=[pt])
            else:
                P.op("act", ACT(pt.ap, src, AF.Exp, scale=float(SCALE)), reads=[PSB[s0], PSB[s0 + 1]], writes=[pt])
            for u in range(2):
                P.op("pe", MM(ps[:, ob, :], v_fn(2 * j + u), pt.ap[:, u, :], (j == 0 and u == 0), (j == ngroups - 1 and u == 1)),
                     reads=k_reads + [pt], writes=[PSB[ob]], signal=(u == 1))
            if j % 4 == 3:
                for u in range(2):
                    P.op("pe", MM(ps[:, 6, :], onesb.ap, pt.ap[:, u, :], (j == 3 and u == 0), False),
                         reads=[pt, onesb], writes=[PSB[6]], signal=(u == 1))
            elif j == 0:
                P.op("dve", CP(accD.ap, pt.ap), reads=[pt], writes=[accD])
            else:
                P.op("dve", TT(accD.ap, accD.ap, pt.ap, ALU.add), reads=[pt, accD], writes=[accD])

        qk(0)
        for j in range(ngroups):
            if j + 1 < ngroups:
                qk(j + 1)
            softmax_pv(j)
        P.op("dve", TT(accD.ap[:, 0, :], accD.ap[:, 0, :], accD.ap[:, 1, :], ALU.add), reads=[accD], writes=[accD])
        P.op("pe", MM(ps[:, 6, :], ones.ap, accD.ap[:, 0, :], False, True), reads=[accD, ones], writes=[PSB[6]])
        P.op("dve", RCP(rl.ap, ps[:, 6, :]), reads=[PSB[6]], writes=[rl])
        P.op("dve", TT(ostb.ap, ps[:, ob, :], rl.ap, ALU.mult), reads=[PSB[ob], rl], writes=[ostb])
        P.dma("sp", out_dst, ostb.ap, reads=[ostb], writes=[out_tk], add=True)

    def attn_bufs():
        return {
            "pT": [sb([2, 512], BF16) for _ in range(3)],
            "accD": sb([2, 512], F32), "accP": sb([2, 512], F32),
            "sbias": [sb([2, 512], F32) for _ in range(2)],
            "rl": sb([512], F32),
            "ost": [sb([512], BF16) for _ in range(2)],
        }

    MIX_TK = [Tk() for _ in range(16)]

    def phase3(KA_TK, VA_TK, QA_TK):
        reset(BASE)
        kT = sb([S], BF16)
        vv = sb([128, 128], BF16)
        qT = sb([4, TOK], BF16)
        bufs = attn_bufs()
        it = 0
        for kv in range(2):
            for c4 in range(4):
                P.dma("sp", kT.ap[:, c4 * 4096:(c4 + 1) * 4096], kaT_s[kv, :, c4 * 4096:(c4 + 1) * 4096],
                      reads=KA_TK[c4 * 8:(c4 + 1) * 8], writes=[kT], add=(c4 > 0))
                P.dma("sp", vv.ap[:, c4 * 32:(c4 + 1) * 32, :], va_s[:, kv, c4 * 32:(c4 + 1) * 32, :],
                      reads=VA_TK[c4 * 8:(c4 + 1) * 8], writes=[vv], add=(c4 > 0))
            P.dma("sp", qT.ap, qaT_s[kv * 4:(kv + 1) * 4].rearrange("h d t -> d h t"), reads=[QA_TK[kv]], writes=[qT])
            for g in range(4):
                h = kv * 4 + g
                for qb in range(4):
                    issue_weight_casts(3, extra=[("act", P.cnt["act"])] if P.cnt["act"] else ())
                    attn_iter(it, lambda kt: kT.ap[:, kt * 128:(kt + 1) * 128], lambda kt: vv.ap[:, kt, :], [kT, vv],
                              qT.ap[:, g, qb * 512:(qb + 1) * 512], [qT], 64, bufs,
                              mixT_s[h, :, qb * 512:(qb + 1) * 512], MIX_TK[h])
                    it += 1

    def phase4(OUT_TK):
        reset(BASE)
        issue_weight_casts(100)
        kT = [sb([ETOK], BF16) for _ in range(2)]
        vv = [sb([NE, 128], BF16) for _ in range(2)]
        qT = [sb([TOK], BF16) for _ in range(2)]
        bt = [sb([8, 512], F32) for _ in range(2)]
        bufs = attn_bufs()
        it = 0
        for h in range(8):
            hb = h % 2
            P.dma("sp", kT[hb].ap, kbT_s[h], reads=[OUT_TK["kb"][h // 4]], writes=[kT[hb]])
            P.dma("sp", vv[hb].ap, vb_s[:, h, :, :], reads=[OUT_TK["vb"][h // 4]], writes=[vv[hb]])
            P.dma("sp", qT[hb].ap, qbT_s[h], reads=[OUT_TK["qb"][h // 4]], writes=[qT[hb]])
            for qb in range(4):
                typ = 0 if qb == 0 else (2 if qb == 3 else 1)
                btt = bt[it % 2]
                P.dma("sp", btt.ap, biasT[typ, h], writes=[btt])
                attn_iter(it, lambda kt, hb=hb, qb=qb: kT[hb].ap[:, (4 * qb + kt) * 128:(4 * qb + kt + 1) * 128],
                          lambda kt, hb=hb, qb=qb: vv[hb].ap[:, 4 * qb + kt, :], [kT[hb], vv[hb]],
                          qT[hb].ap[:, qb * 512:(qb + 1) * 512], [qT[hb]], 4, bufs,
                          mixT_s[8 + h, :, qb * 512:(qb + 1) * 512], MIX_TK[8 + h], bias=(btt.ap, btt))
                it += 1

    H1_TK = [Tk() for _ in range(NT)]
    H1N_TK = [Tk() for _ in range(4)]

    def phase5():
        reset(BASE)
        wo = sb([16, D], BF16)
        mixT = [sb([16, 512], BF16) for _ in range(1)]
        gpo = sb([D], F32)
        gpl = sb([D], F32)
        xt = [sb([D], F32) for _ in range(2)]
        tmp = sb([D], F32)
        h1 = [sb([D], F32) for _ in range(2)]
        xg = sb([D], BF16)
        junk = sb([D], BF16)
        hst = [sb([16, 512], BF16) for _ in range(2)]
        ss = [sb([1], F32) for _ in range(2)]
        r = [sb([1], F32) for _ in range(2)]
        ss1 = [sb([1], F32) for _ in range(2)]
        r1 = [sb([1], F32) for _ in range(2)]
        for n in range(4):
            P.dma("pool", wo.ap[:, :, n * 512:(n + 1) * 512], w_o[:, n * 512:(n + 1) * 512].rearrange("(c p) n -> p c n", p=128),
                  writes=[wo], add=(n > 0))
        P.dma("sp", gpo.ap, gains[1], writes=[gpo])
        P.dma("sp", gpl.ap, gains[2], writes=[gpl])

        def A(t):
            sbk, tt = t // 4, t % 4
            mx = mixT[0]
            b = t % 2
            g0 = 0 if t % 2 == 0 else 4
            if tt == 0:
                P.dma("sp", mx.ap, mixT_s[:, :, sbk * 512:(sbk + 1) * 512].rearrange("c d t -> d c t"), reads=MIX_TK, writes=[mx])
            P.dma("sp", xt[b].ap, x_ext[(t + 2) * 128:(t + 3) * 128, :], writes=[xt[b]])
            for n in range(4):
                for c in range(16):
                    P.op("pe", MM(ps[:, g0 + n, :], mx.ap[:, c, tt * 128:(tt + 1) * 128], wo.ap[:, c, n * 512:(n + 1) * 512], c == 0, c == 15),
                         reads=[mx, wo], writes=[PSB[g0 + n]], signal=(c == 15))

        def B(t):
            sbk, tt = t // 4, t % 4
            b = t % 2
            g0 = 0 if t % 2 == 0 else 4
            pall = ps[:, g0:g0 + 4, :]
            P.op("act", ACT(junk.ap.rearrange("p (a b) -> p a b", a=4), pall, AF.Square, accum_out=ss[b].ap[:, 0:1]),
                 reads=PSB[g0:g0 + 4], writes=[ss[b]])
            rstd_ops(ss[b], r[b], D)
            P.op("dve", STT(tmp.ap.rearrange("p (a b) -> p a b", a=4), pall, r[b].ap[:, 0:1], gpo.ap.rearrange("p (a b) -> p a b", a=4), ALU.mult, ALU.mult),
                 reads=PSB[g0:g0 + 4] + [r[b], gpo], writes=[tmp])
            P.op("dve", TT(h1[b].ap, tmp.ap, xt[b].ap, ALU.add), reads=[tmp, xt[b]], writes=[h1[b]])
            P.dma("sp", h1_s[t * 128:(t + 1) * 128, :], h1[b].ap, reads=[h1[b]], writes=[H1_TK[t]])
            norm_to_bf16(h1[b], gpl, junk, ss1[b], r1[b], xg)
            hs = hst[sbk % 2]
            transpose16(xg, lambda c0, c1, hs=hs, tt=tt: hs.ap[:, c0:c1, tt * 128:(tt + 1) * 128], [hs], banks=(g0, g0 + 1))
            if tt == 3:
                P.dma("sp", h1nT_s[sbk], hs.ap, reads=[hs], writes=[H1N_TK[sbk]])

        A(0)
        for t in range(NT):
            if t + 1 < NT:
                A(t + 1)
            B(t)

    M_TK = [Tk() for _ in range(4)]

    def phase6():
        reset(BASE)
        aT = sb([64, 512], BF16)
        AT_TK = [Tk() for _ in range(64)]
        hnT = [sb([16, 512], BF16) for _ in range(2)]
        wu = [sb([16, 512], BF16) for _ in range(2)]
        wd = [sb([4, 1024], BF16) for _ in range(3)]
        rt = [sb([512], F32) for _ in range(2)]
        mst = [sb([1024], F32) for _ in range(2)]
        k_up = 0
        k_dn = 0
        k_ms = 0
        for sbk in range(4):
            hn = hnT[sbk % 2]
            P.dma("sp", hn.ap, h1nT_s[sbk], reads=[H1N_TK[sbk]], writes=[hn])
            for fg in range(16):
                w = wu[k_up % 2]
                k_up += 1
                P.dma("sp", w.ap, wup_s[:, fg * 512:(fg + 1) * 512].rearrange("(c p) n -> p c n", p=128), reads=WUP_TK, writes=[w])
                for j in range(4):
                    fc = fg * 4 + j
                    bk = fc % 4
                    for c in range(16):
                        P.op("pe", MM(ps[:, bk, :], w.ap[:, c, j * 128:(j + 1) * 128], hn.ap[:, c, :], c == 0, c == 15),
                             reads=[w, hn], writes=[PSB[bk]], signal=(c == 15))
                    rtt = rt[fc % 2]
                    P.op("act", ACT(rtt.ap, ps[:, bk, :], AF.Relu), reads=[PSB[bk]], writes=[rtt])
                    P.op("dve", TT(aT.ap[:, fc, :], rtt.ap, rtt.ap, ALU.mult), reads=[rtt], writes=[AT_TK[fc]])
            for hf in range(2):
                for fgd in range(16):
                    w = wd[k_dn % 3]
                    k_dn += 1
                    P.dma("sp", w.ap, wdn_s[fgd * 512:(fgd + 1) * 512, hf * 1024:(hf + 1) * 1024].rearrange("(j p) n -> p j n", p=128),
                          reads=WDN_TK[fgd * 4:(fgd + 1) * 4], writes=[w])
                    for j in range(4):
                        fc = fgd * 4 + j
                        for tt in range(4):
                            for n in range(2):
                                bk = tt * 2 + n
                                last = (fc == 63)
                                P.op("pe", MM(ps[:, bk, :], aT.ap[:, fc, tt * 128:(tt + 1) * 128], w.ap[:, j, n * 512:(n + 1) * 512], fc == 0, last),
                                     reads=[AT_TK[fc], w], writes=[PSB[bk]], signal=(last or (j == 3 and tt == 3 and n == 1)))
                for tt in range(4):
                    ms = mst[k_ms % 2]
                    k_ms += 1
                    src = ps[:, tt * 2:tt * 2 + 2, :]
                    dst = ms.ap.rearrange("p (a b) -> p a b", a=2)
                    fn = ACT(dst, src, AF.Copy) if tt % 2 == 0 else CP(dst, src)
                    P.op("act" if tt % 2 == 0 else "dve", fn, reads=[PSB[tt * 2], PSB[tt * 2 + 1]], writes=[ms])
                    t = sbk * 4 + tt
                    P.dma("sp", m_s[t * 128:(t + 1) * 128, hf * 1024:(hf + 1) * 1024], ms.ap, reads=[ms], writes=[M_TK[sbk]], add=True)

    def phase7():
        reset(BASE)
        wg = sb([16, D], BF16)
        wp = sb([2, D], BF16)
        g3 = sb([D], F32)
        g4 = sb([D], F32)
        g5 = sb([D], F32)
        mt = [sb([D], F32) for _ in range(2)]
        h1 = [sb([D], F32) for _ in range(2)]
        pt = [sb([2, 128], BF16) for _ in range(2)]
        tmp = sb([D], F32)
        xg = sb([D], BF16)
        junk = sb([D], BF16)
        hnT = [sb([16, 128], BF16) for _ in range(2)]
        sg = [sb([1024], F32) for _ in range(2)]
        et = sb([D], F32)
        yt = [sb([D], F32) for _ in range(2)]
        ssA = [sb([1], F32) for _ in range(2)]
        rA = [sb([1], F32) for _ in range(2)]
        ssB = [sb([1], F32) for _ in range(2)]
        rB = [sb([1], F32) for _ in range(2)]
        ssC = [sb([1], F32) for _ in range(2)]
        rC = [sb([1], F32) for _ in range(2)]
        for n in range(4):
            P.dma("pool", wg.ap[:, :, n * 512:(n + 1) * 512], w_g[:, n * 512:(n + 1) * 512].rearrange("(c p) n -> p c n", p=128),
                  writes=[wg], add=(n > 0))
        P.dma("pool", wp.ap, w_p.rearrange("(c p) n -> p c n", p=128), writes=[wp])
        P.dma("sp", g3.ap, gains[3], writes=[g3])
        P.dma("sp", g4.ap, gains[4], writes=[g4])
        P.dma("sp", g5.ap, gains[5], writes=[g5])
        Y_TK = []

        def B1(t):
            b = t % 2
            P.dma("sp", mt[b].ap, m_s[t * 128:(t + 1) * 128, :], reads=[M_TK[t // 4]], writes=[mt[b]])
            P.dma("sp", h1[b].ap, h1_s[t * 128:(t + 1) * 128, :], reads=[H1_TK[t]], writes=[h1[b]])
            P.dma("pool", pt[b].ap, pT_in[:, t * 128:(t + 1) * 128].rearrange("(c p) t -> p c t", p=128), writes=[pt[b]])
            P.op("act", ACT(junk.ap, mt[b].ap, AF.Square, accum_out=ssA[b].ap[:, 0:1]), reads=[mt[b]], writes=[ssA[b]])
            rstd_ops(ssA[b], rA[b], D)
            P.op("dve", STT(tmp.ap, mt[b].ap, rA[b].ap[:, 0:1], g3.ap, ALU.mult, ALU.mult), reads=[mt[b], rA[b], g3], writes=[tmp])
            P.op("dve", TT(h1[b].ap, tmp.ap, h1[b].ap, ALU.add), reads=[tmp, h1[b]], writes=[h1[b]])
            norm_to_bf16(h1[b], g4, junk, ssB[b], rB[b], xg)

        def C(t):
            b = t % 2
            transpose16(xg, lambda c0, c1, b=b: hnT[b].ap[:, c0:c1, :], [hnT[b]], banks=(6, 7))

        def A(t):
            b = t % 2
            for hf in range(2):
                gb = hf * 2
                for n in range(2):
                    col = hf * 1024 + n * 512
                    for c in range(16):
                        P.op("pe", MM(ps[:, gb + n, :], hnT[b].ap[:, c, :], wg.ap[:, c, col:col + 512], c == 0, c == 15),
                             reads=[hnT[b], wg], writes=[PSB[gb + n]], signal=(c == 15))
                for n in range(2):
                    col = hf * 1024 + n * 512
                    for c in range(2):
                        P.op("pe", MM(ps[:, 4 + n, :], pt[b].ap[:, c, :], wp.ap[:, c, col:col + 512], c == 0, c == 1),
                             reads=[pt[b], wp], writes=[PSB[4 + n]], signal=(c == 1))
                sgt = sg[hf]
                P.op("act", ACT(sgt.ap.rearrange("p (a b) -> p a b", a=2), ps[:, gb:gb + 2, :], AF.Sigmoid),
                     reads=[PSB[gb], PSB[gb + 1]], writes=[sgt])
                P.op("dve", TT(et.ap[:, hf * 1024:(hf + 1) * 1024].rearrange("p (a b) -> p a b", a=2), ps[:, 4:6, :],
                               sgt.ap.rearrange("p (a b) -> p a b", a=2), ALU.mult),
                     reads=[PSB[4], PSB[5], sgt], writes=[et], add=(hf == 1))

        def B2(t):
            b = t % 2
            P.op("act", ACT(junk.ap, et.ap, AF.Square, accum_out=ssC[b].ap[:, 0:1]), reads=[et], writes=[ssC[b]])
            rstd_ops(ssC[b], rC[b], D)
            P.op("dve", STT(tmp.ap, et.ap, rC[b].ap[:, 0:1], g5.ap, ALU.mult, ALU.mult), reads=[et, rC[b], g5], writes=[tmp])
            P.op("dve", TT(yt[b].ap, tmp.ap, h1[b].ap, ALU.add), reads=[tmp, h1[b]], writes=[yt[b]])
            ytk = Tk()
            P.dma("sp", y_out[t * 128:(t + 1) * 128, :], yt[b].ap, reads=[yt[b]], writes=[ytk])
            Y_TK.append(ytk)

        B1(0)
        C(0)
        for t in range(NT):
            if t + 1 < NT:
                B1(t + 1)
            A(t)
            if t + 1 < NT:
                C(t + 1)
            B2(t)
        return Y_TK

    KA_TK, VA_TK = phase1()
    P.barrier()
    OUT_TK = phase2()
    P.barrier()
    phase3(KA_TK, VA_TK, OUT_TK["qa"])
    P.barrier()
    phase4(OUT_TK)
    P.barrier()
    phase5()
    P.barrier()
    phase6()
    P.barrier()
    Y_TK = phase7()
    P.barrier()
    P.op("sp", lambda e: e.nop(), signal=False)
    for e in ("pe", "act", "dve", "pool"):
        P.bar[e] = []

    sem_names = ["pe", "act", "dve", "pool"] + ["d_sp_%d" % i for i in range(P.ring["sp"])] + ["d_pool_%d" % i for i in range(P.ring["pool"])]
    sem_cms = [nc.semaphore(n) for n in sem_names]
    sems = {n: cm.__enter__() for n, cm in zip(sem_names, sem_cms)}

    def replay(name, e):
        for ws, fn, sig in P.q[name]:
            for k, v in ws:
                e.wait_ge(sems[k], v)
            ins = fn(e)
            if sig is not None:
                ins.then_inc(sems[sig[0]], sig[1])

    with nc.Block() as block:
        @block.tensor
        def _(e):
            replay("pe", e)

        @block.scalar
        def _(e):
            replay("act", e)

        @block.vector
        def _(e):
            replay("dve", e)

        @block.gpsimd
        def _(e):
            replay("pool", e)

        @block.sync
        def _(e):
            replay("sp", e)

    for cm in reversed(sem_cms):
        cm.__exit__(None, None, None)
    psum_cm.__exit__(None, None, None)
    arena_cm.__exit__(None, None, None)
    return nc


DEBUG_OUTS = ()


def _rope_tables():
    t = np.arange(S)
    row = (t // 64).astype(np.float32)
    col = (t % 64).astype(np.float32)
    freqs = (np.float32(10000.0) ** (-np.arange(32, dtype=np.float32) / np.float32(32))).astype(np.float32)
    ar = row[:, None] * freqs[None, :]
    ac = col[:, None] * freqs[None, :]
    cos = np.concatenate([np.cos(ar), np.cos(ar), np.cos(ac), np.cos(ac)], axis=1).astype(np.float32)
    sin = np.concatenate([-np.sin(ar), np.sin(ar), -np.sin(ac), np.sin(ac)], axis=1).astype(np.float32)
    return cos, sin


def _bias_tiles(rpb, core):
    out = np.full((3, 8, 8 * 128, 512), NEG, dtype=np.float32)
    for typ, qb in ((0, 0), (1, 1), (2, 3)):
        R = 32 * core + 8 * qb
        q = np.arange(512)
        r = R + q // 64
        c = q % 64
        r0 = np.clip(r - 4, 0, 256 - 8)
        c0 = np.clip(c - 8, 0, 64 - 16)
        k = np.arange(1024)
        kr = (R - 4) + k // 64
        kc = k % 64
        inr = (kr[:, None] >= r0[None, :]) & (kr[:, None] < r0[None, :] + 8)
        inc = (kc[:, None] >= c0[None, :]) & (kc[:, None] < c0[None, :] + 16)
        ok = inr & inc
        dr = np.clip(kr[:, None] - r[None, :] + 7, 0, 14)
        dc = np.clip(kc[:, None] - c[None, :] + 15, 0, 30)
        for h in range(8):
            g = rpb[h][dr, dc]
            out[typ, h] = np.where(ok, g, np.float32(NEG))
    return np.ascontiguousarray(out.reshape(3, 8, 8, 128, 512).transpose(0, 1, 3, 2, 4))


def _prepare_inputs(x, p, pre_mix_norm, w_in, q_norm, k_norm, rel_pos_bias, w_o, post_mix_norm, pre_mlp_norm,
                    w_up, w_down, post_mlp_norm, pre_ple_norm, w_ple_gate, w_ple_proj, post_ple_norm):
    f = lambda a: np.ascontiguousarray(np.asarray(a, dtype=np.float32))
    x2 = f(x)[0]
    p2 = f(p)[0, 0]
    gains = np.stack([np.broadcast_to(f(g)[0][None, :], (128, D)) for g in
                      (pre_mix_norm, post_mix_norm, pre_mlp_norm, post_mlp_norm, pre_ple_norm, post_ple_norm)])
    gains = np.ascontiguousarray(gains)
    gainsT = np.ascontiguousarray(np.stack([f(g)[0].reshape(16, 128).T for g in
                                            (pre_mix_norm, post_mix_norm, pre_mlp_norm, post_mlp_norm, pre_ple_norm, post_ple_norm)]))
    gq = np.ascontiguousarray(np.broadcast_to(np.tile(f(q_norm)[0], 4)[None, :], (128, 512)))
    gk = np.ascontiguousarray(np.broadcast_to(np.tile(f(k_norm)[0], 2)[None, :], (128, 256)))
    cos, sin = _rope_tables()
    rpb = f(rel_pos_bias)[0]
    ident = np.eye(128, dtype=np.float32)
    shared = {
        "x_all": x2, "w_in": f(w_in)[0], "w_o": f(w_o)[0], "w_up": f(w_up)[0], "w_down": f(w_down)[0],
        "w_gate": f(w_ple_gate)[0], "w_proj": f(w_ple_proj)[0], "gains": gains, "gainsT": gainsT, "gq": gq, "gk": gk,
        "cos_all": cos, "sin_all": sin, "ident": ident,
    }
    in_maps = []
    xpad = np.concatenate([np.zeros((256, D), np.float32), x2, np.zeros((256, D), np.float32)], axis=0)
    for c in range(NCORES):
        m = dict(shared)
        m["x_ext"] = np.ascontiguousarray(xpad[c * TOK:c * TOK + ETOK])
        m["pT"] = np.ascontiguousarray(p2[c * TOK:(c + 1) * TOK].T)
        m["cos_own"] = np.ascontiguousarray(cos[c * TOK:(c + 1) * TOK])
        m["sin_own"] = np.ascontiguousarray(sin[c * TOK:(c + 1) * TOK])
        m["biasT"] = _bias_tiles(rpb, c)
        in_maps.append(m)
    return in_maps


_NC_CACHE = {}


def kernel(**inputs):
    in_maps = _prepare_inputs(**inputs)
    if "nc" not in _NC_CACHE:
        _NC_CACHE["nc"] = build_program()
    nc = _NC_CACHE["nc"]
    res = run_bass_kernel_spmd(nc, in_maps, core_ids=list(range(NCORES)))
    out = np.concatenate([np.asarray(r["y"], dtype=np.float32) for r in res.results], axis=0)
    if DEBUG:
        kernel.last_results = res.results
    return out.reshape(1, S, D)
```

```python
import numpy as np
import concourse.bass as bass
import concourse.mybir as mybir
from concourse.bass_utils import run_bass_kernel_spmd

F32 = mybir.dt.float32
BF16 = mybir.dt.bfloat16
AF = mybir.ActivationFunctionType
ALU = mybir.AluOpType

NCORES = 8
S = 16384
D = 2048
TOK = 2048
NT = 16
NE = 20
ETOK = NE * 128
DFF = 8192
EPS = 1e-6
SCALE = 1.0 / np.sqrt(128.0)
NEG = -30000.0
DEBUG = False


class Tk:
    __slots__ = ("w", "r")

    def __init__(self):
        self.w = {}
        self.r = {}


class Tile:
    __slots__ = ("ap", "tk")

    def __init__(self, ap):
        self.ap = ap
        self.tk = Tk()


def _tk(x):
    return x.tk if isinstance(x, Tile) else x


ENGS = ("pe", "act", "dve", "pool", "sp")


class Prog:
    def __init__(self):
        self.q = {e: [] for e in ENGS}
        self.cnt = {e: 0 for e in ENGS}
        self.waited = {e: {} for e in ENGS}
        self.ring = {"sp": 28, "pool": 12}
        self.dma_i = {"sp": 0, "pool": 0}
        self.dma_last = {}
        self.bar = {e: [] for e in ENGS}

    def _collect(self, eng, reads, writes, extra, add):
        toks = list(extra) + self.bar[eng]
        self.bar[eng] = []
        for t in reads:
            toks.extend(_tk(t).w.items())
        for t in writes:
            t = _tk(t)
            if not add:
                toks.extend(t.w.items())
            toks.extend(t.r.items())
        out = []
        wd = self.waited[eng]
        for k, v in toks:
            if k == "pe" and eng == "pe":
                continue
            if wd.get(k, 0) >= v:
                continue
            wd[k] = v
            out.append((k, v))
        return out

    def _mark(self, tok, reads, writes, add):
        k, v = tok
        for t in reads:
            t = _tk(t)
            if t.r.get(k, 0) < v:
                t.r[k] = v
        for t in writes:
            t = _tk(t)
            if add:
                if t.w.get(k, 0) < v:
                    t.w[k] = v
            else:
                t.w = {k: v}
            t.r = {}

    def op(self, eng, fn, reads=(), writes=(), signal=True, extra=(), add=False):
        ws = self._collect(eng, reads, writes, extra, add)
        if signal:
            self.cnt[eng] += 1
            tok = (eng, self.cnt[eng])
            sig = (eng, 1)
        else:
            tok = (eng, self.cnt[eng] + 1)
            sig = None
        self.q[eng].append((ws, fn, sig))
        self._mark(tok, reads, writes, add)
        return tok

    def dma(self, queue, out_ap, in_ap, reads=(), writes=(), add=False, extra=()):
        i = self.dma_i[queue]
        self.dma_i[queue] += 1
        n = self.ring[queue]
        key = "d_%s_%d" % (queue, i % n)
        val = 16 * (i // n + 1)
        extra = list(extra) + ([(key, val - 16)] if i >= n else [])
        ws = self._collect(queue, reads, writes, extra, add)
        self.q[queue].append((ws, (lambda e, o=out_ap, a=in_ap: e.dma_start(out=o, in_=a)), (key, 16)))
        tok = (key, val)
        self.dma_last[key] = val
        self._mark(tok, reads, writes, add)
        return tok

    def barrier(self):
        toks = [(e, self.cnt[e]) for e in ("pe", "act", "dve", "pool") if self.cnt[e] > 0]
        toks += list(self.dma_last.items())
        for e in ENGS:
            self.bar[e] = self.bar[e] + toks


def MM(out, lhsT, rhs, start, stop):
    return lambda e: e.matmul(out, lhsT, rhs, start=start, stop=stop)


def TR(out, in_, ident):
    return lambda e: e.transpose(out, in_, ident)


def ACT(out, in_, func, **kw):
    return lambda e: e.activation(out=out, in_=in_, func=func, **kw)


def TT(out, in0, in1, op):
    return lambda e: e.tensor_tensor(out=out, in0=in0, in1=in1, op=op)


def TS(out, in0, s1, s2, op0, op1=None):
    if op1 is None:
        return lambda e: e.tensor_scalar(out=out, in0=in0, scalar1=s1, scalar2=None, op0=op0)
    return lambda e: e.tensor_scalar(out=out, in0=in0, scalar1=s1, scalar2=s2, op0=op0, op1=op1)


def STT(out, in0, scalar, in1, op0, op1):
    return lambda e: e.scalar_tensor_tensor(out=out, in0=in0, scalar=scalar, in1=in1, op0=op0, op1=op1)


def CP(out, in_):
    return lambda e: e.tensor_copy(out, in_)


def RCP(out, in_):
    return lambda e: e.reciprocal(out=out, in_=in_)


def MSET(ap, v):
    return lambda e: e.memset(ap, v)


def build_program():
    nc = bass.Bass("TRN2", target_bir_lowering=False)
    P = Prog()

    def din(name, shape, dt=F32):
        return nc.dram_tensor(name, list(shape), dt, kind="ExternalInput").ap()

    def dscr(name, shape, dt=BF16):
        kind = "ExternalOutput" if (DEBUG and name in DEBUG_OUTS) else "Internal"
        return nc.dram_tensor(name, list(shape), dt, kind=kind).ap()

    x_all = din("x_all", [S, D])
    x_ext = din("x_ext", [ETOK, D])
    pT_in = din("pT", [256, TOK])
    w_in = din("w_in", [D, 4608])
    w_o = din("w_o", [D, D])
    w_up = din("w_up", [D, DFF])
    w_dn = din("w_down", [DFF, D])
    w_g = din("w_gate", [D, D])
    w_p = din("w_proj", [256, D])
    gains = din("gains", [6, 128, D])
    gainsT = din("gainsT", [6, 128, 16])
    gq_in = din("gq", [128, 512])
    gk_in = din("gk", [128, 256])
    cos_all = din("cos_all", [S, 128])
    sin_all = din("sin_all", [S, 128])
    cos_own = din("cos_own", [TOK, 128])
    sin_own = din("sin_own", [TOK, 128])
    biasT = din("biasT", [3, 8, 128, 8, 512])
    ident_in = din("ident", [128, 128])
    y_out = nc.dram_tensor("y", [TOK, D], F32, kind="ExternalOutput").ap()

    kaT_s = dscr("kaT_s", [2, 128, S])
    va_s = dscr("va_s", [128, 2, 128, 128])
    qaT_s = dscr("qaT_s", [8, 128, TOK])
    qbT_s = dscr("qbT_s", [8, 128, TOK])
    kbT_s = dscr("kbT_s", [8, 128, ETOK])
    vb_s = dscr("vb_s", [128, 8, NE, 128])
    mixT_s = dscr("mixT_s", [16, 128, TOK])
    h1_s = dscr("h1_s", [TOK, D], F32)
    h1nT_s = dscr("h1nT_s", [4, 128, 16, 512])
    wup_s = dscr("wup_s", [D, DFF])
    wdn_s = dscr("wdn_s", [DFF, D])
    m_s = dscr("m_s", [TOK, D], F32)

    ARENA_W = 48 * 1024
    arena_cm = nc.sbuf_tensor("arena", [128, ARENA_W], F32)
    psum_cm = nc.psum_tensor("ps", [128, 8, 512], F32)
    arena = arena_cm.__enter__()
    ps = psum_cm.__enter__()
    psb = ps[:].bitcast(BF16)
    PSB = [Tk() for _ in range(8)]

    st = {"off": 0}

    def reset(off=0):
        st["off"] = off

    def sb(shape, dt):
        n = int(np.prod(shape))
        nb = n * (2 if dt == BF16 else 4)
        nb = (nb + 63) // 64 * 64
        off = st["off"]
        st["off"] = off + nb
        assert st["off"] <= ARENA_W * 4, "sbuf arena overflow %d" % st["off"]
        v = arena[:, off // 4:(off + nb) // 4]
        if dt == BF16:
            v = v.bitcast(BF16)
        v = v[:, 0:n]
        if len(shape) == 2:
            v = v.rearrange("p (a b) -> p a b", a=shape[0])
        elif len(shape) == 3:
            v = v.rearrange("p (a b c) -> p a b c", a=shape[0], b=shape[1])
        return Tile(v)

    ident = sb([128], BF16)
    ones = sb([128], F32)
    onesb = sb([128], BF16)
    P.dma("pool", ident.ap, ident_in[:, :], writes=[ident])
    P.op("pool", MSET(ones.ap, 1.0), writes=[ones])
    P.op("pool", MSET(onesb.ap, 1.0), writes=[onesb])
    BASE = st["off"]

    WUP_TK = [Tk() for _ in range(16)]
    WDN_TK = [Tk() for _ in range(64)]

    cast_list = [(wup_s[c * 128:(c + 1) * 128, :], w_up[c * 128:(c + 1) * 128, :], WUP_TK[c]) for c in range(16)] + \
                [(wdn_s[c * 128:(c + 1) * 128, :], w_dn[c * 128:(c + 1) * 128, :], WDN_TK[c]) for c in range(64)]

    def issue_weight_casts(n, extra=()):
        for _ in range(n):
            if cast_list:
                o, i_, tk = cast_list.pop(0)
                P.dma("pool", o, i_, writes=[tk], extra=extra)

    def rstd_ops(ss, r, n):
        P.op("dve", TS(r.ap, ss.ap, 1.0 / n, EPS, ALU.mult, ALU.add), reads=[ss], writes=[r])
        P.op("act", ACT(r.ap, r.ap, AF.Sqrt), reads=[r], writes=[r])
        P.op("dve", RCP(r.ap, r.ap), reads=[r], writes=[r])

    def norm_to_bf16(src, gain, junk, ss, r, xg, mode="stt"):
        P.op("act", ACT(junk.ap, src.ap, AF.Square, accum_out=ss.ap[:, 0:1]), reads=[src], writes=[ss])
        rstd_ops(ss, r, D)
        if mode == "fold":
            P.op("dve", TS(xg.ap, src.ap, r.ap[:, 0:1], None, ALU.mult), reads=[src, r], writes=[xg])
        else:
            P.op("dve", STT(xg.ap, src.ap, r.ap[:, 0:1], gain.ap, ALU.mult, ALU.mult), reads=[src, r, gain], writes=[xg])

    def fold_gain(wt, gT, nchunks=16):
        for c in range(nchunks):
            P.op("dve", TS(wt.ap[:, c, :], wt.ap[:, c, :], gT.ap[:, c:c + 1], None, ALU.mult), reads=[wt, gT], writes=[wt])

    def transpose16(xg, dst_fn, dst_tiles, banks=(6, 7)):
        for half in range(2):
            bk = banks[half]
            for j in range(8):
                c = half * 8 + j
                P.op("pe", TR(psb[:, bk, j * 128:(j + 1) * 128], xg.ap[:, c * 128:(c + 1) * 128], ident.ap),
                     reads=[xg, ident], writes=[PSB[bk]], signal=(j == 7))
            eng = "act" if half == 0 else "dve"
            src = psb[:, bk, :].rearrange("p (c t) -> p c t", c=8)
            fn = ACT(dst_fn(half * 8, half * 8 + 8), src, AF.Copy) if eng == "act" else CP(dst_fn(half * 8, half * 8 + 8), src)
            P.op(eng, fn, reads=[PSB[bk]], writes=dst_tiles)

    def norm_rope(src_ap, src_tk, nh, gain, cs, sn, kg, t1, t2, hss, hr, junk, outb, rs=None):
        W = nh * 128
        rr = [rs] if rs is not None else []
        for h in range(nh):
            kw = {"scale": rs.ap[:, 0:1]} if rs is not None else {}
            P.op("act", ACT(junk.ap[:, 0:128], src_ap[:, h * 128:(h + 1) * 128], AF.Square, accum_out=hss.ap[:, h:h + 1], **kw),
                 reads=[src_tk] + rr, writes=[hss], add=True)
        rstd_ops(hss, hr, 128)
        if rs is not None:
            P.op("dve", STT(kg.ap[:, 0:W], src_ap, rs.ap[:, 0:1], gain.ap[:, 0:W], ALU.mult, ALU.mult), reads=[src_tk, gain, rs], writes=[kg])
        else:
            P.op("dve", TT(kg.ap[:, 0:W], src_ap, gain.ap[:, 0:W], ALU.mult), reads=[src_tk, gain], writes=[kg])
        kg3 = kg.ap[:, 0:W].rearrange("p (h d) -> p h d", h=nh)
        t13 = t1.ap[:, 0:W].rearrange("p (h d) -> p h d", h=nh)
        P.op("dve", TT(t13, kg3, cs.ap.unsqueeze(1).broadcast_to([128, nh, 128]), ALU.mult), reads=[kg, cs], writes=[t1])
        kg5 = kg.ap[:, 0:W].rearrange("p (h s f i) -> p h s f i", h=nh, s=2, f=2)
        t25 = t2.ap[:, 0:W].rearrange("p (h s f i) -> p h s f i", h=nh, s=2, f=2)
        sn4 = sn.ap.rearrange("p (s f i) -> p s f i", s=2, f=2)
        for f in range(2):
            P.op("dve", TT(t25[:, :, :, f, :], kg5[:, :, :, 1 - f, :],
                           sn4[:, :, f, :].unsqueeze(1).broadcast_to([128, nh, 2, 32]), ALU.mult),
                 reads=[kg, sn], writes=[t2], add=(f == 1))
        P.op("dve", TT(t1.ap[:, 0:W], t1.ap[:, 0:W], t2.ap[:, 0:W], ALU.add), reads=[t1, t2], writes=[t1])
        for h in range(nh):
            P.op("act", ACT(outb.ap[:, h, :], t1.ap[:, h * 128:(h + 1) * 128], AF.Copy, scale=hr.ap[:, h:h + 1]),
                 reads=[t1, hr], writes=[outb], add=(h > 0))

    def phase1():
        reset(BASE)
        wkv = sb([16, 512], BF16)
        gT = sb([16], F32)
        gk = sb([256], F32)
        xt = [sb([D], F32) for _ in range(2)]
        xg = [sb([D], BF16) for _ in range(2)]
        xnT = [sb([16, 128], BF16) for _ in range(2)]
        junk = sb([D], BF16)
        cs = [sb([128], F32) for _ in range(2)]
        sn = [sb([128], F32) for _ in range(2)]
        ss = [sb([1], F32) for _ in range(2)]
        r = [sb([1], F32) for _ in range(2)]
        hss = [sb([2], F32) for _ in range(2)]
        hr = [sb([2], F32) for _ in range(2)]
        kg = sb([256], F32)
        t1 = sb([256], F32)
        t2 = sb([256], F32)
        kb = [sb([2, 128], BF16) for _ in range(2)]
        kst = [sb([2, 512], BF16) for _ in range(2)]
        vst = [sb([2, 4, 128], BF16) for _ in range(2)]
        KA_TK = [Tk() for _ in range(32)]
        VA_TK = [Tk() for _ in range(32)]

        P.dma("pool", wkv.ap, w_in[:, 1024:1536].rearrange("(c p) n -> p c n", p=128), writes=[wkv])
        P.dma("sp", gT.ap, gainsT[0], writes=[gT])
        P.dma("sp", gk.ap, gk_in[:, :], writes=[gk])
        fold_gain(wkv, gT)

        def front(i):
            b = i % 2
            P.dma("sp", xt[b].ap, x_all[i * 128:(i + 1) * 128, :], writes=[xt[b]])
            P.dma("sp", cs[b].ap, cos_all[i * 128:(i + 1) * 128, :], writes=[cs[b]])
            P.dma("sp", sn[b].ap, sin_all[i * 128:(i + 1) * 128, :], writes=[sn[b]])
            P.op("dve", CP(xg[b].ap, xt[b].ap), reads=[xt[b]], writes=[xg[b]])
            P.op("act", ACT(junk.ap, xt[b].ap, AF.Square, accum_out=ss[b].ap[:, 0:1]), reads=[xt[b]], writes=[ss[b]])
            rstd_ops(ss[b], r[b], D)
            transpose16(xg[b], lambda c0, c1, b=b: xnT[b].ap[:, c0:c1, :], [xnT[b]], banks=(6, 7))
            bank = i % 2
            for c in range(16):
                P.op("pe", MM(ps[:, bank, :], xnT[b].ap[:, c, :], wkv.ap[:, c, :], c == 0, c == 15),
                     reads=[xnT[b], wkv], writes=[PSB[bank]], signal=(c == 15))

        def back(i):
            b = i % 2
            g4 = i // 4
            gb = g4 % 2
            bank = i % 2
            P.op("act", ACT(vst[gb].ap[:, :, i % 4, :], ps[:, bank, 256:512].rearrange("p (k d) -> p k d", k=2), AF.Copy, scale=r[b].ap[:, 0:1]),
                 reads=[PSB[bank], r[b]], writes=[vst[gb]], add=(i % 4 != 0))
            norm_rope(ps[:, bank, 0:256], PSB[bank], 2, gk, cs[b], sn[b], kg, t1, t2, hss[b], hr[b], junk, kb[b], rs=r[b])
            tb = 4 + (i % 2)
            for h in range(2):
                P.op("pe", TR(psb[:, tb, h * 128:(h + 1) * 128], kb[b].ap[:, h, :], ident.ap),
                     reads=[kb[b], ident], writes=[PSB[tb]], signal=(h == 1))
            P.op("dve", CP(kst[gb].ap[:, :, (i % 4) * 128:(i % 4 + 1) * 128], psb[:, tb, 0:256].rearrange("p (k t) -> p k t", k=2)),
                 reads=[PSB[tb]], writes=[kst[gb]], add=(i % 4 != 0))
            if i % 4 == 3:
                P.dma("sp", kaT_s[:, :, g4 * 512:(g4 + 1) * 512].rearrange("k d t -> d k t"), kst[gb].ap,
                      reads=[kst[gb]], writes=[KA_TK[g4]])
                P.dma("sp", va_s[:, :, g4 * 4:(g4 + 1) * 4, :], vst[gb].ap, reads=[vst[gb]], writes=[VA_TK[g4]])

        for i in range(129):
            if i < 128:
                front(i)
            if i >= 1:
                back(i - 1)
        return KA_TK, VA_TK

    def phase2():
        reset(BASE)
        xnT = sb([16, ETOK], BF16)
        XN_TK = [Tk() for _ in range(NE)]
        gT = sb([16], F32)
        gq = sb([512], F32)
        mark = st["off"]
        xt = [sb([D], F32) for _ in range(2)]
        xg = [sb([D], BF16) for _ in range(2)]
        junk = sb([D], BF16)
        ss = [sb([1], F32) for _ in range(2)]
        r = [sb([1], F32) for _ in range(2)]

        P.dma("sp", gT.ap, gainsT[0], writes=[gT])
        P.dma("sp", gq.ap, gq_in[:, :], writes=[gq])
        for e in range(NE):
            b = e % 2
            P.dma("sp", xt[b].ap, x_ext[e * 128:(e + 1) * 128, :], writes=[xt[b]])
            norm_to_bf16(xt[b], None, junk, ss[b], r[b], xg[b], mode="fold")
            transpose16(xg[b], lambda c0, c1, e=e: xnT.ap[:, c0:c1, e * 128:(e + 1) * 128], [XN_TK[e]],
                        banks=(6, 7) if e % 2 == 0 else (4, 5))
        P.barrier()
        reset(mark)
        wblk = [sb([16, 512], BF16) for _ in range(2)]
        stage = [sb([4 * ETOK], BF16) for _ in range(2)]
        junk = sb([D], BF16)
        cs = [sb([128], F32) for _ in range(2)]
        sn = [sb([128], F32) for _ in range(2)]
        hss = [sb([4], F32) for _ in range(2)]
        hr = [sb([4], F32) for _ in range(2)]
        kg = sb([512], F32)
        t1 = sb([512], F32)
        t2 = sb([512], F32)
        qb_ = [sb([4, 128], BF16) for _ in range(2)]

        blocks = [("qa", 0, 0), ("qa", 1, 512), ("qb", 0, 1536), ("qb", 1, 2048),
                  ("kb", 0, 2560), ("kb", 1, 3072), ("vb", 0, 3584), ("vb", 1, 4096)]
        bankrot = [0]

        def nbank():
            bk = bankrot[0] % 4
            bankrot[0] += 1
            return bk

        OUT_TK = {"qa": [Tk(), Tk()], "qb": [Tk(), Tk()], "kb": [Tk(), Tk()], "vb": [Tk(), Tk()]}
        for bi, (kind, half, col) in enumerate(blocks):
            wb = wblk[bi % 2]
            sg = stage[bi % 2]
            P.dma("pool", wb.ap, w_in[:, col:col + 512].rearrange("(c p) n -> p c n", p=128), writes=[wb])
            fold_gain(wb, gT)
            if kind == "qa":
                sgv = sg.ap[:, 0:4 * TOK].rearrange("p (h t) -> p h t", h=4)
                for t in range(NT):
                    e = t + 2
                    b = t % 2
                    P.dma("sp", cs[b].ap, cos_own[t * 128:(t + 1) * 128, :], writes=[cs[b]])
                    P.dma("sp", sn[b].ap, sin_own[t * 128:(t + 1) * 128, :], writes=[sn[b]])
                    bk = nbank()
                    for c in range(16):
                        P.op("pe", MM(ps[:, bk, :], xnT.ap[:, c, e * 128:(e + 1) * 128], wb.ap[:, c, :], c == 0, c == 15),
                             reads=[XN_TK[e], wb], writes=[PSB[bk]], signal=(c == 15))
                    norm_rope(ps[:, bk, :], PSB[bk], 4, gq, cs[b], sn[b], kg, t1, t2, hss[b], hr[b], junk, qb_[b])
                    tb = 4 + (t % 2)
                    for h in range(4):
                        P.op("pe", TR(psb[:, tb, h * 128:(h + 1) * 128], qb_[b].ap[:, h, :], ident.ap),
                             reads=[qb_[b], ident], writes=[PSB[tb]], signal=(h == 3))
                    P.op("dve", CP(sgv[:, :, t * 128:(t + 1) * 128], psb[:, tb, 0:512].rearrange("p (h t) -> p h t", h=4)),
                         reads=[PSB[tb]], writes=[sg], add=(t > 0))
                P.dma("sp", qaT_s[half * 4:(half + 1) * 4].rearrange("h d t -> d h t"), sgv, reads=[sg], writes=[OUT_TK[kind][half]])
            elif kind in ("qb", "kb"):
                ntok = TOK if kind == "qb" else ETOK
                e0 = 2 if kind == "qb" else 0
                sgv = sg.ap[:, 0:4 * ntok].rearrange("p (h t) -> p h t", h=4)
                first = True
                for j in range(4):
                    for sbk in range(ntok // 512):
                        tk0 = e0 * 128 + sbk * 512
                        bk = nbank()
                        for c in range(16):
                            P.op("pe", MM(ps[:, bk, :], wb.ap[:, c, j * 128:(j + 1) * 128], xnT.ap[:, c, tk0:tk0 + 512], c == 0, c == 15),
                                 reads=[wb] + XN_TK[tk0 // 128:tk0 // 128 + 4], writes=[PSB[bk]], signal=(c == 15))
                        eng = "act" if (sbk % 2 == 0) else "dve"
                        dst = sgv[:, j, sbk * 512:(sbk + 1) * 512]
                        fn = ACT(dst, ps[:, bk, :], AF.Copy) if eng == "act" else CP(dst, ps[:, bk, :])
                        P.op(eng, fn, reads=[PSB[bk]], writes=[sg], add=(not first))
                        first = False
                dst_s = qbT_s if kind == "qb" else kbT_s
                P.dma("sp", dst_s[half * 4:(half + 1) * 4].rearrange("h d t -> d h t"), sgv, reads=[sg], writes=[OUT_TK[kind][half]])
            else:
                sgv = sg.ap[:, 0:NE * 512].rearrange("p (h e d) -> p h e d", h=4, e=NE)
                for e in range(NE):
                    bk = nbank()
                    for c in range(16):
                        P.op("pe", MM(ps[:, bk, :], xnT.ap[:, c, e * 128:(e + 1) * 128], wb.ap[:, c, :], c == 0, c == 15),
                             reads=[XN_TK[e], wb], writes=[PSB[bk]], signal=(c == 15))
                    eng = "act" if (e % 2 == 0) else "dve"
                    dst = sgv[:, :, e, :]
                    src = ps[:, bk, :].rearrange("p (h d) -> p h d", h=4)
                    fn = ACT(dst, src, AF.Copy) if eng == "act" else CP(dst, src)
                    P.op(eng, fn, reads=[PSB[bk]], writes=[sg], add=(e > 0))
                P.dma("sp", vb_s[:, half * 4:(half + 1) * 4, :, :], sgv, reads=[sg], writes=[OUT_TK[kind][half]])
        return OUT_TK

    def attn_iter(it, kT_fn, v_fn, k_reads, q_ap, q_reads, ngroups, bufs, out_dst, out_tk, bias=None):
        pT, accD, accP, sbias, rl, ost = bufs["pT"], bufs["accD"], bufs["accP"], bufs["sbias"], bufs["rl"], bufs["ost"]
        ob = 4 + (it % 2)
        ostb = ost[it % 2]

        def qk(j):
            s0 = (j % 2) * 2
            for u in range(2):
                P.op("pe", MM(ps[:, s0 + u, :], kT_fn(2 * j + u), q_ap, True, True),
                     reads=k_reads + q_reads, writes=[PSB[s0 + u]], signal=(u == 1))

        def softmax_pv(j):
            s0 = (j % 2) * 2
            pt = pT[j % 3]
            src = ps[:, s0:s0 + 2, :]
            if bias is not None:
                sbt = sbias[j % 2]
                P.op("dve", STT(sbt.ap, src, float(SCALE), bias[0][:, 2 * j:2 * j + 2, :], ALU.mult, ALU.add),
                     reads=[PSB[s0], PSB[s0 + 1], bias[1]], writes=[sbt])
                P.op("act", ACT(pt.ap, sbt.ap, AF.Exp), reads=[sbt], writes=[pt])
            else:
                P.op("act", ACT(pt.ap, src, AF.Exp, scale=float(SCALE)), reads=[PSB[s0], PSB[s0 + 1]], writes=[pt])
            for u in range(2):
                P.op("pe", MM(ps[:, ob, :], v_fn(2 * j + u), pt.ap[:, u, :], (j == 0 and u == 0), (j == ngroups - 1 and u == 1)),
                     reads=k_reads + [pt], writes=[PSB[ob]], signal=(u == 1))
            if j % 4 == 3:
                for u in range(2):
                    P.op("pe", MM(ps[:, 6, :], onesb.ap, pt.ap[:, u, :], (j == 3 and u == 0), False),
                         reads=[pt, onesb], writes=[PSB[6]], signal=(u == 1))
            elif j == 0:
                P.op("dve", CP(accD.ap, pt.ap), reads=[pt], writes=[accD])
            else:
                P.op("dve", TT(accD.ap, accD.ap, pt.ap, ALU.add), reads=[pt, accD], writes=[accD])

        qk(0)
        for j in range(ngroups):
            if j + 1 < ngroups:
                qk(j + 1)
            softmax_pv(j)
        P.op("dve", TT(accD.ap[:, 0, :], accD.ap[:, 0, :], accD.ap[:, 1, :], ALU.add), reads=[accD], writes=[accD])
        P.op("pe", MM(ps[:, 6, :], ones.ap, accD.ap[:, 0, :], False, True), reads=[accD, ones], writes=[PSB[6]])
        P.op("dve", RCP(rl.ap, ps[:, 6, :]), reads=[PSB[6]], writes=[rl])
        P.op("dve", TT(ostb.ap, ps[:, ob, :], rl.ap, ALU.mult), reads=[PSB[ob], rl], writes=[ostb])
        P.dma("sp", out_dst, ostb.ap, reads=[ostb], writes=[out_tk], add=True)

    def attn_bufs():
        return {
            "pT": [sb([2, 512], BF16) for _ in range(3)],
            "accD": sb([2, 512], F32), "accP": sb([2, 512], F32),
            "sbias": [sb([2, 512], F32) for _ in range(2)],
            "rl": sb([512], F32),
            "ost": [sb([512], BF16) for _ in range(2)],
        }

    MIX_TK = [Tk() for _ in range(16)]

    def phase3(KA_TK, VA_TK, QA_TK):
        reset(BASE)
        kT = sb([S], BF16)
        vv = sb([128, 128], BF16)
        qT = sb([4, TOK], BF16)
        bufs = attn_bufs()
        it = 0
        for kv in range(2):
            for c4 in range(4):
                P.dma("sp", kT.ap[:, c4 * 4096:(c4 + 1) * 4096], kaT_s[kv, :, c4 * 4096:(c4 + 1) * 4096],
                      reads=KA_TK[c4 * 8:(c4 + 1) * 8], writes=[kT], add=(c4 > 0))
                P.dma("sp", vv.ap[:, c4 * 32:(c4 + 1) * 32, :], va_s[:, kv, c4 * 32:(c4 + 1) * 32, :],
                      reads=VA_TK[c4 * 8:(c4 + 1) * 8], writes=[vv], add=(c4 > 0))
            P.dma("sp", qT.ap, qaT_s[kv * 4:(kv + 1) * 4].rearrange("h d t -> d h t"), reads=[QA_TK[kv]], writes=[qT])
            for g in range(4):
                h = kv * 4 + g
                for qb in range(4):
                    issue_weight_casts(3, extra=[("act", P.cnt["act"])] if P.cnt["act"] else ())
                    attn_iter(it, lambda kt: kT.ap[:, kt * 128:(kt + 1) * 128], lambda kt: vv.ap[:, kt, :], [kT, vv],
                              qT.ap[:, g, qb * 512:(qb + 1) * 512], [qT], 64, bufs,
                              mixT_s[h, :, qb * 512:(qb + 1) * 512], MIX_TK[h])
                    it += 1

    def phase4(OUT_TK):
        reset(BASE)
        issue_weight_casts(100)
        kT = [sb([ETOK], BF16) for _ in range(2)]
        vv = [sb([NE, 128], BF16) for _ in range(2)]
        qT = [sb([TOK], BF16) for _ in range(2)]
        bt = [sb([8, 512], F32) for _ in range(2)]
        bufs = attn_bufs()
        it = 0
        for h in range(8):
            hb = h % 2
            P.dma("sp", kT[hb].ap, kbT_s[h], reads=[OUT_TK["kb"][h // 4]], writes=[kT[hb]])
            P.dma("sp", vv[hb].ap, vb_s[:, h, :, :], reads=[OUT_TK["vb"][h // 4]], writes=[vv[hb]])
            P.dma("sp", qT[hb].ap, qbT_s[h], reads=[OUT_TK["qb"][h // 4]], writes=[qT[hb]])
            for qb in range(4):
                typ = 0 if qb == 0 else (2 if qb == 3 else 1)
                btt = bt[it % 2]
                P.dma("sp", btt.ap, biasT[typ, h], writes=[btt])
                attn_iter(it, lambda kt, hb=hb, qb=qb: kT[hb].ap[:, (4 * qb + kt) * 128:(4 * qb + kt + 1) * 128],
                          lambda kt, hb=hb, qb=qb: vv[hb].ap[:, 4 * qb + kt, :], [kT[hb], vv[hb]],
                          qT[hb].ap[:, qb * 512:(qb + 1) * 512], [qT[hb]], 4, bufs,
                          mixT_s[8 + h, :, qb * 512:(qb + 1) * 512], MIX_TK[8 + h], bias=(btt.ap, btt))
                it += 1

    H1_TK = [Tk() for _ in range(NT)]
    H1N_TK = [Tk() for _ in range(4)]

    def phase5():
        reset(BASE)
        wo = sb([16, D], BF16)
        mixT = [sb([16, 512], BF16) for _ in range(1)]
        gpo = sb([D], F32)
        gpl = sb([D], F32)
        xt = [sb([D], F32) for _ in range(2)]
        tmp = sb([D], F32)
        h1 = [sb([D], F32) for _ in range(2)]
        xg = sb([D], BF16)
        junk = sb([D], BF16)
        hst = [sb([16, 512], BF16) for _ in range(2)]
        ss = [sb([1], F32) for _ in range(2)]
        r = [sb([1], F32) for _ in range(2)]
        ss1 = [sb([1], F32) for _ in range(2)]
        r1 = [sb([1], F32) for _ in range(2)]
        for n in range(4):
            P.dma("pool", wo.ap[:, :, n * 512:(n + 1) * 512], w_o[:, n * 512:(n + 1) * 512].rearrange("(c p) n -> p c n", p=128),
                  writes=[wo], add=(n > 0))
        P.dma("sp", gpo.ap, gains[1], writes=[gpo])
        P.dma("sp", gpl.ap, gains[2], writes=[gpl])

        def A(t):
            sbk, tt = t // 4, t % 4
            mx = mixT[0]
            b = t % 2
            g0 = 0 if t % 2 == 0 else 4
            if tt == 0:
                P.dma("sp", mx.ap, mixT_s[:, :, sbk * 512:(sbk + 1) * 512].rearrange("c d t -> d c t"), reads=MIX_TK, writes=[mx])
            P.dma("sp", xt[b].ap, x_ext[(t + 2) * 128:(t + 3) * 128, :], writes=[xt[b]])
            for n in range(4):
                for c in range(16):
                    P.op("pe", MM(ps[:, g0 + n, :], mx.ap[:, c, tt * 128:(tt + 1) * 128], wo.ap[:, c, n * 512:(n + 1) * 512], c == 0, c == 15),
                         reads=[mx, wo], writes=[PSB[g0 + n]], signal=(c == 15))

        def B(t):
            sbk, tt = t // 4, t % 4
            b = t % 2
            g0 = 0 if t % 2 == 0 else 4
            pall = ps[:, g0:g0 + 4, :]
            P.op("act", ACT(junk.ap.rearrange("p (a b) -> p a b", a=4), pall, AF.Square, accum_out=ss[b].ap[:, 0:1]),
                 reads=PSB[g0:g0 + 4], writes=[ss[b]])
            rstd_ops(ss[b], r[b], D)
            P.op("dve", STT(tmp.ap.rearrange("p (a b) -> p a b", a=4), pall, r[b].ap[:, 0:1], gpo.ap.rearrange("p (a b) -> p a b", a=4), ALU.mult, ALU.mult),
                 reads=PSB[g0:g0 + 4] + [r[b], gpo], writes=[tmp])
            P.op("dve", TT(h1[b].ap, tmp.ap, xt[b].ap, ALU.add), reads=[tmp, xt[b]], writes=[h1[b]])
            P.dma("sp", h1_s[t * 128:(t + 1) * 128, :], h1[b].ap, reads=[h1[b]], writes=[H1_TK[t]])
            norm_to_bf16(h1[b], gpl, junk, ss1[b], r1[b], xg)
            hs = hst[sbk % 2]
            transpose16(xg, lambda c0, c1, hs=hs, tt=tt: hs.ap[:, c0:c1, tt * 128:(tt + 1) * 128], [hs], banks=(g0, g0 + 1))
            if tt == 3:
                P.dma("sp", h1nT_s[sbk], hs.ap, reads=[hs], writes=[H1N_TK[sbk]])

        A(0)
        for t in range(NT):
            if t + 1 < NT:
                A(t + 1)
            B(t)

    M_TK = [Tk() for _ in range(4)]

    def phase6():
        reset(BASE)
        aT = sb([64, 512], BF16)
        AT_TK = [Tk() for _ in range(64)]
        hnT = [sb([16, 512], BF16) for _ in range(2)]
        wu = [sb([16, 512], BF16) for _ in range(2)]
        wd = [sb([4, 1024], BF16) for _ in range(3)]
        rt = [sb([512], F32) for _ in range(2)]
        mst = [sb([1024], F32) for _ in range(2)]
        k_up = 0
        k_dn = 0
        k_ms = 0
        for sbk in range(4):
            hn = hnT[sbk % 2]
            P.dma("sp", hn.ap, h1nT_s[sbk], reads=[H1N_TK[sbk]], writes=[hn])
            for fg in range(16):
                w = wu[k_up % 2]
                k_up += 1
                P.dma("sp", w.ap, wup_s[:, fg * 512:(fg + 1) * 512].rearrange("(c p) n -> p c n", p=128), reads=WUP_TK, writes=[w])
                for j in range(4):
                    fc = fg * 4 + j
                    bk = fc % 4
                    for c in range(16):
                        P.op("pe", MM(ps[:, bk, :], w.ap[:, c, j * 128:(j + 1) * 128], hn.ap[:, c, :], c == 0, c == 15),
                             reads=[w, hn], writes=[PSB[bk]], signal=(c == 15))
                    rtt = rt[fc % 2]
                    P.op("act", ACT(rtt.ap, ps[:, bk, :], AF.Relu), reads=[PSB[bk]], writes=[rtt])
                    P.op("dve", TT(aT.ap[:, fc, :], rtt.ap, rtt.ap, ALU.mult), reads=[rtt], writes=[AT_TK[fc]])
            for hf in range(2):
                for fgd in range(16):
                    w = wd[k_dn % 3]
                    k_dn += 1
                    P.dma("sp", w.ap, wdn_s[fgd * 512:(fgd + 1) * 512, hf * 1024:(hf + 1) * 1024].rearrange("(j p) n -> p j n", p=128),
                          reads=WDN_TK[fgd * 4:(fgd + 1) * 4], writes=[w])
                    for j in range(4):
                        fc = fgd * 4 + j
                        for tt in range(4):
                            for n in range(2):
                                bk = tt * 2 + n
                                last = (fc == 63)
                                P.op("pe", MM(ps[:, bk, :], aT.ap[:, fc, tt * 128:(tt + 1) * 128], w.ap[:, j, n * 512:(n + 1) * 512], fc == 0, last),
                                     reads=[AT_TK[fc], w], writes=[PSB[bk]], signal=(last or (j == 3 and tt == 3 and n == 1)))
                for tt in range(4):
                    ms = mst[k_ms % 2]
                    k_ms += 1
                    src = ps[:, tt * 2:tt * 2 + 2, :]
                    dst = ms.ap.rearrange("p (a b) -> p a b", a=2)
                    fn = ACT(dst, src, AF.Copy) if tt % 2 == 0 else CP(dst, src)
                    P.op("act" if tt % 2 == 0 else "dve", fn, reads=[PSB[tt * 2], PSB[tt * 2 + 1]], writes=[ms])
                    t = sbk * 4 + tt
                    P.dma("sp", m_s[t * 128:(t + 1) * 128, hf * 1024:(hf + 1) * 1024], ms.ap, reads=[ms], writes=[M_TK[sbk]], add=True)

    def phase7():
        reset(BASE)
        wg = sb([16, D], BF16)
        wp = sb([2, D], BF16)
        g3 = sb([D], F32)
        g4 = sb([D], F32)
        g5 = sb([D], F32)
        mt = [sb([D], F32) for _ in range(2)]
        h1 = [sb([D], F32) for _ in range(2)]
        pt = [sb([2, 128], BF16) for _ in range(2)]
        tmp = sb([D], F32)
        xg = sb([D], BF16)
        junk = sb([D], BF16)
        hnT = [sb([16, 128], BF16) for _ in range(2)]
        sg = [sb([1024], F32) for _ in range(2)]
        et = sb([D], F32)
        yt = [sb([D], F32) for _ in range(2)]
        ssA = [sb([1], F32) for _ in range(2)]
        rA = [sb([1], F32) for _ in range(2)]
        ssB = [sb([1], F32) for _ in range(2)]
        rB = [sb([1], F32) for _ in range(2)]
        ssC = [sb([1], F32) for _ in range(2)]
        rC = [sb([1], F32) for _ in range(2)]
        for n in range(4):
            P.dma("pool", wg.ap[:, :, n * 512:(n + 1) * 512], w_g[:, n * 512:(n + 1) * 512].rearrange("(c p) n -> p c n", p=128),
                  writes=[wg], add=(n > 0))
        P.dma("pool", wp.ap, w_p.rearrange("(c p) n -> p c n", p=128), writes=[wp])
        P.dma("sp", g3.ap, gains[3], writes=[g3])
        P.dma("sp", g4.ap, gains[4], writes=[g4])
        P.dma("sp", g5.ap, gains[5], writes=[g5])
        Y_TK = []

        def B1(t):
            b = t % 2
            P.dma("sp", mt[b].ap, m_s[t * 128:(t + 1) * 128, :], reads=[M_TK[t // 4]], writes=[mt[b]])
            P.dma("sp", h1[b].ap, h1_s[t * 128:(t + 1) * 128, :], reads=[H1_TK[t]], writes=[h1[b]])
            P.dma("pool", pt[b].ap, pT_in[:, t * 128:(t + 1) * 128].rearrange("(c p) t -> p c t", p=128), writes=[pt[b]])
            P.op("act", ACT(junk.ap, mt[b].ap, AF.Square, accum_out=ssA[b].ap[:, 0:1]), reads=[mt[b]], writes=[ssA[b]])
            rstd_ops(ssA[b], rA[b], D)
            P.op("dve", STT(tmp.ap, mt[b].ap, rA[b].ap[:, 0:1], g3.ap, ALU.mult, ALU.mult), reads=[mt[b], rA[b], g3], writes=[tmp])
            P.op("dve", TT(h1[b].ap, tmp.ap, h1[b].ap, ALU.add), reads=[tmp, h1[b]], writes=[h1[b]])
            norm_to_bf16(h1[b], g4, junk, ssB[b], rB[b], xg)

        def C(t):
            b = t % 2
            transpose16(xg, lambda c0, c1, b=b: hnT[b].ap[:, c0:c1, :], [hnT[b]], banks=(6, 7))

        def A(t):
            b = t % 2
            for hf in range(2):
                gb = hf * 2
                for n in range(2):
                    col = hf * 1024 + n * 512
                    for c in range(16):
                        P.op("pe", MM(ps[:, gb + n, :], hnT[b].ap[:, c, :], wg.ap[:, c, col:col + 512], c == 0, c == 15),
                             reads=[hnT[b], wg], writes=[PSB[gb + n]], signal=(c == 15))
                for n in range(2):
                    col = hf * 1024 + n * 512
                    for c in range(2):
                        P.op("pe", MM(ps[:, 4 + n, :], pt[b].ap[:, c, :], wp.ap[:, c, col:col + 512], c == 0, c == 1),
                             reads=[pt[b], wp], writes=[PSB[4 + n]], signal=(c == 1))
                sgt = sg[hf]
                P.op("act", ACT(sgt.ap.rearrange("p (a b) -> p a b", a=2), ps[:, gb:gb + 2, :], AF.Sigmoid),
                     reads=[PSB[gb], PSB[gb + 1]], writes=[sgt])
                P.op("dve", TT(et.ap[:, hf * 1024:(hf + 1) * 1024].rearrange("p (a b) -> p a b", a=2), ps[:, 4:6, :],
                               sgt.ap.rearrange("p (a b) -> p a b", a=2), ALU.mult),
                     reads=[PSB[4], PSB[5], sgt], writes=[et], add=(hf == 1))

        def B2(t):
            b = t % 2
            P.op("act", ACT(junk.ap, et.ap, AF.Square, accum_out=ssC[b].ap[:, 0:1]), reads=[et], writes=[ssC[b]])
            rstd_ops(ssC[b], rC[b], D)
            P.op("dve", STT(tmp.ap, et.ap, rC[b].ap[:, 0:1], g5.ap, ALU.mult, ALU.mult), reads=[et, rC[b], g5], writes=[tmp])
            P.op("dve", TT(yt[b].ap, tmp.ap, h1[b].ap, ALU.add), reads=[tmp, h1[b]], writes=[yt[b]])
            ytk = Tk()
            P.dma("sp", y_out[t * 128:(t + 1) * 128, :], yt[b].ap, reads=[yt[b]], writes=[ytk])
            Y_TK.append(ytk)

        B1(0)
        C(0)
        for t in range(NT):
            if t + 1 < NT:
                B1(t + 1)
            A(t)
            if t + 1 < NT:
                C(t + 1)
            B2(t)
        return Y_TK

    KA_TK, VA_TK = phase1()
    P.barrier()
    OUT_TK = phase2()
    P.barrier()
    phase3(KA_TK, VA_TK, OUT_TK["qa"])
    P.barrier()
    phase4(OUT_TK)
    P.barrier()
    phase5()
    P.barrier()
    phase6()
    P.barrier()
    Y_TK = phase7()
    P.barrier()
    P.op("sp", lambda e: e.nop(), signal=False)
    for e in ("pe", "act", "dve", "pool"):
        P.bar[e] = []

    sem_names = ["pe", "act", "dve", "pool"] + ["d_sp_%d" % i for i in range(P.ring["sp"])] + ["d_pool_%d" % i for i in range(P.ring["pool"])]
    sem_cms = [nc.semaphore(n) for n in sem_names]
    sems = {n: cm.__enter__() for n, cm in zip(sem_names, sem_cms)}

    ref = {k: set() for k in ("pe", "act", "dve", "pool")}
    for name in ENGS:
        for ws, fn, sig in P.q[name]:
            for k, v in ws:
                if k in ref:
                    ref[k].add(v)
    remap = {k: {v: i + 1 for i, v in enumerate(sorted(vs))} for k, vs in ref.items()}

    def replay(name, e):
        c = 0
        for ws, fn, sig in P.q[name]:
            for k, v in ws:
                e.wait_ge(sems[k], remap[k][v] if k in remap else v)
            ins = fn(e)
            if sig is not None:
                if sig[0] in remap:
                    c += 1
                    if c in ref[sig[0]]:
                        ins.then_inc(sems[sig[0]], 1)
                else:
                    ins.then_inc(sems[sig[0]], sig[1])

    with nc.Block() as block:
        @block.tensor
        def _(e):
            replay("pe", e)

        @block.scalar
        def _(e):
            replay("act", e)

        @block.vector
        def _(e):
            replay("dve", e)

        @block.gpsimd
        def _(e):
            replay("pool", e)

        @block.sync
        def _(e):
            replay("sp", e)

    for cm in reversed(sem_cms):
        cm.__exit__(None, None, None)
    psum_cm.__exit__(None, None, None)
    arena_cm.__exit__(None, None, None)
    return nc


DEBUG_OUTS = ()


def _rope_tables():
    t = np.arange(S)
    row = (t // 64).astype(np.float32)
    col = (t % 64).astype(np.float32)
    freqs = (np.float32(10000.0) ** (-np.arange(32, dtype=np.float32) / np.float32(32))).astype(np.float32)
    ar = row[:, None] * freqs[None, :]
    ac = col[:, None] * freqs[None, :]
    cos = np.concatenate([np.cos(ar), np.cos(ar), np.cos(ac), np.cos(ac)], axis=1).astype(np.float32)
    sin = np.concatenate([-np.sin(ar), np.sin(ar), -np.sin(ac), np.sin(ac)], axis=1).astype(np.float32)
    return cos, sin


def _bias_tiles(rpb, core):
    out = np.full((3, 8, 8 * 128, 512), NEG, dtype=np.float32)
    for typ, qb in ((0, 0), (1, 1), (2, 3)):
        R = 32 * core + 8 * qb
        q = np.arange(512)
        r = R + q // 64
        c = q % 64
        r0 = np.clip(r - 4, 0, 256 - 8)
        c0 = np.clip(c - 8, 0, 64 - 16)
        k = np.arange(1024)
        kr = (R - 4) + k // 64
        kc = k % 64
        inr = (kr[:, None] >= r0[None, :]) & (kr[:, None] < r0[None, :] + 8)
        inc = (kc[:, None] >= c0[None, :]) & (kc[:, None] < c0[None, :] + 16)
        ok = inr & inc
        dr = np.clip(kr[:, None] - r[None, :] + 7, 0, 14)
        dc = np.clip(kc[:, None] - c[None, :] + 15, 0, 30)
        for h in range(8):
            g = rpb[h][dr, dc]
            out[typ, h] = np.where(ok, g, np.float32(NEG))
    return np.ascontiguousarray(out.reshape(3, 8, 8, 128, 512).transpose(0, 1, 3, 2, 4))


def _prepare_inputs(x, p, pre_mix_norm, w_in, q_norm, k_norm, rel_pos_bias, w_o, post_mix_norm, pre_mlp_norm,
                    w_up, w_down, post_mlp_norm, pre_ple_norm, w_ple_gate, w_ple_proj, post_ple_norm):
    f = lambda a: np.ascontiguousarray(np.asarray(a, dtype=np.float32))
    x2 = f(x)[0]
    p2 = f(p)[0, 0]
    gains = np.stack([np.broadcast_to(f(g)[0][None, :], (128, D)) for g in
                      (pre_mix_norm, post_mix_norm, pre_mlp_norm, post_mlp_norm, pre_ple_norm, post_ple_norm)])
    gains = np.ascontiguousarray(gains)
    gainsT = np.ascontiguousarray(np.stack([f(g)[0].reshape(16, 128).T for g in
                                            (pre_mix_norm, post_mix_norm, pre_mlp_norm, post_mlp_norm, pre_ple_norm, post_ple_norm)]))
    gq = np.ascontiguousarray(np.broadcast_to(np.tile(f(q_norm)[0], 4)[None, :], (128, 512)))
    gk = np.ascontiguousarray(np.broadcast_to(np.tile(f(k_norm)[0], 2)[None, :], (128, 256)))
    cos, sin = _rope_tables()
    rpb = f(rel_pos_bias)[0]
    ident = np.eye(128, dtype=np.float32)
    shared = {
        "x_all": x2, "w_in": f(w_in)[0], "w_o": f(w_o)[0], "w_up": f(w_up)[0], "w_down": f(w_down)[0],
        "w_gate": f(w_ple_gate)[0], "w_proj": f(w_ple_proj)[0], "gains": gains, "gainsT": gainsT, "gq": gq, "gk": gk,
        "cos_all": cos, "sin_all": sin, "ident": ident,
    }
    in_maps = []
    xpad = np.concatenate([np.zeros((256, D), np.float32), x2, np.zeros((256, D), np.float32)], axis=0)
    for c in range(NCORES):
        m = dict(shared)
        m["x_ext"] = np.ascontiguousarray(xpad[c * TOK:c * TOK + ETOK])
        m["pT"] = np.ascontiguousarray(p2[c * TOK:(c + 1) * TOK].T)
        m["cos_own"] = np.ascontiguousarray(cos[c * TOK:(c + 1) * TOK])
        m["sin_own"] = np.ascontiguousarray(sin[c * TOK:(c + 1) * TOK])
        m["biasT"] = _bias_tiles(rpb, c)
        in_maps.append(m)
    return in_maps


_NC_CACHE = {}


def kernel(**inputs):
    in_maps = _prepare_inputs(**inputs)
    if "nc" not in _NC_CACHE:
        _NC_CACHE["nc"] = build_program()
    nc = _NC_CACHE["nc"]
    res = run_bass_kernel_spmd(nc, in_maps, core_ids=list(range(NCORES)))
    out = np.concatenate([np.asarray(r["y"], dtype=np.float32) for r in res.results], axis=0)
    if DEBUG:
        kernel.last_results = res.results
    return out.reshape(1, S, D)
```

```python
import numpy as np
import concourse.bass as bass
import concourse.mybir as mybir
from concourse.bass_utils import run_bass_kernel_spmd

F32 = mybir.dt.float32
BF16 = mybir.dt.bfloat16
AF = mybir.ActivationFunctionType
ALU = mybir.AluOpType

NCORES = 8
S = 16384
D = 2048
TOK = 2048
NT = 16
NE = 20
ETOK = NE * 128
DFF = 8192
EPS = 1e-6
SCALE = 1.0 / np.sqrt(128.0)
NEG = -30000.0
DEBUG = False


class Tk:
    __slots__ = ("w", "r")

    def __init__(self):
        self.w = {}
        self.r = {}


class Tile:
    __slots__ = ("ap", "tk")

    def __init__(self, ap):
        self.ap = ap
        self.tk = Tk()


def _tk(x):
    return x.tk if isinstance(x, Tile) else x


ENGS = ("pe", "act", "dve", "pool", "sp")


class Prog:
    def __init__(self):
        self.q = {e: [] for e in ENGS}
        self.cnt = {e: 0 for e in ENGS}
        self.waited = {e: {} for e in ENGS}
        self.ring = {"sp": 28, "pool": 12}
        self.dma_i = {"sp": 0, "pool": 0}
        self.dma_last = {}
        self.bar = {e: [] for e in ENGS}

    def _collect(self, eng, reads, writes, extra, add):
        toks = list(extra) + self.bar[eng]
        self.bar[eng] = []
        for t in reads:
            toks.extend(_tk(t).w.items())
        for t in writes:
            t = _tk(t)
            if not add:
                toks.extend(t.w.items())
            toks.extend(t.r.items())
        out = []
        wd = self.waited[eng]
        for k, v in toks:
            if k == "pe" and eng == "pe":
                continue
            if wd.get(k, 0) >= v:
                continue
            wd[k] = v
            out.append((k, v))
        return out

    def _mark(self, tok, reads, writes, add):
        k, v = tok
        for t in reads:
            t = _tk(t)
            if t.r.get(k, 0) < v:
                t.r[k] = v
        for t in writes:
            t = _tk(t)
            if add:
                if t.w.get(k, 0) < v:
                    t.w[k] = v
            else:
                t.w = {k: v}
            t.r = {}

    def op(self, eng, fn, reads=(), writes=(), signal=True, extra=(), add=False):
        ws = self._collect(eng, reads, writes, extra, add)
        if signal:
            self.cnt[eng] += 1
            tok = (eng, self.cnt[eng])
            sig = (eng, 1)
        else:
            tok = (eng, self.cnt[eng] + 1)
            sig = None
        self.q[eng].append((ws, fn, sig))
        self._mark(tok, reads, writes, add)
        return tok

    def dma(self, queue, out_ap, in_ap, reads=(), writes=(), add=False, extra=()):
        i = self.dma_i[queue]
        self.dma_i[queue] += 1
        n = self.ring[queue]
        key = "d_%s_%d" % (queue, i % n)
        val = 16 * (i // n + 1)
        extra = list(extra) + ([(key, val - 16)] if i >= n else [])
        ws = self._collect(queue, reads, writes, extra, add)
        self.q[queue].append((ws, (lambda e, o=out_ap, a=in_ap: e.dma_start(out=o, in_=a)), (key, 16)))
        tok = (key, val)
        self.dma_last[key] = val
        self._mark(tok, reads, writes, add)
        return tok

    def barrier(self):
        toks = [(e, self.cnt[e]) for e in ("pe", "act", "dve", "pool") if self.cnt[e] > 0]
        toks += list(self.dma_last.items())
        for e in ENGS:
            self.bar[e] = self.bar[e] + toks


def MM(out, lhsT, rhs, start, stop):
    return lambda e: e.matmul(out, lhsT, rhs, start=start, stop=stop)


def TR(out, in_, ident):
    return lambda e: e.transpose(out, in_, ident)


def ACT(out, in_, func, **kw):
    return lambda e: e.activation(out=out, in_=in_, func=func, **kw)


def TT(out, in0, in1, op):
    return lambda e: e.tensor_tensor(out=out, in0=in0, in1=in1, op=op)


def TS(out, in0, s1, s2, op0, op1=None):
    if op1 is None:
        return lambda e: e.tensor_scalar(out=out, in0=in0, scalar1=s1, scalar2=None, op0=op0)
    return lambda e: e.tensor_scalar(out=out, in0=in0, scalar1=s1, scalar2=s2, op0=op0, op1=op1)


def STT(out, in0, scalar, in1, op0, op1):
    return lambda e: e.scalar_tensor_tensor(out=out, in0=in0, scalar=scalar, in1=in1, op0=op0, op1=op1)


def CP(out, in_):
    return lambda e: e.tensor_copy(out, in_)


def RCP(out, in_):
    return lambda e: e.reciprocal(out=out, in_=in_)


def MSET(ap, v):
    return lambda e: e.memset(ap, v)


def build_program():
    nc = bass.Bass("TRN2", target_bir_lowering=False)
    P = Prog()

    def din(name, shape, dt=F32):
        return nc.dram_tensor(name, list(shape), dt, kind="ExternalInput").ap()

    def dscr(name, shape, dt=BF16):
        kind = "ExternalOutput" if (DEBUG and name in DEBUG_OUTS) else "Internal"
        return nc.dram_tensor(name, list(shape), dt, kind=kind).ap()

    x_all = din("x_all", [S, D])
    x_ext = din("x_ext", [ETOK, D])
    pT_in = din("pT", [256, TOK])
    w_in = din("w_in", [D, 4608])
    w_o = din("w_o", [D, D])
    w_up = din("w_up", [D, DFF])
    w_dn = din("w_down", [DFF, D])
    w_g = din("w_gate", [D, D])
    w_p = din("w_proj", [256, D])
    gains = din("gains", [6, 128, D])
    gainsT = din("gainsT", [6, 128, 16])
    gq_in = din("gq", [128, 512])
    gk_in = din("gk", [128, 256])
    cos_all = din("cos_all", [S, 128])
    sin_all = din("sin_all", [S, 128])
    cos_own = din("cos_own", [TOK, 128])
    sin_own = din("sin_own", [TOK, 128])
    biasT = din("biasT", [3, 8, 128, 8, 512])
    ident_in = din("ident", [128, 128])
    y_out = nc.dram_tensor("y", [TOK, D], F32, kind="ExternalOutput").ap()

    kaT_s = dscr("kaT_s", [2, 128, S])
    va_s = dscr("va_s", [128, 2, 128, 128])
    qaT_s = dscr("qaT_s", [8, 128, TOK])
    qbT_s = dscr("qbT_s", [8, 128, TOK])
    kbT_s = dscr("kbT_s", [8, 128, ETOK])
    vb_s = dscr("vb_s", [128, 8, NE, 128])
    mixT_s = dscr("mixT_s", [16, 128, TOK])
    h1_s = dscr("h1_s", [TOK, D], F32)
    h1nT_s = dscr("h1nT_s", [4, 128, 16, 512])
    wup_s = dscr("wup_s", [D, DFF])
    wdn_s = dscr("wdn_s", [DFF, D])
    m_s = dscr("m_s", [TOK, D], F32)

    ARENA_W = 48 * 1024
    arena_cm = nc.sbuf_tensor("arena", [128, ARENA_W], F32)
    psum_cm = nc.psum_tensor("ps", [128, 8, 512], F32)
    arena = arena_cm.__enter__()
    ps = psum_cm.__enter__()
    psb = ps[:].bitcast(BF16)
    PSB = [Tk() for _ in range(8)]

    st = {"off": 0}

    def reset(off=0):
        st["off"] = off

    def sb(shape, dt):
        n = int(np.prod(shape))
        nb = n * (2 if dt == BF16 else 4)
        nb = (nb + 63) // 64 * 64
        off = st["off"]
        st["off"] = off + nb
        assert st["off"] <= ARENA_W * 4, "sbuf arena overflow %d" % st["off"]
        v = arena[:, off // 4:(off + nb) // 4]
        if dt == BF16:
            v = v.bitcast(BF16)
        v = v[:, 0:n]
        if len(shape) == 2:
            v = v.rearrange("p (a b) -> p a b", a=shape[0])
        elif len(shape) == 3:
            v = v.rearrange("p (a b c) -> p a b c", a=shape[0], b=shape[1])
        return Tile(v)

    ident = sb([128], BF16)
    ones = sb([128], F32)
    onesb = sb([128], BF16)
    P.dma("pool", ident.ap, ident_in[:, :], writes=[ident])
    P.op("pool", MSET(ones.ap, 1.0), writes=[ones])
    P.op("pool", MSET(onesb.ap, 1.0), writes=[onesb])
    BASE = st["off"]

    WUP_TK = [Tk() for _ in range(16)]
    WDN_TK = [Tk() for _ in range(64)]

    cast_list = [(wup_s[c * 128:(c + 1) * 128, :], w_up[c * 128:(c + 1) * 128, :], WUP_TK[c]) for c in range(16)] + \
                [(wdn_s[c * 128:(c + 1) * 128, :], w_dn[c * 128:(c + 1) * 128, :], WDN_TK[c]) for c in range(64)]

    def issue_weight_casts(n, extra=()):
        for _ in range(n):
            if cast_list:
                o, i_, tk = cast_list.pop(0)
                P.dma("pool", o, i_, writes=[tk], extra=extra)

    def rstd_ops(ss, r, n):
        P.op("dve", TS(r.ap, ss.ap, 1.0 / n, EPS, ALU.mult, ALU.add), reads=[ss], writes=[r])
        P.op("act", ACT(r.ap, r.ap, AF.Sqrt), reads=[r], writes=[r])
        P.op("dve", RCP(r.ap, r.ap), reads=[r], writes=[r])

    def norm_to_bf16(src, gain, junk, ss, r, xg, mode="stt"):
        P.op("act", ACT(junk.ap, src.ap, AF.Square, accum_out=ss.ap[:, 0:1]), reads=[src], writes=[ss])
        rstd_ops(ss, r, D)
        if mode == "fold":
            P.op("dve", TS(xg.ap, src.ap, r.ap[:, 0:1], None, ALU.mult), reads=[src, r], writes=[xg])
        else:
            P.op("dve", STT(xg.ap, src.ap, r.ap[:, 0:1], gain.ap, ALU.mult, ALU.mult), reads=[src, r, gain], writes=[xg])

    def fold_gain(wt, gT, nchunks=16):
        for c in range(nchunks):
            P.op("dve", TS(wt.ap[:, c, :], wt.ap[:, c, :], gT.ap[:, c:c + 1], None, ALU.mult), reads=[wt, gT], writes=[wt])

    def transpose16(xg, dst_fn, dst_tiles, banks=(6, 7)):
        for half in range(2):
            bk = banks[half]
            for j in range(8):
                c = half * 8 + j
                P.op("pe", TR(psb[:, bk, j * 128:(j + 1) * 128], xg.ap[:, c * 128:(c + 1) * 128], ident.ap),
                     reads=[xg, ident], writes=[PSB[bk]], signal=(j == 7))
            eng = "act" if half == 0 else "dve"
            src = psb[:, bk, :].rearrange("p (c t) -> p c t", c=8)
            fn = ACT(dst_fn(half * 8, half * 8 + 8), src, AF.Copy) if eng == "act" else CP(dst_fn(half * 8, half * 8 + 8), src)
            P.op(eng, fn, reads=[PSB[bk]], writes=dst_tiles)

    def norm_rope(src_ap, src_tk, nh, gain, cs, sn, kg, t1, t2, hss, hr, junk, outb, rs=None):
        W = nh * 128
        rr = [rs] if rs is not None else []
        for h in range(nh):
            kw = {"scale": rs.ap[:, 0:1]} if rs is not None else {}
            P.op("act", ACT(junk.ap[:, 0:128], src_ap[:, h * 128:(h + 1) * 128], AF.Square, accum_out=hss.ap[:, h:h + 1], **kw),
                 reads=[src_tk] + rr, writes=[hss], add=True)
        rstd_ops(hss, hr, 128)
        if rs is not None:
            P.op("dve", STT(kg.ap[:, 0:W], src_ap, rs.ap[:, 0:1], gain.ap[:, 0:W], ALU.mult, ALU.mult), reads=[src_tk, gain, rs], writes=[kg])
        else:
            P.op("dve", TT(kg.ap[:, 0:W], src_ap, gain.ap[:, 0:W], ALU.mult), reads=[src_tk, gain], writes=[kg])
        kg3 = kg.ap[:, 0:W].rearrange("p (h d) -> p h d", h=nh)
        t13 = t1.ap[:, 0:W].rearrange("p (h d) -> p h d", h=nh)
        P.op("dve", TT(t13, kg3, cs.ap.unsqueeze(1).broadcast_to([128, nh, 128]), ALU.mult), reads=[kg, cs], writes=[t1])
        kg5 = kg.ap[:, 0:W].rearrange("p (h s f i) -> p h s f i", h=nh, s=2, f=2)
        t25 = t2.ap[:, 0:W].rearrange("p (h s f i) -> p h s f i", h=nh, s=2, f=2)
        sn4 = sn.ap.rearrange("p (s f i) -> p s f i", s=2, f=2)
        for f in range(2):
            P.op("dve", TT(t25[:, :, :, f, :], kg5[:, :, :, 1 - f, :],
                           sn4[:, :, f, :].unsqueeze(1).broadcast_to([128, nh, 2, 32]), ALU.mult),
                 reads=[kg, sn], writes=[t2], add=(f == 1))
        P.op("dve", TT(t1.ap[:, 0:W], t1.ap[:, 0:W], t2.ap[:, 0:W], ALU.add), reads=[t1, t2], writes=[t1])
        for h in range(nh):
            P.op("act", ACT(outb.ap[:, h, :], t1.ap[:, h * 128:(h + 1) * 128], AF.Copy, scale=hr.ap[:, h:h + 1]),
                 reads=[t1, hr], writes=[outb], add=(h > 0))

    def phase1():
        reset(BASE)
        wkv = sb([16, 512], BF16)
        gT = sb([16], F32)
        gk = sb([256], F32)
        xt = [sb([D], F32) for _ in range(3)]
        xg = [sb([D], BF16) for _ in range(3)]
        xnT = [sb([16, 128], BF16) for _ in range(2)]
        junk = sb([D], BF16)
        cs = [sb([128], F32) for _ in range(3)]
        sn = [sb([128], F32) for _ in range(3)]
        ss = [sb([1], F32) for _ in range(3)]
        r = [sb([1], F32) for _ in range(3)]
        hss = [sb([2], F32) for _ in range(2)]
        hr = [sb([2], F32) for _ in range(2)]
        kg = sb([256], F32)
        t1 = sb([256], F32)
        t2 = sb([256], F32)
        kb = [sb([2, 128], BF16) for _ in range(2)]
        kst = [sb([2, 512], BF16) for _ in range(2)]
        vst = [sb([2, 4, 128], BF16) for _ in range(2)]
        KA_TK = [Tk() for _ in range(32)]
        VA_TK = [Tk() for _ in range(32)]

        P.dma("pool", wkv.ap, w_in[:, 1024:1536].rearrange("(c p) n -> p c n", p=128), writes=[wkv])
        P.dma("sp", gT.ap, gainsT[0], writes=[gT])
        P.dma("sp", gk.ap, gk_in[:, :], writes=[gk])
        fold_gain(wkv, gT)

        def front_a(i):
            b3 = i % 3
            P.dma("sp", xt[b3].ap, x_all[i * 128:(i + 1) * 128, :], writes=[xt[b3]])
            P.dma("sp", cs[b3].ap, cos_all[i * 128:(i + 1) * 128, :], writes=[cs[b3]])
            P.dma("sp", sn[b3].ap, sin_all[i * 128:(i + 1) * 128, :], writes=[sn[b3]])
            P.op("dve", CP(xg[b3].ap, xt[b3].ap), reads=[xt[b3]], writes=[xg[b3]])
            P.op("act", ACT(junk.ap, xt[b3].ap, AF.Square, accum_out=ss[b3].ap[:, 0:1]), reads=[xt[b3]], writes=[ss[b3]])
            rstd_ops(ss[b3], r[b3], D)

        def front_b(i):
            b = i % 2
            b3 = i % 3
            transpose16(xg[b3], lambda c0, c1, b=b: xnT[b].ap[:, c0:c1, :], [xnT[b]], banks=(6, 7))
            bank = i % 2
            for c in range(16):
                P.op("pe", MM(ps[:, bank, :], xnT[b].ap[:, c, :], wkv.ap[:, c, :], c == 0, c == 15),
                     reads=[xnT[b], wkv], writes=[PSB[bank]], signal=(c == 15))

        def back(i):
            b = i % 2
            b3 = i % 3
            g4 = i // 4
            gb = g4 % 2
            bank = i % 2
            P.op("act", ACT(vst[gb].ap[:, :, i % 4, :], ps[:, bank, 256:512].rearrange("p (k d) -> p k d", k=2), AF.Copy, scale=r[b3].ap[:, 0:1]),
                 reads=[PSB[bank], r[b3]], writes=[vst[gb]], add=(i % 4 != 0))
            norm_rope(ps[:, bank, 0:256], PSB[bank], 2, gk, cs[b3], sn[b3], kg, t1, t2, hss[b], hr[b], junk, kb[b], rs=r[b3])
            tb = 4 + (i % 2)
            for h in range(2):
                P.op("pe", TR(psb[:, tb, h * 128:(h + 1) * 128], kb[b].ap[:, h, :], ident.ap),
                     reads=[kb[b], ident], writes=[PSB[tb]], signal=(h == 1))
            P.op("dve", CP(kst[gb].ap[:, :, (i % 4) * 128:(i % 4 + 1) * 128], psb[:, tb, 0:256].rearrange("p (k t) -> p k t", k=2)),
                 reads=[PSB[tb]], writes=[kst[gb]], add=(i % 4 != 0))
            if i % 4 == 3:
                P.dma("sp", kaT_s[:, :, g4 * 512:(g4 + 1) * 512].rearrange("k d t -> d k t"), kst[gb].ap,
                      reads=[kst[gb]], writes=[KA_TK[g4]])
                P.dma("sp", va_s[:, :, g4 * 4:(g4 + 1) * 4, :], vst[gb].ap, reads=[vst[gb]], writes=[VA_TK[g4]])

        front_a(0)
        front_a(1)
        front_b(0)
        for i in range(128):
            if i + 2 < 128:
                front_a(i + 2)
            if i + 1 < 128:
                front_b(i + 1)
            back(i)
        return KA_TK, VA_TK

    def phase2():
        reset(BASE)
        xnT = sb([16, ETOK], BF16)
        XN_TK = [Tk() for _ in range(NE)]
        gT = sb([16], F32)
        gq = sb([512], F32)
        mark = st["off"]
        xt = [sb([D], F32) for _ in range(2)]
        xg = [sb([D], BF16) for _ in range(2)]
        junk = sb([D], BF16)
        ss = [sb([1], F32) for _ in range(2)]
        r = [sb([1], F32) for _ in range(2)]

        P.dma("sp", gT.ap, gainsT[0], writes=[gT])
        P.dma("sp", gq.ap, gq_in[:, :], writes=[gq])
        for e in range(NE):
            b = e % 2
            P.dma("sp", xt[b].ap, x_ext[e * 128:(e + 1) * 128, :], writes=[xt[b]])
            norm_to_bf16(xt[b], None, junk, ss[b], r[b], xg[b], mode="fold")
            transpose16(xg[b], lambda c0, c1, e=e: xnT.ap[:, c0:c1, e * 128:(e + 1) * 128], [XN_TK[e]],
                        banks=(6, 7) if e % 2 == 0 else (4, 5))
        P.barrier()
        reset(mark)
        wblk = [sb([16, 512], BF16) for _ in range(2)]
        stage = [sb([4 * ETOK], BF16) for _ in range(2)]
        junk = sb([D], BF16)
        cs = [sb([128], F32) for _ in range(2)]
        sn = [sb([128], F32) for _ in range(2)]
        hss = [sb([4], F32) for _ in range(2)]
        hr = [sb([4], F32) for _ in range(2)]
        kg = sb([512], F32)
        t1 = sb([512], F32)
        t2 = sb([512], F32)
        qb_ = [sb([4, 128], BF16) for _ in range(2)]

        blocks = [("qa", 0, 0), ("qa", 1, 512), ("qb", 0, 1536), ("qb", 1, 2048),
                  ("kb", 0, 2560), ("kb", 1, 3072), ("vb", 0, 3584), ("vb", 1, 4096)]
        bankrot = [0]

        def nbank():
            bk = bankrot[0] % 4
            bankrot[0] += 1
            return bk

        OUT_TK = {"qa": [Tk(), Tk()], "qb": [Tk(), Tk()], "kb": [Tk(), Tk()], "vb": [Tk(), Tk()]}
        pend = [None]

        def load_block(bi_):
            wb_ = wblk[bi_ % 2]
            col_ = blocks[bi_][2]
            P.dma("pool", wb_.ap, w_in[:, col_:col_ + 512].rearrange("(c p) n -> p c n", p=128), writes=[wb_])
            pend[0] = wb_

        def maybe_fold():
            if pend[0] is not None:
                fold_gain(pend[0], gT)
                pend[0] = None

        load_block(0)
        maybe_fold()
        for bi, (kind, half, col) in enumerate(blocks):
            wb = wblk[bi % 2]
            sg = stage[bi % 2]
            if bi + 1 < len(blocks):
                load_block(bi + 1)
            if kind == "qa":
                sgv = sg.ap[:, 0:4 * TOK].rearrange("p (h t) -> p h t", h=4)
                for t in range(NT):
                    if t == 5:
                        maybe_fold()
                    e = t + 2
                    b = t % 2
                    P.dma("sp", cs[b].ap, cos_own[t * 128:(t + 1) * 128, :], writes=[cs[b]])
                    P.dma("sp", sn[b].ap, sin_own[t * 128:(t + 1) * 128, :], writes=[sn[b]])
                    bk = nbank()
                    for c in range(16):
                        P.op("pe", MM(ps[:, bk, :], xnT.ap[:, c, e * 128:(e + 1) * 128], wb.ap[:, c, :], c == 0, c == 15),
                             reads=[XN_TK[e], wb], writes=[PSB[bk]], signal=(c == 15))
                    norm_rope(ps[:, bk, :], PSB[bk], 4, gq, cs[b], sn[b], kg, t1, t2, hss[b], hr[b], junk, qb_[b])
                    tb = 4 + (t % 2)
                    for h in range(4):
                        P.op("pe", TR(psb[:, tb, h * 128:(h + 1) * 128], qb_[b].ap[:, h, :], ident.ap),
                             reads=[qb_[b], ident], writes=[PSB[tb]], signal=(h == 3))
                    P.op("dve", CP(sgv[:, :, t * 128:(t + 1) * 128], psb[:, tb, 0:512].rearrange("p (h t) -> p h t", h=4)),
                         reads=[PSB[tb]], writes=[sg], add=(t > 0))
                P.dma("sp", qaT_s[half * 4:(half + 1) * 4].rearrange("h d t -> d h t"), sgv, reads=[sg], writes=[OUT_TK[kind][half]])
            elif kind in ("qb", "kb"):
                ntok = TOK if kind == "qb" else ETOK
                e0 = 2 if kind == "qb" else 0
                sgv = sg.ap[:, 0:4 * ntok].rearrange("p (h t) -> p h t", h=4)
                first = True
                for j in range(4):
                    if j == 1:
                        maybe_fold()
                    for sbk in range(ntok // 512):
                        tk0 = e0 * 128 + sbk * 512
                        bk = nbank()
                        for c in range(16):
                            P.op("pe", MM(ps[:, bk, :], wb.ap[:, c, j * 128:(j + 1) * 128], xnT.ap[:, c, tk0:tk0 + 512], c == 0, c == 15),
                                 reads=[wb] + XN_TK[tk0 // 128:tk0 // 128 + 4], writes=[PSB[bk]], signal=(c == 15))
                        eng = "act" if (sbk % 2 == 0) else "dve"
                        dst = sgv[:, j, sbk * 512:(sbk + 1) * 512]
                        fn = ACT(dst, ps[:, bk, :], AF.Copy) if eng == "act" else CP(dst, ps[:, bk, :])
                        P.op(eng, fn, reads=[PSB[bk]], writes=[sg], add=(not first))
                        first = False
                dst_s = qbT_s if kind == "qb" else kbT_s
                P.dma("sp", dst_s[half * 4:(half + 1) * 4].rearrange("h d t -> d h t"), sgv, reads=[sg], writes=[OUT_TK[kind][half]])
            else:
                sgv = sg.ap[:, 0:NE * 512].rearrange("p (h e d) -> p h e d", h=4, e=NE)
                for e in range(NE):
                    if e == 6:
                        maybe_fold()
                    bk = nbank()
                    for c in range(16):
                        P.op("pe", MM(ps[:, bk, :], xnT.ap[:, c, e * 128:(e + 1) * 128], wb.ap[:, c, :], c == 0, c == 15),
                             reads=[XN_TK[e], wb], writes=[PSB[bk]], signal=(c == 15))
                    eng = "act" if (e % 2 == 0) else "dve"
                    dst = sgv[:, :, e, :]
                    src = ps[:, bk, :].rearrange("p (h d) -> p h d", h=4)
                    fn = ACT(dst, src, AF.Copy) if eng == "act" else CP(dst, src)
                    P.op(eng, fn, reads=[PSB[bk]], writes=[sg], add=(e > 0))
                P.dma("sp", vb_s[:, half * 4:(half + 1) * 4, :, :], sgv, reads=[sg], writes=[OUT_TK[kind][half]])
        return OUT_TK

    def attn_iter(it, kT_fn, v_fn, k_reads, q_ap, q_reads, ngroups, bufs, out_dst, out_tk, bias=None, pek=4):
        pT, accD, accP, sbias, rl, ost = bufs["pT"], bufs["accD"], bufs["accP"], bufs["sbias"], bufs["rl"], bufs["ost"]
        ob = 4 + (it % 2)
        ostb = ost[it % 2]

        def qk(j):
            s0 = (j % 2) * 2
            for u in range(2):
                P.op("pe", MM(ps[:, s0 + u, :], kT_fn(2 * j + u), q_ap, True, True),
                     reads=k_reads + q_reads, writes=[PSB[s0 + u]], signal=(u == 1))

        def softmax_pv(j):
            s0 = (j % 2) * 2
            pt = pT[j % 3]
            src = ps[:, s0:s0 + 2, :]
            if bias is not None:
                sbt = sbias[j % 2]
                P.op("dve", STT(sbt.ap, src, float(SCALE), bias[0][:, 2 * j:2 * j + 2, :], ALU.mult, ALU.add),
                     reads=[PSB[s0], PSB[s0 + 1], bias[1]], writes=[sbt])
                P.op("act", ACT(pt.ap, sbt.ap, AF.Exp), reads=[sbt], writes=[pt])
            else:
                P.op("act", ACT(pt.ap, src, AF.Exp, scale=float(SCALE)), reads=[PSB[s0], PSB[s0 + 1]], writes=[pt])
            for u in range(2):
                P.op("pe", MM(ps[:, ob, :], v_fn(2 * j + u), pt.ap[:, u, :], (j == 0 and u == 0), (j == ngroups - 1 and u == 1)),
                     reads=k_reads + [pt], writes=[PSB[ob]], signal=(u == 1))
            if j in pe_groups:
                for u in range(2):
                    P.op("pe", MM(ps[:, 6, :], onesb.ap, pt.ap[:, u, :], (j == pe_groups[0] and u == 0),
                                  (not dve_groups) and j == pe_groups[-1] and u == 1),
                         reads=[pt, onesb], writes=[PSB[6]], signal=(u == 1))
            elif j == dve_groups[0]:
                P.op("dve", CP(accD.ap, pt.ap), reads=[pt], writes=[accD])
            else:
                P.op("dve", TT(accD.ap, accD.ap, pt.ap, ALU.add), reads=[pt, accD], writes=[accD])

        pe_groups = [j for j in range(ngroups) if j % pek == pek - 1]
        dve_groups = [j for j in range(ngroups) if j % pek != pek - 1]
        qk(0)
        for j in range(ngroups):
            if j + 1 < ngroups:
                qk(j + 1)
            softmax_pv(j)
        if dve_groups:
            P.op("dve", TT(accD.ap[:, 0, :], accD.ap[:, 0, :], accD.ap[:, 1, :], ALU.add), reads=[accD], writes=[accD])
            P.op("pe", MM(ps[:, 6, :], ones.ap, accD.ap[:, 0, :], not pe_groups, True), reads=[accD, ones], writes=[PSB[6]])
        P.op("dve", RCP(rl.ap, ps[:, 6, :]), reads=[PSB[6]], writes=[rl])
        P.op("dve", TT(ostb.ap, ps[:, ob, :], rl.ap, ALU.mult), reads=[PSB[ob], rl], writes=[ostb])
        P.dma("sp", out_dst, ostb.ap, reads=[ostb], writes=[out_tk], add=True)

    def attn_bufs():
        return {
            "pT": [sb([2, 512], BF16) for _ in range(3)],
            "accD": sb([2, 512], F32), "accP": sb([2, 512], F32),
            "sbias": [sb([2, 512], F32) for _ in range(2)],
            "rl": sb([512], F32),
            "ost": [sb([512], BF16) for _ in range(2)],
        }

    MIX_TK = [Tk() for _ in range(16)]

    def phase3(KA_TK, VA_TK, QA_TK):
        reset(BASE)
        kT = sb([S], BF16)
        vv = sb([128, 128], BF16)
        qT = sb([4, TOK], BF16)
        bufs = attn_bufs()
        it = 0
        for kv in range(2):
            for c4 in range(4):
                P.dma("sp", kT.ap[:, c4 * 4096:(c4 + 1) * 4096], kaT_s[kv, :, c4 * 4096:(c4 + 1) * 4096],
                      reads=KA_TK[c4 * 8:(c4 + 1) * 8], writes=[kT], add=(c4 > 0))
                P.dma("sp", vv.ap[:, c4 * 32:(c4 + 1) * 32, :], va_s[:, kv, c4 * 32:(c4 + 1) * 32, :],
                      reads=VA_TK[c4 * 8:(c4 + 1) * 8], writes=[vv], add=(c4 > 0))
            P.dma("sp", qT.ap, qaT_s[kv * 4:(kv + 1) * 4].rearrange("h d t -> d h t"), reads=[QA_TK[kv]], writes=[qT])
            for g in range(4):
                h = kv * 4 + g
                for qb in range(4):
                    issue_weight_casts(3, extra=[("act", P.cnt["act"])] if P.cnt["act"] else ())
                    attn_iter(it, lambda kt: kT.ap[:, kt * 128:(kt + 1) * 128], lambda kt: vv.ap[:, kt, :], [kT, vv],
                              qT.ap[:, g, qb * 512:(qb + 1) * 512], [qT], 64, bufs,
                              mixT_s[h, :, qb * 512:(qb + 1) * 512], MIX_TK[h])
                    it += 1

    def phase4(OUT_TK):
        reset(BASE)
        issue_weight_casts(100)
        kT = [sb([ETOK], BF16) for _ in range(2)]
        vv = [sb([NE, 128], BF16) for _ in range(2)]
        qT = [sb([TOK], BF16) for _ in range(2)]
        bt = [sb([8, 512], F32) for _ in range(2)]
        bufs = attn_bufs()
        it = 0

        def load_bias(it_):
            h_, qb_ = it_ // 4, it_ % 4
            typ = 0 if qb_ == 0 else (2 if qb_ == 3 else 1)
            P.dma("sp", bt[it_ % 2].ap, biasT[typ, h_], writes=[bt[it_ % 2]])

        def load_head(h_):
            hb_ = h_ % 2
            P.dma("sp", kT[hb_].ap, kbT_s[h_], reads=[OUT_TK["kb"][h_ // 4]], writes=[kT[hb_]])
            P.dma("sp", vv[hb_].ap, vb_s[:, h_, :, :], reads=[OUT_TK["vb"][h_ // 4]], writes=[vv[hb_]])
            P.dma("sp", qT[hb_].ap, qbT_s[h_], reads=[OUT_TK["qb"][h_ // 4]], writes=[qT[hb_]])

        load_head(0)
        load_bias(0)
        for h in range(8):
            hb = h % 2
            for qb in range(4):
                if it + 1 < 32:
                    if qb == 3:
                        load_head(h + 1)
                    load_bias(it + 1)
                btt = bt[it % 2]
                attn_iter(it, lambda kt, hb=hb, qb=qb: kT[hb].ap[:, (4 * qb + kt) * 128:(4 * qb + kt + 1) * 128],
                          lambda kt, hb=hb, qb=qb: vv[hb].ap[:, 4 * qb + kt, :], [kT[hb], vv[hb]],
                          qT[hb].ap[:, qb * 512:(qb + 1) * 512], [qT[hb]], 4, bufs,
                          mixT_s[8 + h, :, qb * 512:(qb + 1) * 512], MIX_TK[8 + h], bias=(btt.ap, btt), pek=1)
                it += 1

    H1_TK = [Tk() for _ in range(NT)]
    H1N_TK = [Tk() for _ in range(4)]

    def phase5():
        reset(BASE)
        wo = sb([16, D], BF16)
        mixT = [sb([16, 512], BF16) for _ in range(1)]
        gpo = sb([D], F32)
        gpl = sb([D], F32)
        xt = [sb([D], F32) for _ in range(2)]
        tmp = sb([D], F32)
        h1 = [sb([D], F32) for _ in range(2)]
        xg = sb([D], BF16)
        junk = sb([D], BF16)
        hst = [sb([16, 512], BF16) for _ in range(2)]
        ss = [sb([1], F32) for _ in range(2)]
        r = [sb([1], F32) for _ in range(2)]
        ss1 = [sb([1], F32) for _ in range(2)]
        r1 = [sb([1], F32) for _ in range(2)]
        for n in range(4):
            P.dma("pool", wo.ap[:, :, n * 512:(n + 1) * 512], w_o[:, n * 512:(n + 1) * 512].rearrange("(c p) n -> p c n", p=128),
                  writes=[wo], add=(n > 0))
        P.dma("sp", gpo.ap, gains[1], writes=[gpo])
        P.dma("sp", gpl.ap, gains[2], writes=[gpl])

        def A(t):
            sbk, tt = t // 4, t % 4
            mx = mixT[0]
            b = t % 2
            g0 = 0 if t % 2 == 0 else 4
            if tt == 0:
                P.dma("sp", mx.ap, mixT_s[:, :, sbk * 512:(sbk + 1) * 512].rearrange("c d t -> d c t"), reads=MIX_TK, writes=[mx])
            P.dma("sp", xt[b].ap, x_ext[(t + 2) * 128:(t + 3) * 128, :], writes=[xt[b]])
            for n in range(4):
                for c in range(16):
                    P.op("pe", MM(ps[:, g0 + n, :], mx.ap[:, c, tt * 128:(tt + 1) * 128], wo.ap[:, c, n * 512:(n + 1) * 512], c == 0, c == 15),
                         reads=[mx, wo], writes=[PSB[g0 + n]], signal=(c == 15))

        def B(t):
            sbk, tt = t // 4, t % 4
            b = t % 2
            g0 = 0 if t % 2 == 0 else 4
            pall = ps[:, g0:g0 + 4, :]
            P.op("act", ACT(junk.ap.rearrange("p (a b) -> p a b", a=4), pall, AF.Square, accum_out=ss[b].ap[:, 0:1]),
                 reads=PSB[g0:g0 + 4], writes=[ss[b]])
            rstd_ops(ss[b], r[b], D)
            P.op("dve", STT(tmp.ap.rearrange("p (a b) -> p a b", a=4), pall, r[b].ap[:, 0:1], gpo.ap.rearrange("p (a b) -> p a b", a=4), ALU.mult, ALU.mult),
                 reads=PSB[g0:g0 + 4] + [r[b], gpo], writes=[tmp])
            P.op("dve", TT(h1[b].ap, tmp.ap, xt[b].ap, ALU.add), reads=[tmp, xt[b]], writes=[h1[b]])
            P.dma("sp", h1_s[t * 128:(t + 1) * 128, :], h1[b].ap, reads=[h1[b]], writes=[H1_TK[t]])
            norm_to_bf16(h1[b], gpl, junk, ss1[b], r1[b], xg)
            hs = hst[sbk % 2]
            transpose16(xg, lambda c0, c1, hs=hs, tt=tt: hs.ap[:, c0:c1, tt * 128:(tt + 1) * 128], [hs], banks=(g0, g0 + 1))
            if tt == 3:
                P.dma("sp", h1nT_s[sbk], hs.ap, reads=[hs], writes=[H1N_TK[sbk]])

        A(0)
        for t in range(NT):
            if t + 1 < NT:
                A(t + 1)
            B(t)

    M_TK = [Tk() for _ in range(4)]

    def phase6():
        reset(BASE)
        aT = sb([64, 512], BF16)
        AT_TK = [Tk() for _ in range(64)]
        hnT = [sb([16, 512], BF16) for _ in range(2)]
        wu = [sb([16, 512], BF16) for _ in range(2)]
        wd = [sb([4, 1024], BF16) for _ in range(3)]
        rt = [sb([512], F32) for _ in range(2)]
        mst = [sb([1024], F32) for _ in range(2)]
        k_up = 0
        k_dn = 0
        k_ms = 0
        for sbk in range(4):
            hn = hnT[sbk % 2]
            P.dma("sp", hn.ap, h1nT_s[sbk], reads=[H1N_TK[sbk]], writes=[hn])
            for fg in range(16):
                w = wu[k_up % 2]
                k_up += 1
                P.dma("sp", w.ap, wup_s[:, fg * 512:(fg + 1) * 512].rearrange("(c p) n -> p c n", p=128), reads=WUP_TK, writes=[w])
                for j in range(4):
                    fc = fg * 4 + j
                    bk = fc % 4
                    for c in range(16):
                        P.op("pe", MM(ps[:, bk, :], w.ap[:, c, j * 128:(j + 1) * 128], hn.ap[:, c, :], c == 0, c == 15),
                             reads=[w, hn], writes=[PSB[bk]], signal=(c == 15))
                    rtt = rt[fc % 2]
                    P.op("act", ACT(rtt.ap, ps[:, bk, :], AF.Relu), reads=[PSB[bk]], writes=[rtt])
                    P.op("dve", TT(aT.ap[:, fc, :], rtt.ap, rtt.ap, ALU.mult), reads=[rtt], writes=[AT_TK[fc]])
            for hf in range(2):
                for fgd in range(16):
                    w = wd[k_dn % 3]
                    k_dn += 1
                    P.dma("sp", w.ap, wdn_s[fgd * 512:(fgd + 1) * 512, hf * 1024:(hf + 1) * 1024].rearrange("(j p) n -> p j n", p=128),
                          reads=WDN_TK[fgd * 4:(fgd + 1) * 4], writes=[w])
                    for j in range(4):
                        fc = fgd * 4 + j
                        for tt in range(4):
                            for n in range(2):
                                bk = tt * 2 + n
                                last = (fc == 63)
                                P.op("pe", MM(ps[:, bk, :], aT.ap[:, fc, tt * 128:(tt + 1) * 128], w.ap[:, j, n * 512:(n + 1) * 512], fc == 0, last),
                                     reads=[AT_TK[fc], w], writes=[PSB[bk]], signal=(last or (j == 3 and tt == 3 and n == 1)))
                for tt in range(4):
                    ms = mst[k_ms % 2]
                    k_ms += 1
                    src = ps[:, tt * 2:tt * 2 + 2, :]
                    dst = ms.ap.rearrange("p (a b) -> p a b", a=2)
                    fn = ACT(dst, src, AF.Copy) if tt % 2 == 0 else CP(dst, src)
                    P.op("act" if tt % 2 == 0 else "dve", fn, reads=[PSB[tt * 2], PSB[tt * 2 + 1]], writes=[ms])
                    t = sbk * 4 + tt
                    P.dma("sp", m_s[t * 128:(t + 1) * 128, hf * 1024:(hf + 1) * 1024], ms.ap, reads=[ms], writes=[M_TK[sbk]], add=True)

    def phase7():
        reset(BASE)
        wg = sb([16, D], BF16)
        wp = sb([2, D], BF16)
        g3 = sb([D], F32)
        g4 = sb([D], F32)
        g5 = sb([D], F32)
        mt = [sb([D], F32) for _ in range(2)]
        h1 = [sb([D], F32) for _ in range(2)]
        pt = [sb([2, 128], BF16) for _ in range(2)]
        tmp = sb([D], F32)
        xg = sb([D], BF16)
        junk = sb([D], BF16)
        hnT = [sb([16, 128], BF16) for _ in range(2)]
        sg = [sb([1024], F32) for _ in range(2)]
        et = sb([D], F32)
        yt = [sb([D], F32) for _ in range(2)]
        ssA = [sb([1], F32) for _ in range(2)]
        rA = [sb([1], F32) for _ in range(2)]
        ssB = [sb([1], F32) for _ in range(2)]
        rB = [sb([1], F32) for _ in range(2)]
        ssC = [sb([1], F32) for _ in range(2)]
        rC = [sb([1], F32) for _ in range(2)]
        for n in range(4):
            P.dma("pool", wg.ap[:, :, n * 512:(n + 1) * 512], w_g[:, n * 512:(n + 1) * 512].rearrange("(c p) n -> p c n", p=128),
                  writes=[wg], add=(n > 0))
        P.dma("pool", wp.ap, w_p.rearrange("(c p) n -> p c n", p=128), writes=[wp])
        P.dma("sp", g3.ap, gains[3], writes=[g3])
        P.dma("sp", g4.ap, gains[4], writes=[g4])
        P.dma("sp", g5.ap, gains[5], writes=[g5])
        Y_TK = []

        def B1(t):
            b = t % 2
            P.dma("sp", mt[b].ap, m_s[t * 128:(t + 1) * 128, :], reads=[M_TK[t // 4]], writes=[mt[b]])
            P.dma("sp", h1[b].ap, h1_s[t * 128:(t + 1) * 128, :], reads=[H1_TK[t]], writes=[h1[b]])
            P.dma("pool", pt[b].ap, pT_in[:, t * 128:(t + 1) * 128].rearrange("(c p) t -> p c t", p=128), writes=[pt[b]])
            P.op("act", ACT(junk.ap, mt[b].ap, AF.Square, accum_out=ssA[b].ap[:, 0:1]), reads=[mt[b]], writes=[ssA[b]])
            rstd_ops(ssA[b], rA[b], D)
            P.op("dve", STT(tmp.ap, mt[b].ap, rA[b].ap[:, 0:1], g3.ap, ALU.mult, ALU.mult), reads=[mt[b], rA[b], g3], writes=[tmp])
            P.op("dve", TT(h1[b].ap, tmp.ap, h1[b].ap, ALU.add), reads=[tmp, h1[b]], writes=[h1[b]])
            norm_to_bf16(h1[b], g4, junk, ssB[b], rB[b], xg)

        def C(t):
            b = t % 2
            transpose16(xg, lambda c0, c1, b=b: hnT[b].ap[:, c0:c1, :], [hnT[b]], banks=(6, 7))

        def A(t):
            b = t % 2
            for hf in range(2):
                gb = hf * 2
                for n in range(2):
                    col = hf * 1024 + n * 512
                    for c in range(16):
                        P.op("pe", MM(ps[:, gb + n, :], hnT[b].ap[:, c, :], wg.ap[:, c, col:col + 512], c == 0, c == 15),
                             reads=[hnT[b], wg], writes=[PSB[gb + n]], signal=(c == 15))
                for n in range(2):
                    col = hf * 1024 + n * 512
                    for c in range(2):
                        P.op("pe", MM(ps[:, 4 + n, :], pt[b].ap[:, c, :], wp.ap[:, c, col:col + 512], c == 0, c == 1),
                             reads=[pt[b], wp], writes=[PSB[4 + n]], signal=(c == 1))
                sgt = sg[hf]
                P.op("act", ACT(sgt.ap.rearrange("p (a b) -> p a b", a=2), ps[:, gb:gb + 2, :], AF.Sigmoid),
                     reads=[PSB[gb], PSB[gb + 1]], writes=[sgt])
                P.op("dve", TT(et.ap[:, hf * 1024:(hf + 1) * 1024].rearrange("p (a b) -> p a b", a=2), ps[:, 4:6, :],
                               sgt.ap.rearrange("p (a b) -> p a b", a=2), ALU.mult),
                     reads=[PSB[4], PSB[5], sgt], writes=[et], add=(hf == 1))

        def B2(t):
            b = t % 2
            P.op("act", ACT(junk.ap, et.ap, AF.Square, accum_out=ssC[b].ap[:, 0:1]), reads=[et], writes=[ssC[b]])
            rstd_ops(ssC[b], rC[b], D)
            P.op("dve", STT(tmp.ap, et.ap, rC[b].ap[:, 0:1], g5.ap, ALU.mult, ALU.mult), reads=[et, rC[b], g5], writes=[tmp])
            P.op("dve", TT(yt[b].ap, tmp.ap, h1[b].ap, ALU.add), reads=[tmp, h1[b]], writes=[yt[b]])
            ytk = Tk()
            P.dma("sp", y_out[t * 128:(t + 1) * 128, :], yt[b].ap, reads=[yt[b]], writes=[ytk])
            Y_TK.append(ytk)

        B1(0)
        C(0)
        for t in range(NT):
            if t + 1 < NT:
                B1(t + 1)
            A(t)
            if t + 1 < NT:
                C(t + 1)
            B2(t)
        return Y_TK

    KA_TK, VA_TK = phase1()
    P.barrier()
    OUT_TK = phase2()
    P.barrier()
    phase3(KA_TK, VA_TK, OUT_TK["qa"])
    P.barrier()
    phase4(OUT_TK)
    P.barrier()
    phase5()
    P.barrier()
    phase6()
    P.barrier()
    Y_TK = phase7()
    P.barrier()
    P.op("sp", lambda e: e.nop(), signal=False)
    for e in ("pe", "act", "dve", "pool"):
        P.bar[e] = []

    sem_names = ["pe", "act", "dve", "pool"] + ["d_sp_%d" % i for i in range(P.ring["sp"])] + ["d_pool_%d" % i for i in range(P.ring["pool"])]
    sem_cms = [nc.semaphore(n) for n in sem_names]
    sems = {n: cm.__enter__() for n, cm in zip(sem_names, sem_cms)}

    ref = {k: set() for k in ("pe", "act", "dve", "pool")}
    for name in ENGS:
        for ws, fn, sig in P.q[name]:
            for k, v in ws:
                if k in ref:
                    ref[k].add(v)
    remap = {k: {v: i + 1 for i, v in enumerate(sorted(vs))} for k, vs in ref.items()}

    def replay(name, e):
        c = 0
        for ws, fn, sig in P.q[name]:
            for k, v in ws:
                e.wait_ge(sems[k], remap[k][v] if k in remap else v)
            ins = fn(e)
            if sig is not None:
                if sig[0] in remap:
                    c += 1
                    if c in ref[sig[0]]:
                        ins.then_inc(sems[sig[0]], 1)
                else:
                    ins.then_inc(sems[sig[0]], sig[1])

    with nc.Block() as block:
        @block.tensor
        def _(e):
            replay("pe", e)

        @block.scalar
        def _(e):
            replay("act", e)

        @block.vector
        def _(e):
            replay("dve", e)

        @block.gpsimd
        def _(e):
            replay("pool", e)

        @block.sync
        def _(e):
            replay("sp", e)

    for cm in reversed(sem_cms):
        cm.__exit__(None, None, None)
    psum_cm.__exit__(None, None, None)
    arena_cm.__exit__(None, None, None)
    return nc


DEBUG_OUTS = ()


def _rope_tables():
    t = np.arange(S)
    row = (t // 64).astype(np.float32)
    col = (t % 64).astype(np.float32)
    freqs = (np.float32(10000.0) ** (-np.arange(32, dtype=np.float32) / np.float32(32))).astype(np.float32)
    ar = row[:, None] * freqs[None, :]
    ac = col[:, None] * freqs[None, :]
    cos = np.concatenate([np.cos(ar), np.cos(ar), np.cos(ac), np.cos(ac)], axis=1).astype(np.float32)
    sin = np.concatenate([-np.sin(ar), np.sin(ar), -np.sin(ac), np.sin(ac)], axis=1).astype(np.float32)
    return cos, sin


def _bias_tiles(rpb, core):
    out = np.full((3, 8, 8 * 128, 512), NEG, dtype=np.float32)
    for typ, qb in ((0, 0), (1, 1), (2, 3)):
        R = 32 * core + 8 * qb
        q = np.arange(512)
        r = R + q // 64
        c = q % 64
        r0 = np.clip(r - 4, 0, 256 - 8)
        c0 = np.clip(c - 8, 0, 64 - 16)
        k = np.arange(1024)
        kr = (R - 4) + k // 64
        kc = k % 64
        inr = (kr[:, None] >= r0[None, :]) & (kr[:, None] < r0[None, :] + 8)
        inc = (kc[:, None] >= c0[None, :]) & (kc[:, None] < c0[None, :] + 16)
        ok = inr & inc
        dr = np.clip(kr[:, None] - r[None, :] + 7, 0, 14)
        dc = np.clip(kc[:, None] - c[None, :] + 15, 0, 30)
        for h in range(8):
            g = rpb[h][dr, dc]
            out[typ, h] = np.where(ok, g, np.float32(NEG))
    return np.ascontiguousarray(out.reshape(3, 8, 8, 128, 512).transpose(0, 1, 3, 2, 4))


def _prepare_inputs(x, p, pre_mix_norm, w_in, q_norm, k_norm, rel_pos_bias, w_o, post_mix_norm, pre_mlp_norm,
                    w_up, w_down, post_mlp_norm, pre_ple_norm, w_ple_gate, w_ple_proj, post_ple_norm):
    f = lambda a: np.ascontiguousarray(np.asarray(a, dtype=np.float32))
    x2 = f(x)[0]
    p2 = f(p)[0, 0]
    gains = np.stack([np.broadcast_to(f(g)[0][None, :], (128, D)) for g in
                      (pre_mix_norm, post_mix_norm, pre_mlp_norm, post_mlp_norm, pre_ple_norm, post_ple_norm)])
    gains = np.ascontiguousarray(gains)
    gainsT = np.ascontiguousarray(np.stack([f(g)[0].reshape(16, 128).T for g in
                                            (pre_mix_norm, post_mix_norm, pre_mlp_norm, post_mlp_norm, pre_ple_norm, post_ple_norm)]))
    gq = np.ascontiguousarray(np.broadcast_to(np.tile(f(q_norm)[0], 4)[None, :], (128, 512)))
    gk = np.ascontiguousarray(np.broadcast_to(np.tile(f(k_norm)[0], 2)[None, :], (128, 256)))
    cos, sin = _rope_tables()
    rpb = f(rel_pos_bias)[0]
    ident = np.eye(128, dtype=np.float32)
    shared = {
        "x_all": x2, "w_in": f(w_in)[0], "w_o": f(w_o)[0], "w_up": f(w_up)[0], "w_down": f(w_down)[0],
        "w_gate": f(w_ple_gate)[0], "w_proj": f(w_ple_proj)[0], "gains": gains, "gainsT": gainsT, "gq": gq, "gk": gk,
        "cos_all": cos, "sin_all": sin, "ident": ident,
    }
    in_maps = []
    xpad = np.concatenate([np.zeros((256, D), np.float32), x2, np.zeros((256, D), np.float32)], axis=0)
    for c in range(NCORES):
        m = dict(shared)
        m["x_ext"] = np.ascontiguousarray(xpad[c * TOK:c * TOK + ETOK])
        m["pT"] = np.ascontiguousarray(p2[c * TOK:(c + 1) * TOK].T)
        m["cos_own"] = np.ascontiguousarray(cos[c * TOK:(c + 1) * TOK])
        m["sin_own"] = np.ascontiguousarray(sin[c * TOK:(c + 1) * TOK])
        m["biasT"] = _bias_tiles(rpb, c)
        in_maps.append(m)
    return in_maps


_NC_CACHE = {}


def kernel(**inputs):
    in_maps = _prepare_inputs(**inputs)
    if "nc" not in _NC_CACHE:
        _NC_CACHE["nc"] = build_program()
    nc = _NC_CACHE["nc"]
    res = run_bass_kernel_spmd(nc, in_maps, core_ids=list(range(NCORES)))
    out = np.concatenate([np.asarray(r["y"], dtype=np.float32) for r in res.results], axis=0)
    if DEBUG:
        kernel.last_results = res.results
    return out.reshape(1, S, D)
```

```python
import numpy as np
import concourse.bass as bass
import concourse.mybir as mybir
from concourse.bass_utils import run_bass_kernel_spmd

F32 = mybir.dt.float32
BF16 = mybir.dt.bfloat16
AF = mybir.ActivationFunctionType
ALU = mybir.AluOpType

NCORES = 8
S = 16384
D = 2048
TOK = 2048
NT = 16
NE = 20
ETOK = NE * 128
DFF = 8192
EPS = 1e-6
SCALE = 1.0 / np.sqrt(128.0)
NEG = -30000.0
DEBUG = False


class Tk:
    __slots__ = ("w", "r")

    def __init__(self):
        self.w = {}
        self.r = {}


class Tile:
    __slots__ = ("ap", "tk")

    def __init__(self, ap):
        self.ap = ap
        self.tk = Tk()


def _tk(x):
    return x.tk if isinstance(x, Tile) else x


ENGS = ("pe", "act", "dve", "pool", "sp")


class Prog:
    def __init__(self):
        self.q = {e: [] for e in ENGS}
        self.cnt = {e: 0 for e in ENGS}
        self.waited = {e: {} for e in ENGS}
        self.ring = {"sp": 28, "pool": 12}
        self.dma_i = {"sp": 0, "pool": 0}
        self.dma_last = {}
        self.bar = {e: [] for e in ENGS}

    def _collect(self, eng, reads, writes, extra, add):
        toks = list(extra) + self.bar[eng]
        self.bar[eng] = []
        for t in reads:
            toks.extend(_tk(t).w.items())
        for t in writes:
            t = _tk(t)
            if not add:
                toks.extend(t.w.items())
            toks.extend(t.r.items())
        out = []
        wd = self.waited[eng]
        for k, v in toks:
            if k == "pe" and eng == "pe":
                continue
            if wd.get(k, 0) >= v:
                continue
            wd[k] = v
            out.append((k, v))
        return out

    def _mark(self, tok, reads, writes, add):
        k, v = tok
        for t in reads:
            t = _tk(t)
            if t.r.get(k, 0) < v:
                t.r[k] = v
        for t in writes:
            t = _tk(t)
            if add:
                if t.w.get(k, 0) < v:
                    t.w[k] = v
            else:
                t.w = {k: v}
            t.r = {}

    def op(self, eng, fn, reads=(), writes=(), signal=True, extra=(), add=False):
        ws = self._collect(eng, reads, writes, extra, add)
        if signal:
            self.cnt[eng] += 1
            tok = (eng, self.cnt[eng])
            sig = (eng, 1)
        else:
            tok = (eng, self.cnt[eng] + 1)
            sig = None
        self.q[eng].append((ws, fn, sig))
        self._mark(tok, reads, writes, add)
        return tok

    def dma(self, queue, out_ap, in_ap, reads=(), writes=(), add=False, extra=()):
        i = self.dma_i[queue]
        self.dma_i[queue] += 1
        n = self.ring[queue]
        key = "d_%s_%d" % (queue, i % n)
        val = 16 * (i // n + 1)
        extra = list(extra) + ([(key, val - 16)] if i >= n else [])
        ws = self._collect(queue, reads, writes, extra, add)
        self.q[queue].append((ws, (lambda e, o=out_ap, a=in_ap: e.dma_start(out=o, in_=a)), (key, 16)))
        tok = (key, val)
        self.dma_last[key] = val
        self._mark(tok, reads, writes, add)
        return tok

    def barrier(self):
        toks = [(e, self.cnt[e]) for e in ("pe", "act", "dve", "pool") if self.cnt[e] > 0]
        toks += list(self.dma_last.items())
        for e in ENGS:
            self.bar[e] = self.bar[e] + toks


def MM(out, lhsT, rhs, start, stop):
    return lambda e: e.matmul(out, lhsT, rhs, start=start, stop=stop)


def TR(out, in_, ident):
    return lambda e: e.transpose(out, in_, ident)


def ACT(out, in_, func, **kw):
    return lambda e: e.activation(out=out, in_=in_, func=func, **kw)


def TT(out, in0, in1, op):
    return lambda e: e.tensor_tensor(out=out, in0=in0, in1=in1, op=op)


def TS(out, in0, s1, s2, op0, op1=None):
    if op1 is None:
        return lambda e: e.tensor_scalar(out=out, in0=in0, scalar1=s1, scalar2=None, op0=op0)
    return lambda e: e.tensor_scalar(out=out, in0=in0, scalar1=s1, scalar2=s2, op0=op0, op1=op1)


def STT(out, in0, scalar, in1, op0, op1):
    return lambda e: e.scalar_tensor_tensor(out=out, in0=in0, scalar=scalar, in1=in1, op0=op0, op1=op1)


def CP(out, in_):
    return lambda e: e.tensor_copy(out, in_)


def RCP(out, in_):
    return lambda e: e.reciprocal(out=out, in_=in_)


def MSET(ap, v):
    return lambda e: e.memset(ap, v)


def build_program():
    nc = bass.Bass("TRN2", target_bir_lowering=False)
    P = Prog()

    def din(name, shape, dt=F32):
        return nc.dram_tensor(name, list(shape), dt, kind="ExternalInput").ap()

    def dscr(name, shape, dt=BF16):
        kind = "ExternalOutput" if (DEBUG and name in DEBUG_OUTS) else "Internal"
        return nc.dram_tensor(name, list(shape), dt, kind=kind).ap()

    x_all = din("x_all", [S, D])
    x_ext = din("x_ext", [ETOK, D])
    pT_in = din("pT", [256, TOK])
    w_in = din("w_in", [D, 4608])
    w_o = din("w_o", [D, D])
    w_up = din("w_up", [D, DFF])
    w_dn = din("w_down", [DFF, D])
    w_g = din("w_gate", [D, D])
    w_p = din("w_proj", [256, D])
    gains = din("gains", [6, 128, D])
    gainsT = din("gainsT", [6, 128, 16])
    gq_in = din("gq", [128, 512])
    gk_in = din("gk", [128, 256])
    cos_all = din("cos_all", [S, 128])
    sin_all = din("sin_all", [S, 128])
    cos_own = din("cos_own", [TOK, 128])
    sin_own = din("sin_own", [TOK, 128])
    biasT = din("biasT", [3, 8, 128, 8, 512])
    ident_in = din("ident", [128, 128])
    y_out = nc.dram_tensor("y", [TOK, D], F32, kind="ExternalOutput").ap()

    kaT_s = dscr("kaT_s", [2, 128, S])
    va_s = dscr("va_s", [128, 2, 128, 128])
    qaT_s = dscr("qaT_s", [8, 128, TOK])
    qbT_s = dscr("qbT_s", [8, 128, TOK])
    kbT_s = dscr("kbT_s", [8, 128, ETOK])
    vb_s = dscr("vb_s", [128, 8, NE, 128])
    mixT_s = dscr("mixT_s", [16, 128, TOK])
    h1_s = dscr("h1_s", [TOK, D], F32)
    h1nT_s = dscr("h1nT_s", [4, 128, 16, 512])
    wup_s = dscr("wup_s", [D, DFF])
    wdn_s = dscr("wdn_s", [DFF, D])
    m_s = dscr("m_s", [TOK, D], F32)

    ARENA_W = 48 * 1024
    arena_cm = nc.sbuf_tensor("arena", [128, ARENA_W], F32)
    psum_cm = nc.psum_tensor("ps", [128, 8, 512], F32)
    arena = arena_cm.__enter__()
    ps = psum_cm.__enter__()
    psb = ps[:].bitcast(BF16)
    PSB = [Tk() for _ in range(8)]

    st = {"off": 0}

    def reset(off=0):
        st["off"] = off

    def sb(shape, dt):
        n = int(np.prod(shape))
        nb = n * (2 if dt == BF16 else 4)
        nb = (nb + 63) // 64 * 64
        off = st["off"]
        st["off"] = off + nb
        assert st["off"] <= ARENA_W * 4, "sbuf arena overflow %d" % st["off"]
        v = arena[:, off // 4:(off + nb) // 4]
        if dt == BF16:
            v = v.bitcast(BF16)
        v = v[:, 0:n]
        if len(shape) == 2:
            v = v.rearrange("p (a b) -> p a b", a=shape[0])
        elif len(shape) == 3:
            v = v.rearrange("p (a b c) -> p a b c", a=shape[0], b=shape[1])
        return Tile(v)

    ident = sb([128], BF16)
    ones = sb([128], F32)
    onesb = sb([128], BF16)
    P.dma("pool", ident.ap, ident_in[:, :], writes=[ident])
    P.op("pool", MSET(ones.ap, 1.0), writes=[ones])
    P.op("pool", MSET(onesb.ap, 1.0), writes=[onesb])
    BASE = st["off"]

    WUP_TK = [Tk() for _ in range(16)]
    WDN_TK = [Tk() for _ in range(64)]

    cast_list = [(wup_s[c * 128:(c + 1) * 128, :], w_up[c * 128:(c + 1) * 128, :], WUP_TK[c]) for c in range(16)] + \
                [(wdn_s[c * 128:(c + 1) * 128, :], w_dn[c * 128:(c + 1) * 128, :], WDN_TK[c]) for c in range(64)]

    def issue_weight_casts(n, extra=()):
        for _ in range(n):
            if cast_list:
                o, i_, tk = cast_list.pop(0)
                P.dma("pool", o, i_, writes=[tk], extra=extra)

    def rstd_ops(ss, r, n):
        P.op("dve", TS(r.ap, ss.ap, 1.0 / n, EPS, ALU.mult, ALU.add), reads=[ss], writes=[r])
        P.op("act", ACT(r.ap, r.ap, AF.Sqrt), reads=[r], writes=[r])
        P.op("dve", RCP(r.ap, r.ap), reads=[r], writes=[r])

    def norm_to_bf16(src, gain, junk, ss, r, xg, mode="stt"):
        P.op("act", ACT(junk.ap, src.ap, AF.Square, accum_out=ss.ap[:, 0:1]), reads=[src], writes=[ss])
        rstd_ops(ss, r, D)
        if mode == "fold":
            P.op("dve", TS(xg.ap, src.ap, r.ap[:, 0:1], None, ALU.mult), reads=[src, r], writes=[xg])
        else:
            P.op("dve", STT(xg.ap, src.ap, r.ap[:, 0:1], gain.ap, ALU.mult, ALU.mult), reads=[src, r, gain], writes=[xg])

    def fold_gain(wt, gT, nchunks=16):
        for c in range(nchunks):
            P.op("dve", TS(wt.ap[:, c, :], wt.ap[:, c, :], gT.ap[:, c:c + 1], None, ALU.mult), reads=[wt, gT], writes=[wt])

    def transpose16(xg, dst_fn, dst_tiles, banks=(6, 7)):
        for half in range(2):
            bk = banks[half]
            for j in range(8):
                c = half * 8 + j
                P.op("pe", TR(psb[:, bk, j * 128:(j + 1) * 128], xg.ap[:, c * 128:(c + 1) * 128], ident.ap),
                     reads=[xg, ident], writes=[PSB[bk]], signal=(j == 7))
            eng = "act" if half == 0 else "dve"
            src = psb[:, bk, :].rearrange("p (c t) -> p c t", c=8)
            fn = ACT(dst_fn(half * 8, half * 8 + 8), src, AF.Copy) if eng == "act" else CP(dst_fn(half * 8, half * 8 + 8), src)
            P.op(eng, fn, reads=[PSB[bk]], writes=dst_tiles)

    def norm_rope(src_ap, src_tk, nh, gain, cs, sn, kg, t1, t2, hss, hr, junk, outb, rs=None):
        W = nh * 128
        rr = [rs] if rs is not None else []
        for h in range(nh):
            kw = {"scale": rs.ap[:, 0:1]} if rs is not None else {}
            P.op("act", ACT(junk.ap[:, 0:128], src_ap[:, h * 128:(h + 1) * 128], AF.Square, accum_out=hss.ap[:, h:h + 1], **kw),
                 reads=[src_tk] + rr, writes=[hss], add=True)
        rstd_ops(hss, hr, 128)
        if rs is not None:
            P.op("dve", STT(kg.ap[:, 0:W], src_ap, rs.ap[:, 0:1], gain.ap[:, 0:W], ALU.mult, ALU.mult), reads=[src_tk, gain, rs], writes=[kg])
        else:
            P.op("dve", TT(kg.ap[:, 0:W], src_ap, gain.ap[:, 0:W], ALU.mult), reads=[src_tk, gain], writes=[kg])
        kg3 = kg.ap[:, 0:W].rearrange("p (h d) -> p h d", h=nh)
        t13 = t1.ap[:, 0:W].rearrange("p (h d) -> p h d", h=nh)
        P.op("dve", TT(t13, kg3, cs.ap.unsqueeze(1).broadcast_to([128, nh, 128]), ALU.mult), reads=[kg, cs], writes=[t1])
        kg5 = kg.ap[:, 0:W].rearrange("p (h s f i) -> p h s f i", h=nh, s=2, f=2)
        t25 = t2.ap[:, 0:W].rearrange("p (h s f i) -> p h s f i", h=nh, s=2, f=2)
        sn4 = sn.ap.rearrange("p (s f i) -> p s f i", s=2, f=2)
        for f in range(2):
            P.op("dve", TT(t25[:, :, :, f, :], kg5[:, :, :, 1 - f, :],
                           sn4[:, :, f, :].unsqueeze(1).broadcast_to([128, nh, 2, 32]), ALU.mult),
                 reads=[kg, sn], writes=[t2], add=(f == 1))
        P.op("dve", TT(t1.ap[:, 0:W], t1.ap[:, 0:W], t2.ap[:, 0:W], ALU.add), reads=[t1, t2], writes=[t1])
        for h in range(nh):
            P.op("act", ACT(outb.ap[:, h, :], t1.ap[:, h * 128:(h + 1) * 128], AF.Copy, scale=hr.ap[:, h:h + 1]),
                 reads=[t1, hr], writes=[outb], add=(h > 0))

    def phase1():
        reset(BASE)
        wkv = sb([16, 512], BF16)
        gT = sb([16], F32)
        gk = sb([256], F32)
        xt = [sb([D], F32) for _ in range(3)]
        xg = [sb([D], BF16) for _ in range(3)]
        xnT = [sb([16, 128], BF16) for _ in range(2)]
        junk = sb([D], BF16)
        cs = [sb([128], F32) for _ in range(3)]
        sn = [sb([128], F32) for _ in range(3)]
        ss = [sb([1], F32) for _ in range(3)]
        r = [sb([1], F32) for _ in range(3)]
        hss = [sb([2], F32) for _ in range(2)]
        hr = [sb([2], F32) for _ in range(2)]
        kg = sb([256], F32)
        t1 = sb([256], F32)
        t2 = sb([256], F32)
        kb = [sb([2, 128], BF16) for _ in range(2)]
        kst = [sb([2, 512], BF16) for _ in range(2)]
        vst = [sb([2, 4, 128], BF16) for _ in range(2)]
        KA_TK = [Tk() for _ in range(32)]
        VA_TK = [Tk() for _ in range(32)]

        P.dma("pool", wkv.ap, w_in[:, 1024:1536].rearrange("(c p) n -> p c n", p=128), writes=[wkv])
        P.dma("sp", gT.ap, gainsT[0], writes=[gT])
        P.dma("sp", gk.ap, gk_in[:, :], writes=[gk])
        fold_gain(wkv, gT)

        def front_a(i):
            b3 = i % 3
            P.dma("sp", xt[b3].ap, x_all[i * 128:(i + 1) * 128, :], writes=[xt[b3]])
            P.dma("sp", cs[b3].ap, cos_all[i * 128:(i + 1) * 128, :], writes=[cs[b3]])
            P.dma("sp", sn[b3].ap, sin_all[i * 128:(i + 1) * 128, :], writes=[sn[b3]])
            P.op("dve", CP(xg[b3].ap, xt[b3].ap), reads=[xt[b3]], writes=[xg[b3]])
            P.op("act", ACT(junk.ap, xt[b3].ap, AF.Square, accum_out=ss[b3].ap[:, 0:1]), reads=[xt[b3]], writes=[ss[b3]])
            rstd_ops(ss[b3], r[b3], D)

        def front_b(i):
            b = i % 2
            b3 = i % 3
            transpose16(xg[b3], lambda c0, c1, b=b: xnT[b].ap[:, c0:c1, :], [xnT[b]], banks=(6, 7))
            bank = i % 2
            for c in range(16):
                P.op("pe", MM(ps[:, bank, :], xnT[b].ap[:, c, :], wkv.ap[:, c, :], c == 0, c == 15),
                     reads=[xnT[b], wkv], writes=[PSB[bank]], signal=(c == 15))

        def back(i):
            b = i % 2
            b3 = i % 3
            g4 = i // 4
            gb = g4 % 2
            bank = i % 2
            P.op("act", ACT(vst[gb].ap[:, :, i % 4, :], ps[:, bank, 256:512].rearrange("p (k d) -> p k d", k=2), AF.Copy, scale=r[b3].ap[:, 0:1]),
                 reads=[PSB[bank], r[b3]], writes=[vst[gb]], add=(i % 4 != 0))
            norm_rope(ps[:, bank, 0:256], PSB[bank], 2, gk, cs[b3], sn[b3], kg, t1, t2, hss[b], hr[b], junk, kb[b], rs=r[b3])
            tb = 4 + (i % 2)
            for h in range(2):
                P.op("pe", TR(psb[:, tb, h * 128:(h + 1) * 128], kb[b].ap[:, h, :], ident.ap),
                     reads=[kb[b], ident], writes=[PSB[tb]], signal=(h == 1))
            P.op("dve", CP(kst[gb].ap[:, :, (i % 4) * 128:(i % 4 + 1) * 128], psb[:, tb, 0:256].rearrange("p (k t) -> p k t", k=2)),
                 reads=[PSB[tb]], writes=[kst[gb]], add=(i % 4 != 0))
            if i % 4 == 3:
                P.dma("sp", kaT_s[:, :, g4 * 512:(g4 + 1) * 512].rearrange("k d t -> d k t"), kst[gb].ap,
                      reads=[kst[gb]], writes=[KA_TK[g4]])
                P.dma("sp", va_s[:, :, g4 * 4:(g4 + 1) * 4, :], vst[gb].ap, reads=[vst[gb]], writes=[VA_TK[g4]])

        front_a(0)
        front_a(1)
        front_b(0)
        for i in range(128):
            if i + 1 < 128:
                front_b(i + 1)
            if i + 2 < 128:
                front_a(i + 2)
            back(i)
        return KA_TK, VA_TK

    def phase2():
        reset(BASE)
        xnT = sb([16, ETOK], BF16)
        XN_TK = [Tk() for _ in range(NE)]
        gT = sb([16], F32)
        gq = sb([512], F32)
        mark = st["off"]
        xt = [sb([D], F32) for _ in range(2)]
        xg = [sb([D], BF16) for _ in range(2)]
        junk = sb([D], BF16)
        ss = [sb([1], F32) for _ in range(2)]
        r = [sb([1], F32) for _ in range(2)]

        P.dma("sp", gT.ap, gainsT[0], writes=[gT])
        P.dma("sp", gq.ap, gq_in[:, :], writes=[gq])
        for e in range(NE):
            b = e % 2
            P.dma("sp", xt[b].ap, x_ext[e * 128:(e + 1) * 128, :], writes=[xt[b]])
            norm_to_bf16(xt[b], None, junk, ss[b], r[b], xg[b], mode="fold")
            transpose16(xg[b], lambda c0, c1, e=e: xnT.ap[:, c0:c1, e * 128:(e + 1) * 128], [XN_TK[e]],
                        banks=(6, 7) if e % 2 == 0 else (4, 5))
        P.barrier()
        reset(mark)
        wblk = [sb([16, 512], BF16) for _ in range(2)]
        stage = [sb([4 * ETOK], BF16) for _ in range(2)]
        junk = sb([D], BF16)
        cs = [sb([128], F32) for _ in range(2)]
        sn = [sb([128], F32) for _ in range(2)]
        hss = [sb([4], F32) for _ in range(2)]
        hr = [sb([4], F32) for _ in range(2)]
        kg = sb([512], F32)
        t1 = sb([512], F32)
        t2 = sb([512], F32)
        qb_ = [sb([4, 128], BF16) for _ in range(2)]

        blocks = [("qa", 0, 0), ("qa", 1, 512), ("qb", 0, 1536), ("qb", 1, 2048),
                  ("kb", 0, 2560), ("kb", 1, 3072), ("vb", 0, 3584), ("vb", 1, 4096)]
        bankrot = [0]

        def nbank():
            bk = bankrot[0] % 4
            bankrot[0] += 1
            return bk

        OUT_TK = {"qa": [Tk(), Tk()], "qb": [Tk(), Tk()], "kb": [Tk(), Tk()], "vb": [Tk(), Tk()]}
        pend = [None]

        def load_block(bi_):
            wb_ = wblk[bi_ % 2]
            col_ = blocks[bi_][2]
            P.dma("pool", wb_.ap, w_in[:, col_:col_ + 512].rearrange("(c p) n -> p c n", p=128), writes=[wb_])
            pend[0] = wb_

        def maybe_fold():
            if pend[0] is not None:
                fold_gain(pend[0], gT)
                pend[0] = None

        load_block(0)
        maybe_fold()
        for bi, (kind, half, col) in enumerate(blocks):
            wb = wblk[bi % 2]
            sg = stage[bi % 2]
            if bi + 1 < len(blocks):
                load_block(bi + 1)
            if kind == "qa":
                sgv = sg.ap[:, 0:4 * TOK].rearrange("p (h t) -> p h t", h=4)
                for t in range(NT):
                    if t == 5:
                        maybe_fold()
                    e = t + 2
                    b = t % 2
                    P.dma("sp", cs[b].ap, cos_own[t * 128:(t + 1) * 128, :], writes=[cs[b]])
                    P.dma("sp", sn[b].ap, sin_own[t * 128:(t + 1) * 128, :], writes=[sn[b]])
                    bk = nbank()
                    for c in range(16):
                        P.op("pe", MM(ps[:, bk, :], xnT.ap[:, c, e * 128:(e + 1) * 128], wb.ap[:, c, :], c == 0, c == 15),
                             reads=[XN_TK[e], wb], writes=[PSB[bk]], signal=(c == 15))
                    norm_rope(ps[:, bk, :], PSB[bk], 4, gq, cs[b], sn[b], kg, t1, t2, hss[b], hr[b], junk, qb_[b])
                    tb = 4 + (t % 2)
                    for h in range(4):
                        P.op("pe", TR(psb[:, tb, h * 128:(h + 1) * 128], qb_[b].ap[:, h, :], ident.ap),
                             reads=[qb_[b], ident], writes=[PSB[tb]], signal=(h == 3))
                    P.op("dve", CP(sgv[:, :, t * 128:(t + 1) * 128], psb[:, tb, 0:512].rearrange("p (h t) -> p h t", h=4)),
                         reads=[PSB[tb]], writes=[sg], add=(t > 0))
                P.dma("sp", qaT_s[half * 4:(half + 1) * 4].rearrange("h d t -> d h t"), sgv, reads=[sg], writes=[OUT_TK[kind][half]])
            elif kind in ("qb", "kb"):
                ntok = TOK if kind == "qb" else ETOK
                e0 = 2 if kind == "qb" else 0
                sgv = sg.ap[:, 0:4 * ntok].rearrange("p (h t) -> p h t", h=4)
                first = True
                for j in range(4):
                    if j == 1:
                        maybe_fold()
                    for sbk in range(ntok // 512):
                        tk0 = e0 * 128 + sbk * 512
                        bk = nbank()
                        for c in range(16):
                            P.op("pe", MM(ps[:, bk, :], wb.ap[:, c, j * 128:(j + 1) * 128], xnT.ap[:, c, tk0:tk0 + 512], c == 0, c == 15),
                                 reads=[wb] + XN_TK[tk0 // 128:tk0 // 128 + 4], writes=[PSB[bk]], signal=(c == 15))
                        eng = "act" if (sbk % 2 == 0) else "dve"
                        dst = sgv[:, j, sbk * 512:(sbk + 1) * 512]
                        fn = ACT(dst, ps[:, bk, :], AF.Copy) if eng == "act" else CP(dst, ps[:, bk, :])
                        P.op(eng, fn, reads=[PSB[bk]], writes=[sg], add=(not first))
                        first = False
                dst_s = qbT_s if kind == "qb" else kbT_s
                P.dma("sp", dst_s[half * 4:(half + 1) * 4].rearrange("h d t -> d h t"), sgv, reads=[sg], writes=[OUT_TK[kind][half]])
            else:
                sgv = sg.ap[:, 0:NE * 512].rearrange("p (h e d) -> p h e d", h=4, e=NE)
                for e in range(NE):
                    if e == 6:
                        maybe_fold()
                    bk = nbank()
                    for c in range(16):
                        P.op("pe", MM(ps[:, bk, :], xnT.ap[:, c, e * 128:(e + 1) * 128], wb.ap[:, c, :], c == 0, c == 15),
                             reads=[XN_TK[e], wb], writes=[PSB[bk]], signal=(c == 15))
                    eng = "act" if (e % 2 == 0) else "dve"
                    dst = sgv[:, :, e, :]
                    src = ps[:, bk, :].rearrange("p (h d) -> p h d", h=4)
                    fn = ACT(dst, src, AF.Copy) if eng == "act" else CP(dst, src)
                    P.op(eng, fn, reads=[PSB[bk]], writes=[sg], add=(e > 0))
                P.dma("sp", vb_s[:, half * 4:(half + 1) * 4, :, :], sgv, reads=[sg], writes=[OUT_TK[kind][half]])
        return OUT_TK

    def attn_iter(it, kT_fn, v_fn, k_reads, q_ap, q_reads, ngroups, bufs, out_dst, out_tk, bias=None, pek=4):
        pT, accD, ofp, sbias, rl, ost = bufs["pT"], bufs["accD"], bufs["ofp"], bufs["sbias"], bufs["rl"], bufs["ost"]
        ob = 6
        ostb = ost[it % 2]

        def qk(j):
            s0 = (j % 3) * 2
            for u in range(2):
                P.op("pe", MM(ps[:, s0 + u, :], kT_fn(2 * j + u), q_ap, True, True),
                     reads=k_reads + q_reads, writes=[PSB[s0 + u]], signal=(u == 1))

        def softmax_pv(j):
            s0 = (j % 3) * 2
            pt = pT[j % 3]
            src = ps[:, s0:s0 + 2, :]
            if bias is not None:
                sbt = sbias[j % 2]
                P.op("dve", STT(sbt.ap, src, float(SCALE), bias[0][:, 2 * j:2 * j + 2, :], ALU.mult, ALU.add),
                     reads=[PSB[s0], PSB[s0 + 1], bias[1]], writes=[sbt])
                P.op("act", ACT(pt.ap, sbt.ap, AF.Exp), reads=[sbt], writes=[pt])
            else:
                P.op("act", ACT(pt.ap, src, AF.Exp, scale=float(SCALE)), reads=[PSB[s0], PSB[s0 + 1]], writes=[pt])
            for u in range(2):
                P.op("pe", MM(ps[:, ob, :], v_fn(2 * j + u), pt.ap[:, u, :], (j == 0 and u == 0), (j == ngroups - 1 and u == 1)),
                     reads=k_reads + [pt], writes=[PSB[ob]], signal=(u == 1))
            if j in pe_groups:
                for u in range(2):
                    P.op("pe", MM(ps[:, 7, :], onesb.ap, pt.ap[:, u, :], (j == pe_groups[0] and u == 0),
                                  (not dve_groups) and j == pe_groups[-1] and u == 1),
                         reads=[pt, onesb], writes=[PSB[7]], signal=(u == 1))
            elif j == dve_groups[0]:
                P.op("dve", CP(accD.ap, pt.ap), reads=[pt], writes=[accD])
            else:
                P.op("dve", TT(accD.ap, accD.ap, pt.ap, ALU.add), reads=[pt, accD], writes=[accD])

        pe_groups = [j for j in range(ngroups) if j % pek == pek - 1]
        dve_groups = [j for j in range(ngroups) if j % pek != pek - 1]
        qk(0)
        if ngroups > 1:
            qk(1)
        for j in range(ngroups):
            if j + 2 < ngroups:
                qk(j + 2)
            softmax_pv(j)
        P.op("dve", CP(ofp.ap, ps[:, ob, :]), reads=[PSB[ob]], writes=[ofp])
        if dve_groups:
            P.op("dve", TT(accD.ap[:, 0, :], accD.ap[:, 0, :], accD.ap[:, 1, :], ALU.add), reads=[accD], writes=[accD])
            P.op("pe", MM(ps[:, 7, :], ones.ap, accD.ap[:, 0, :], not pe_groups, True), reads=[accD, ones], writes=[PSB[7]])
        P.op("dve", RCP(rl.ap, ps[:, 7, :]), reads=[PSB[7]], writes=[rl])
        P.op("dve", TT(ostb.ap, ofp.ap, rl.ap, ALU.mult), reads=[ofp, rl], writes=[ostb])
        P.dma("sp", out_dst, ostb.ap, reads=[ostb], writes=[out_tk], add=True)

    def attn_bufs():
        return {
            "pT": [sb([2, 512], BF16) for _ in range(3)],
            "accD": sb([2, 512], F32), "ofp": sb([512], F32),
            "sbias": [sb([2, 512], F32) for _ in range(2)],
            "rl": sb([512], F32),
            "ost": [sb([512], BF16) for _ in range(2)],
        }

    MIX_TK = [Tk() for _ in range(16)]

    def phase3(KA_TK, VA_TK, QA_TK):
        reset(BASE)
        kT = sb([S], BF16)
        vv = sb([128, 128], BF16)
        qT = sb([4, TOK], BF16)
        bufs = attn_bufs()
        it = 0
        for kv in range(2):
            for c4 in range(4):
                P.dma("sp", kT.ap[:, c4 * 4096:(c4 + 1) * 4096], kaT_s[kv, :, c4 * 4096:(c4 + 1) * 4096],
                      reads=KA_TK[c4 * 8:(c4 + 1) * 8], writes=[kT], add=(c4 > 0))
                P.dma("sp", vv.ap[:, c4 * 32:(c4 + 1) * 32, :], va_s[:, kv, c4 * 32:(c4 + 1) * 32, :],
                      reads=VA_TK[c4 * 8:(c4 + 1) * 8], writes=[vv], add=(c4 > 0))
            P.dma("sp", qT.ap, qaT_s[kv * 4:(kv + 1) * 4].rearrange("h d t -> d h t"), reads=[QA_TK[kv]], writes=[qT])
            for g in range(4):
                h = kv * 4 + g
                for qb in range(4):
                    issue_weight_casts(3, extra=[("act", P.cnt["act"])] if P.cnt["act"] else ())
                    attn_iter(it, lambda kt: kT.ap[:, kt * 128:(kt + 1) * 128], lambda kt: vv.ap[:, kt, :], [kT, vv],
                              qT.ap[:, g, qb * 512:(qb + 1) * 512], [qT], 64, bufs,
                              mixT_s[h, :, qb * 512:(qb + 1) * 512], MIX_TK[h])
                    it += 1

    def phase4(OUT_TK):
        reset(BASE)
        issue_weight_casts(100)
        kT = [sb([ETOK], BF16) for _ in range(2)]
        vv = [sb([NE, 128], BF16) for _ in range(2)]
        qT = [sb([TOK], BF16) for _ in range(2)]
        bt = [sb([8, 512], F32) for _ in range(2)]
        bufs = attn_bufs()
        it = 0

        def load_bias(it_):
            h_, qb_ = it_ // 4, it_ % 4
            typ = 0 if qb_ == 0 else (2 if qb_ == 3 else 1)
            P.dma("sp", bt[it_ % 2].ap, biasT[typ, h_], writes=[bt[it_ % 2]])

        def load_head(h_):
            hb_ = h_ % 2
            P.dma("sp", kT[hb_].ap, kbT_s[h_], reads=[OUT_TK["kb"][h_ // 4]], writes=[kT[hb_]])
            P.dma("sp", vv[hb_].ap, vb_s[:, h_, :, :], reads=[OUT_TK["vb"][h_ // 4]], writes=[vv[hb_]])
            P.dma("sp", qT[hb_].ap, qbT_s[h_], reads=[OUT_TK["qb"][h_ // 4]], writes=[qT[hb_]])

        load_head(0)
        load_bias(0)
        for h in range(8):
            hb = h % 2
            for qb in range(4):
                if it + 1 < 32:
                    if qb == 3:
                        load_head(h + 1)
                    load_bias(it + 1)
                btt = bt[it % 2]
                attn_iter(it, lambda kt, hb=hb, qb=qb: kT[hb].ap[:, (4 * qb + kt) * 128:(4 * qb + kt + 1) * 128],
                          lambda kt, hb=hb, qb=qb: vv[hb].ap[:, 4 * qb + kt, :], [kT[hb], vv[hb]],
                          qT[hb].ap[:, qb * 512:(qb + 1) * 512], [qT[hb]], 4, bufs,
                          mixT_s[8 + h, :, qb * 512:(qb + 1) * 512], MIX_TK[8 + h], bias=(btt.ap, btt), pek=1)
                it += 1

    H1_TK = [Tk() for _ in range(NT)]
    H1N_TK = [Tk() for _ in range(4)]

    def phase5():
        reset(BASE)
        wo = sb([16, D], BF16)
        mixT = [sb([16, 512], BF16) for _ in range(1)]
        gpo = sb([D], F32)
        gpl = sb([D], F32)
        xt = [sb([D], F32) for _ in range(2)]
        tmp = sb([D], F32)
        h1 = [sb([D], F32) for _ in range(2)]
        xg = sb([D], BF16)
        junk = sb([D], BF16)
        hst = [sb([16, 512], BF16) for _ in range(2)]
        ss = [sb([1], F32) for _ in range(2)]
        r = [sb([1], F32) for _ in range(2)]
        ss1 = [sb([1], F32) for _ in range(2)]
        r1 = [sb([1], F32) for _ in range(2)]
        for n in range(4):
            P.dma("pool", wo.ap[:, :, n * 512:(n + 1) * 512], w_o[:, n * 512:(n + 1) * 512].rearrange("(c p) n -> p c n", p=128),
                  writes=[wo], add=(n > 0))
        P.dma("sp", gpo.ap, gains[1], writes=[gpo])
        P.dma("sp", gpl.ap, gains[2], writes=[gpl])

        def A(t):
            sbk, tt = t // 4, t % 4
            mx = mixT[0]
            b = t % 2
            g0 = 0 if t % 2 == 0 else 4
            if tt == 0:
                P.dma("sp", mx.ap, mixT_s[:, :, sbk * 512:(sbk + 1) * 512].rearrange("c d t -> d c t"), reads=MIX_TK, writes=[mx])
            P.dma("sp", xt[b].ap, x_ext[(t + 2) * 128:(t + 3) * 128, :], writes=[xt[b]])
            for n in range(4):
                for c in range(16):
                    P.op("pe", MM(ps[:, g0 + n, :], mx.ap[:, c, tt * 128:(tt + 1) * 128], wo.ap[:, c, n * 512:(n + 1) * 512], c == 0, c == 15),
                         reads=[mx, wo], writes=[PSB[g0 + n]], signal=(c == 15))

        def B(t):
            sbk, tt = t // 4, t % 4
            b = t % 2
            g0 = 0 if t % 2 == 0 else 4
            pall = ps[:, g0:g0 + 4, :]
            P.op("act", ACT(junk.ap.rearrange("p (a b) -> p a b", a=4), pall, AF.Square, accum_out=ss[b].ap[:, 0:1]),
                 reads=PSB[g0:g0 + 4], writes=[ss[b]])
            rstd_ops(ss[b], r[b], D)
            P.op("dve", STT(tmp.ap.rearrange("p (a b) -> p a b", a=4), pall, r[b].ap[:, 0:1], gpo.ap.rearrange("p (a b) -> p a b", a=4), ALU.mult, ALU.mult),
                 reads=PSB[g0:g0 + 4] + [r[b], gpo], writes=[tmp])
            P.op("dve", TT(h1[b].ap, tmp.ap, xt[b].ap, ALU.add), reads=[tmp, xt[b]], writes=[h1[b]])
            P.dma("sp", h1_s[t * 128:(t + 1) * 128, :], h1[b].ap, reads=[h1[b]], writes=[H1_TK[t]])
            norm_to_bf16(h1[b], gpl, junk, ss1[b], r1[b], xg)
            hs = hst[sbk % 2]
            transpose16(xg, lambda c0, c1, hs=hs, tt=tt: hs.ap[:, c0:c1, tt * 128:(tt + 1) * 128], [hs], banks=(g0, g0 + 1))
            if tt == 3:
                P.dma("sp", h1nT_s[sbk], hs.ap, reads=[hs], writes=[H1N_TK[sbk]])

        A(0)
        for t in range(NT):
            if t + 1 < NT:
                A(t + 1)
            B(t)

    M_TK = [Tk() for _ in range(4)]

    def phase6():
        reset(BASE)
        aT = sb([64, 512], BF16)
        AT_TK = [Tk() for _ in range(64)]
        hnT = [sb([16, 512], BF16) for _ in range(2)]
        wu = [sb([16, 512], BF16) for _ in range(2)]
        wd = [sb([4, 1024], BF16) for _ in range(3)]
        rt = [sb([512], F32) for _ in range(2)]
        mst = [sb([1024], F32) for _ in range(2)]
        k_up = 0
        k_dn = 0
        k_ms = 0
        for sbk in range(4):
            hn = hnT[sbk % 2]
            P.dma("sp", hn.ap, h1nT_s[sbk], reads=[H1N_TK[sbk]], writes=[hn])
            for fg in range(16):
                w = wu[k_up % 2]
                k_up += 1
                P.dma("sp", w.ap, wup_s[:, fg * 512:(fg + 1) * 512].rearrange("(c p) n -> p c n", p=128), reads=WUP_TK, writes=[w])
                for j in range(4):
                    fc = fg * 4 + j
                    bk = fc % 4
                    for c in range(16):
                        P.op("pe", MM(ps[:, bk, :], w.ap[:, c, j * 128:(j + 1) * 128], hn.ap[:, c, :], c == 0, c == 15),
                             reads=[w, hn], writes=[PSB[bk]], signal=(c == 15))
                    rtt = rt[fc % 2]
                    P.op("act", ACT(rtt.ap, ps[:, bk, :], AF.Relu), reads=[PSB[bk]], writes=[rtt])
                    P.op("dve", TT(aT.ap[:, fc, :], rtt.ap, rtt.ap, ALU.mult), reads=[rtt], writes=[AT_TK[fc]])
            for hf in range(2):
                for fgd in range(16):
                    w = wd[k_dn % 3]
                    k_dn += 1
                    P.dma("sp", w.ap, wdn_s[fgd * 512:(fgd + 1) * 512, hf * 1024:(hf + 1) * 1024].rearrange("(j p) n -> p j n", p=128),
                          reads=WDN_TK[fgd * 4:(fgd + 1) * 4], writes=[w])
                    for j in range(4):
                        fc = fgd * 4 + j
                        for tt in range(4):
                            for n in range(2):
                                bk = tt * 2 + n
                                last = (fc == 63)
                                P.op("pe", MM(ps[:, bk, :], aT.ap[:, fc, tt * 128:(tt + 1) * 128], w.ap[:, j, n * 512:(n + 1) * 512], fc == 0, last),
                                     reads=[AT_TK[fc], w], writes=[PSB[bk]], signal=(last or (j == 3 and tt == 3 and n == 1)))
                for tt in range(4):
                    ms = mst[k_ms % 2]
                    k_ms += 1
                    src = ps[:, tt * 2:tt * 2 + 2, :]
                    dst = ms.ap.rearrange("p (a b) -> p a b", a=2)
                    fn = ACT(dst, src, AF.Copy) if tt % 2 == 0 else CP(dst, src)
                    P.op("act" if tt % 2 == 0 else "dve", fn, reads=[PSB[tt * 2], PSB[tt * 2 + 1]], writes=[ms])
                    t = sbk * 4 + tt
                    P.dma("sp", m_s[t * 128:(t + 1) * 128, hf * 1024:(hf + 1) * 1024], ms.ap, reads=[ms], writes=[M_TK[sbk]], add=True)

    def phase7():
        reset(BASE)
        wg = sb([16, D], BF16)
        wp = sb([2, D], BF16)
        g3 = sb([D], F32)
        g4 = sb([D], F32)
        g5 = sb([D], F32)
        mt = [sb([D], F32) for _ in range(2)]
        h1 = [sb([D], F32) for _ in range(2)]
        pt = [sb([2, 128], BF16) for _ in range(2)]
        tmp = sb([D], F32)
        xg = sb([D], BF16)
        junk = sb([D], BF16)
        hnT = [sb([16, 128], BF16) for _ in range(2)]
        sg = [sb([1024], F32) for _ in range(2)]
        et = sb([D], F32)
        yt = [sb([D], F32) for _ in range(2)]
        ssA = [sb([1], F32) for _ in range(2)]
        rA = [sb([1], F32) for _ in range(2)]
        ssB = [sb([1], F32) for _ in range(2)]
        rB = [sb([1], F32) for _ in range(2)]
        ssC = [sb([1], F32) for _ in range(2)]
        rC = [sb([1], F32) for _ in range(2)]
        for n in range(4):
            P.dma("pool", wg.ap[:, :, n * 512:(n + 1) * 512], w_g[:, n * 512:(n + 1) * 512].rearrange("(c p) n -> p c n", p=128),
                  writes=[wg], add=(n > 0))
        P.dma("pool", wp.ap, w_p.rearrange("(c p) n -> p c n", p=128), writes=[wp])
        P.dma("sp", g3.ap, gains[3], writes=[g3])
        P.dma("sp", g4.ap, gains[4], writes=[g4])
        P.dma("sp", g5.ap, gains[5], writes=[g5])
        Y_TK = []

        def B1(t):
            b = t % 2
            P.dma("sp", mt[b].ap, m_s[t * 128:(t + 1) * 128, :], reads=[M_TK[t // 4]], writes=[mt[b]])
            P.dma("sp", h1[b].ap, h1_s[t * 128:(t + 1) * 128, :], reads=[H1_TK[t]], writes=[h1[b]])
            P.dma("pool", pt[b].ap, pT_in[:, t * 128:(t + 1) * 128].rearrange("(c p) t -> p c t", p=128), writes=[pt[b]])
            P.op("act", ACT(junk.ap, mt[b].ap, AF.Square, accum_out=ssA[b].ap[:, 0:1]), reads=[mt[b]], writes=[ssA[b]])
            rstd_ops(ssA[b], rA[b], D)
            P.op("dve", STT(tmp.ap, mt[b].ap, rA[b].ap[:, 0:1], g3.ap, ALU.mult, ALU.mult), reads=[mt[b], rA[b], g3], writes=[tmp])
            P.op("dve", TT(h1[b].ap, tmp.ap, h1[b].ap, ALU.add), reads=[tmp, h1[b]], writes=[h1[b]])
            norm_to_bf16(h1[b], g4, junk, ssB[b], rB[b], xg)

        def C(t):
            b = t % 2
            transpose16(xg, lambda c0, c1, b=b: hnT[b].ap[:, c0:c1, :], [hnT[b]], banks=(6, 7))

        def A(t):
            b = t % 2
            for hf in range(2):
                gb = hf * 2
                for n in range(2):
                    col = hf * 1024 + n * 512
                    for c in range(16):
                        P.op("pe", MM(ps[:, gb + n, :], hnT[b].ap[:, c, :], wg.ap[:, c, col:col + 512], c == 0, c == 15),
                             reads=[hnT[b], wg], writes=[PSB[gb + n]], signal=(c == 15))
                for n in range(2):
                    col = hf * 1024 + n * 512
                    for c in range(2):
                        P.op("pe", MM(ps[:, 4 + n, :], pt[b].ap[:, c, :], wp.ap[:, c, col:col + 512], c == 0, c == 1),
                             reads=[pt[b], wp], writes=[PSB[4 + n]], signal=(c == 1))
                sgt = sg[hf]
                P.op("act", ACT(sgt.ap.rearrange("p (a b) -> p a b", a=2), ps[:, gb:gb + 2, :], AF.Sigmoid),
                     reads=[PSB[gb], PSB[gb + 1]], writes=[sgt])
                P.op("dve", TT(et.ap[:, hf * 1024:(hf + 1) * 1024].rearrange("p (a b) -> p a b", a=2), ps[:, 4:6, :],
                               sgt.ap.rearrange("p (a b) -> p a b", a=2), ALU.mult),
                     reads=[PSB[4], PSB[5], sgt], writes=[et], add=(hf == 1))

        def B2(t):
            b = t % 2
            P.op("act", ACT(junk.ap, et.ap, AF.Square, accum_out=ssC[b].ap[:, 0:1]), reads=[et], writes=[ssC[b]])
            rstd_ops(ssC[b], rC[b], D)
            P.op("dve", STT(tmp.ap, et.ap, rC[b].ap[:, 0:1], g5.ap, ALU.mult, ALU.mult), reads=[et, rC[b], g5], writes=[tmp])
            P.op("dve", TT(yt[b].ap, tmp.ap, h1[b].ap, ALU.add), reads=[tmp, h1[b]], writes=[yt[b]])
            ytk = Tk()
            P.dma("sp", y_out[t * 128:(t + 1) * 128, :], yt[b].ap, reads=[yt[b]], writes=[ytk])
            Y_TK.append(ytk)

        B1(0)
        C(0)
        for t in range(NT):
            if t + 1 < NT:
                B1(t + 1)
            A(t)
            if t + 1 < NT:
                C(t + 1)
            B2(t)
        return Y_TK

    KA_TK, VA_TK = phase1()
    P.barrier()
    OUT_TK = phase2()
    P.barrier()
    phase3(KA_TK, VA_TK, OUT_TK["qa"])
    P.barrier()
    phase4(OUT_TK)
    P.barrier()
    phase5()
    P.barrier()
    phase6()
    P.barrier()
    Y_TK = phase7()
    P.barrier()
    P.op("sp", lambda e: e.nop(), signal=False)
    for e in ("pe", "act", "dve", "pool"):
        P.bar[e] = []

    sem_names = ["pe", "act", "dve", "pool"] + ["d_sp_%d" % i for i in range(P.ring["sp"])] + ["d_pool_%d" % i for i in range(P.ring["pool"])]
    sem_cms = [nc.semaphore(n) for n in sem_names]
    sems = {n: cm.__enter__() for n, cm in zip(sem_names, sem_cms)}

    ref = {k: set() for k in ("pe", "act", "dve", "pool")}
    for name in ENGS:
        for ws, fn, sig in P.q[name]:
            for k, v in ws:
                if k in ref:
                    ref[k].add(v)
    remap = {k: {v: i + 1 for i, v in enumerate(sorted(vs))} for k, vs in ref.items()}

    def replay(name, e):
        c = 0
        for ws, fn, sig in P.q[name]:
            for k, v in ws:
                e.wait_ge(sems[k], remap[k][v] if k in remap else v)
            ins = fn(e)
            if sig is not None:
                if sig[0] in remap:
                    c += 1
                    if c in ref[sig[0]]:
                        ins.then_inc(sems[sig[0]], 1)
                else:
                    ins.then_inc(sems[sig[0]], sig[1])

    with nc.Block() as block:
        @block.tensor
        def _(e):
            replay("pe", e)

        @block.scalar
        def _(e):
            replay("act", e)

        @block.vector
        def _(e):
            replay("dve", e)

        @block.gpsimd
        def _(e):
            replay("pool", e)

        @block.sync
        def _(e):
            replay("sp", e)

    for cm in reversed(sem_cms):
        cm.__exit__(None, None, None)
    psum_cm.__exit__(None, None, None)
    arena_cm.__exit__(None, None, None)
    return nc


DEBUG_OUTS = ()


def _rope_tables():
    t = np.arange(S)
    row = (t // 64).astype(np.float32)
    col = (t % 64).astype(np.float32)
    freqs = (np.float32(10000.0) ** (-np.arange(32, dtype=np.float32) / np.float32(32))).astype(np.float32)
    ar = row[:, None] * freqs[None, :]
    ac = col[:, None] * freqs[None, :]
    cos = np.concatenate([np.cos(ar), np.cos(ar), np.cos(ac), np.cos(ac)], axis=1).astype(np.float32)
    sin = np.concatenate([-np.sin(ar), np.sin(ar), -np.sin(ac), np.sin(ac)], axis=1).astype(np.float32)
    return cos, sin


def _bias_tiles(rpb, core):
    out = np.full((3, 8, 8 * 128, 512), NEG, dtype=np.float32)
    for typ, qb in ((0, 0), (1, 1), (2, 3)):
        R = 32 * core + 8 * qb
        q = np.arange(512)
        r = R + q // 64
        c = q % 64
        r0 = np.clip(r - 4, 0, 256 - 8)
        c0 = np.clip(c - 8, 0, 64 - 16)
        k = np.arange(1024)
        kr = (R - 4) + k // 64
        kc = k % 64
        inr = (kr[:, None] >= r0[None, :]) & (kr[:, None] < r0[None, :] + 8)
        inc = (kc[:, None] >= c0[None, :]) & (kc[:, None] < c0[None, :] + 16)
        ok = inr & inc
        dr = np.clip(kr[:, None] - r[None, :] + 7, 0, 14)
        dc = np.clip(kc[:, None] - c[None, :] + 15, 0, 30)
        for h in range(8):
            g = rpb[h][dr, dc]
            out[typ, h] = np.where(ok, g, np.float32(NEG))
    return np.ascontiguousarray(out.reshape(3, 8, 8, 128, 512).transpose(0, 1, 3, 2, 4))


def _prepare_inputs(x, p, pre_mix_norm, w_in, q_norm, k_norm, rel_pos_bias, w_o, post_mix_norm, pre_mlp_norm,
                    w_up, w_down, post_mlp_norm, pre_ple_norm, w_ple_gate, w_ple_proj, post_ple_norm):
    f = lambda a: np.ascontiguousarray(np.asarray(a, dtype=np.float32))
    x2 = f(x)[0]
    p2 = f(p)[0, 0]
    gains = np.stack([np.broadcast_to(f(g)[0][None, :], (128, D)) for g in
                      (pre_mix_norm, post_mix_norm, pre_mlp_norm, post_mlp_norm, pre_ple_norm, post_ple_norm)])
    gains = np.ascontiguousarray(gains)
    gainsT = np.ascontiguousarray(np.stack([f(g)[0].reshape(16, 128).T for g in
                                            (pre_mix_norm, post_mix_norm, pre_mlp_norm, post_mlp_norm, pre_ple_norm, post_ple_norm)]))
    gq = np.ascontiguousarray(np.broadcast_to(np.tile(f(q_norm)[0], 4)[None, :], (128, 512)))
    gk = np.ascontiguousarray(np.broadcast_to(np.tile(f(k_norm)[0], 2)[None, :], (128, 256)))
    cos, sin = _rope_tables()
    rpb = f(rel_pos_bias)[0]
    ident = np.eye(128, dtype=np.float32)
    shared = {
        "x_all": x2, "w_in": f(w_in)[0], "w_o": f(w_o)[0], "w_up": f(w_up)[0], "w_down": f(w_down)[0],
        "w_gate": f(w_ple_gate)[0], "w_proj": f(w_ple_proj)[0], "gains": gains, "gainsT": gainsT, "gq": gq, "gk": gk,
        "cos_all": cos, "sin_all": sin, "ident": ident,
    }
    in_maps = []
    xpad = np.concatenate([np.zeros((256, D), np.float32), x2, np.zeros((256, D), np.float32)], axis=0)
    for c in range(NCORES):
        m = dict(shared)
        m["x_ext"] = np.ascontiguousarray(xpad[c * TOK:c * TOK + ETOK])
        m["pT"] = np.ascontiguousarray(p2[c * TOK:(c + 1) * TOK].T)
        m["cos_own"] = np.ascontiguousarray(cos[c * TOK:(c + 1) * TOK])
        m["sin_own"] = np.ascontiguousarray(sin[c * TOK:(c + 1) * TOK])
        m["biasT"] = _bias_tiles(rpb, c)
        in_maps.append(m)
    return in_maps


_NC_CACHE = {}


def kernel(**inputs):
    in_maps = _prepare_inputs(**inputs)
    if "nc" not in _NC_CACHE:
        _NC_CACHE["nc"] = build_program()
    nc = _NC_CACHE["nc"]
    res = run_bass_kernel_spmd(nc, in_maps, core_ids=list(range(NCORES)))
    out = np.concatenate([np.asarray(r["y"], dtype=np.float32) for r in res.results], axis=0)
    if DEBUG:
        kernel.last_results = res.results
    return out.reshape(1, S, D)
```

```python
import numpy as np
import concourse.bass as bass
import concourse.mybir as mybir
from concourse.bass_utils import run_bass_kernel_spmd

F32 = mybir.dt.float32
BF16 = mybir.dt.bfloat16
AF = mybir.ActivationFunctionType
ALU = mybir.AluOpType

NCORES = 8
S = 16384
D = 2048
TOK = 2048
NT = 16
NE = 20
ETOK = NE * 128
DFF = 8192
EPS = 1e-6
SCALE = 1.0 / np.sqrt(128.0)
NEG = -30000.0
DEBUG = False


class Tk:
    __slots__ = ("w", "r")

    def __init__(self):
        self.w = {}
        self.r = {}


class Tile:
    __slots__ = ("ap", "tk")

    def __init__(self, ap):
        self.ap = ap
        self.tk = Tk()


def _tk(x):
    return x.tk if isinstance(x, Tile) else x


ENGS = ("pe", "act", "dve", "pool", "sp")


class Prog:
    def __init__(self):
        self.q = {e: [] for e in ENGS}
        self.cnt = {e: 0 for e in ENGS}
        self.waited = {e: {} for e in ENGS}
        self.ring = {"sp": 28, "pool": 12}
        self.dma_i = {"sp": 0, "pool": 0}
        self.dma_last = {}
        self.bar = {e: [] for e in ENGS}

    def _collect(self, eng, reads, writes, extra, add):
        toks = list(extra) + self.bar[eng]
        self.bar[eng] = []
        for t in reads:
            toks.extend(_tk(t).w.items())
        for t in writes:
            t = _tk(t)
            if not add:
                toks.extend(t.w.items())
            toks.extend(t.r.items())
        out = []
        wd = self.waited[eng]
        for k, v in toks:
            if k == "pe" and eng == "pe":
                continue
            if wd.get(k, 0) >= v:
                continue
            wd[k] = v
            out.append((k, v))
        return out

    def _mark(self, tok, reads, writes, add):
        k, v = tok
        for t in reads:
            t = _tk(t)
            if t.r.get(k, 0) < v:
                t.r[k] = v
        for t in writes:
            t = _tk(t)
            if add:
                if t.w.get(k, 0) < v:
                    t.w[k] = v
            else:
                t.w = {k: v}
            t.r = {}

    def op(self, eng, fn, reads=(), writes=(), signal=True, extra=(), add=False):
        ws = self._collect(eng, reads, writes, extra, add)
        if signal:
            self.cnt[eng] += 1
            tok = (eng, self.cnt[eng])
            sig = (eng, 1)
        else:
            tok = (eng, self.cnt[eng] + 1)
            sig = None
        self.q[eng].append((ws, fn, sig))
        self._mark(tok, reads, writes, add)
        return tok

    def dma(self, queue, out_ap, in_ap, reads=(), writes=(), add=False, extra=()):
        i = self.dma_i[queue]
        self.dma_i[queue] += 1
        n = self.ring[queue]
        key = "d_%s_%d" % (queue, i % n)
        val = 16 * (i // n + 1)
        extra = list(extra) + ([(key, val - 16)] if i >= n else [])
        ws = self._collect(queue, reads, writes, extra, add)
        self.q[queue].append((ws, (lambda e, o=out_ap, a=in_ap: e.dma_start(out=o, in_=a)), (key, 16)))
        tok = (key, val)
        self.dma_last[key] = val
        self._mark(tok, reads, writes, add)
        return tok

    def barrier(self):
        toks = [(e, self.cnt[e]) for e in ("pe", "act", "dve", "pool") if self.cnt[e] > 0]
        toks += list(self.dma_last.items())
        for e in ENGS:
            self.bar[e] = self.bar[e] + toks


def MM(out, lhsT, rhs, start, stop):
    return lambda e: e.matmul(out, lhsT, rhs, start=start, stop=stop)


def TR(out, in_, ident):
    return lambda e: e.transpose(out, in_, ident)


def ACT(out, in_, func, **kw):
    return lambda e: e.activation(out=out, in_=in_, func=func, **kw)


def TT(out, in0, in1, op):
    return lambda e: e.tensor_tensor(out=out, in0=in0, in1=in1, op=op)


def TS(out, in0, s1, s2, op0, op1=None):
    if op1 is None:
        return lambda e: e.tensor_scalar(out=out, in0=in0, scalar1=s1, scalar2=None, op0=op0)
    return lambda e: e.tensor_scalar(out=out, in0=in0, scalar1=s1, scalar2=s2, op0=op0, op1=op1)


def STT(out, in0, scalar, in1, op0, op1):
    return lambda e: e.scalar_tensor_tensor(out=out, in0=in0, scalar=scalar, in1=in1, op0=op0, op1=op1)


def CP(out, in_):
    return lambda e: e.tensor_copy(out, in_)


def RCP(out, in_):
    return lambda e: e.reciprocal(out=out, in_=in_)


def MSET(ap, v):
    return lambda e: e.memset(ap, v)


def build_program():
    nc = bass.Bass("TRN2", target_bir_lowering=False)
    P = Prog()

    def din(name, shape, dt=F32):
        return nc.dram_tensor(name, list(shape), dt, kind="ExternalInput").ap()

    def dscr(name, shape, dt=BF16):
        kind = "ExternalOutput" if (DEBUG and name in DEBUG_OUTS) else "Internal"
        return nc.dram_tensor(name, list(shape), dt, kind=kind).ap()

    x_all = din("x_all", [S, D])
    x_ext = din("x_ext", [ETOK, D])
    pT_in = din("pT", [256, TOK])
    w_in = din("w_in", [D, 4608])
    w_o = din("w_o", [D, D])
    w_up = din("w_up", [D, DFF])
    w_dn = din("w_down", [DFF, D])
    w_g = din("w_gate", [D, D])
    w_p = din("w_proj", [256, D])
    gains = din("gains", [6, 128, D])
    gainsT = din("gainsT", [6, 128, 16])
    gq_in = din("gq", [128, 512])
    gk_in = din("gk", [128, 256])
    cos_all = din("cos_all", [S, 128])
    sin_all = din("sin_all", [S, 128])
    cos_own = din("cos_own", [TOK, 128])
    sin_own = din("sin_own", [TOK, 128])
    biasT = din("biasT", [3, 8, 128, 8, 512])
    ident_in = din("ident", [128, 128])
    y_out = nc.dram_tensor("y", [TOK, D], F32, kind="ExternalOutput").ap()

    kaT_s = dscr("kaT_s", [2, 128, S])
    va_s = dscr("va_s", [128, 2, 128, 128])
    qaT_s = dscr("qaT_s", [8, 128, TOK])
    qbT_s = dscr("qbT_s", [8, 128, TOK])
    kbT_s = dscr("kbT_s", [8, 128, ETOK])
    vb_s = dscr("vb_s", [128, 8, NE, 128])
    mixT_s = dscr("mixT_s", [16, 128, TOK])
    h1_s = dscr("h1_s", [TOK, D], F32)
    h1nT_s = dscr("h1nT_s", [4, 128, 16, 512])
    wup_s = dscr("wup_s", [D, DFF])
    wdn_s = dscr("wdn_s", [DFF, D])
    m_s = dscr("m_s", [TOK, D], F32)

    ARENA_W = 48 * 1024
    arena_cm = nc.sbuf_tensor("arena", [128, ARENA_W], F32)
    psum_cm = nc.psum_tensor("ps", [128, 8, 512], F32)
    arena = arena_cm.__enter__()
    ps = psum_cm.__enter__()
    psb = ps[:].bitcast(BF16)
    PSB = [Tk() for _ in range(8)]

    st = {"off": 0}

    def reset(off=0):
        st["off"] = off

    def sb(shape, dt):
        n = int(np.prod(shape))
        nb = n * (2 if dt == BF16 else 4)
        nb = (nb + 63) // 64 * 64
        off = st["off"]
        st["off"] = off + nb
        assert st["off"] <= ARENA_W * 4, "sbuf arena overflow %d" % st["off"]
        v = arena[:, off // 4:(off + nb) // 4]
        if dt == BF16:
            v = v.bitcast(BF16)
        v = v[:, 0:n]
        if len(shape) == 2:
            v = v.rearrange("p (a b) -> p a b", a=shape[0])
        elif len(shape) == 3:
            v = v.rearrange("p (a b c) -> p a b c", a=shape[0], b=shape[1])
        return Tile(v)

    ident = sb([128], BF16)
    ones = sb([128], F32)
    onesb = sb([128], BF16)
    P.dma("pool", ident.ap, ident_in[:, :], writes=[ident])
    P.op("pool", MSET(ones.ap, 1.0), writes=[ones])
    P.op("pool", MSET(onesb.ap, 1.0), writes=[onesb])
    BASE = st["off"]

    WUP_TK = [Tk() for _ in range(16)]
    WDN_TK = [Tk() for _ in range(64)]

    cast_list = [(wup_s[c * 128:(c + 1) * 128, :], w_up[c * 128:(c + 1) * 128, :], WUP_TK[c]) for c in range(16)] + \
                [(wdn_s[c * 128:(c + 1) * 128, :], w_dn[c * 128:(c + 1) * 128, :], WDN_TK[c]) for c in range(64)]

    def issue_weight_casts(n, extra=()):
        for _ in range(n):
            if cast_list:
                o, i_, tk = cast_list.pop(0)
                P.dma("pool", o, i_, writes=[tk], extra=extra)

    def rstd_ops(ss, r, n):
        P.op("dve", TS(r.ap, ss.ap, 1.0 / n, EPS, ALU.mult, ALU.add), reads=[ss], writes=[r])
        P.op("act", ACT(r.ap, r.ap, AF.Sqrt), reads=[r], writes=[r])
        P.op("dve", RCP(r.ap, r.ap), reads=[r], writes=[r])

    def norm_to_bf16(src, gain, junk, ss, r, xg, mode="stt"):
        P.op("act", ACT(junk.ap, src.ap, AF.Square, accum_out=ss.ap[:, 0:1]), reads=[src], writes=[ss])
        rstd_ops(ss, r, D)
        if mode == "fold":
            P.op("dve", TS(xg.ap, src.ap, r.ap[:, 0:1], None, ALU.mult), reads=[src, r], writes=[xg])
        else:
            P.op("dve", STT(xg.ap, src.ap, r.ap[:, 0:1], gain.ap, ALU.mult, ALU.mult), reads=[src, r, gain], writes=[xg])

    def fold_gain(wt, gT, nchunks=16):
        for c in range(nchunks):
            P.op("dve", TS(wt.ap[:, c, :], wt.ap[:, c, :], gT.ap[:, c:c + 1], None, ALU.mult), reads=[wt, gT], writes=[wt])

    def transpose16(xg, dst_fn, dst_tiles, banks=(6, 7)):
        for half in range(2):
            bk = banks[half]
            for j in range(8):
                c = half * 8 + j
                P.op("pe", TR(psb[:, bk, j * 128:(j + 1) * 128], xg.ap[:, c * 128:(c + 1) * 128], ident.ap),
                     reads=[xg, ident], writes=[PSB[bk]], signal=(j == 7))
            eng = "act" if half == 0 else "dve"
            src = psb[:, bk, :].rearrange("p (c t) -> p c t", c=8)
            fn = ACT(dst_fn(half * 8, half * 8 + 8), src, AF.Copy) if eng == "act" else CP(dst_fn(half * 8, half * 8 + 8), src)
            P.op(eng, fn, reads=[PSB[bk]], writes=dst_tiles)

    def norm_rope(src_ap, src_tk, nh, gain, cs, sn, kg, t1, t2, hss, hr, junk, outb, rs=None):
        W = nh * 128
        rr = [rs] if rs is not None else []
        for h in range(nh):
            kw = {"scale": rs.ap[:, 0:1]} if rs is not None else {}
            P.op("act", ACT(junk.ap[:, 0:128], src_ap[:, h * 128:(h + 1) * 128], AF.Square, accum_out=hss.ap[:, h:h + 1], **kw),
                 reads=[src_tk] + rr, writes=[hss], add=True)
        rstd_ops(hss, hr, 128)
        if rs is not None:
            P.op("dve", STT(kg.ap[:, 0:W], src_ap, rs.ap[:, 0:1], gain.ap[:, 0:W], ALU.mult, ALU.mult), reads=[src_tk, gain, rs], writes=[kg])
        else:
            P.op("dve", TT(kg.ap[:, 0:W], src_ap, gain.ap[:, 0:W], ALU.mult), reads=[src_tk, gain], writes=[kg])
        kg3 = kg.ap[:, 0:W].rearrange("p (h d) -> p h d", h=nh)
        t13 = t1.ap[:, 0:W].rearrange("p (h d) -> p h d", h=nh)
        P.op("dve", TT(t13, kg3, cs.ap.unsqueeze(1).broadcast_to([128, nh, 128]), ALU.mult), reads=[kg, cs], writes=[t1])
        kg5 = kg.ap[:, 0:W].rearrange("p (h s f i) -> p h s f i", h=nh, s=2, f=2)
        t25 = t2.ap[:, 0:W].rearrange("p (h s f i) -> p h s f i", h=nh, s=2, f=2)
        sn4 = sn.ap.rearrange("p (s f i) -> p s f i", s=2, f=2)
        for f in range(2):
            P.op("dve", TT(t25[:, :, :, f, :], kg5[:, :, :, 1 - f, :],
                           sn4[:, :, f, :].unsqueeze(1).broadcast_to([128, nh, 2, 32]), ALU.mult),
                 reads=[kg, sn], writes=[t2], add=(f == 1))
        P.op("dve", TT(t1.ap[:, 0:W], t1.ap[:, 0:W], t2.ap[:, 0:W], ALU.add), reads=[t1, t2], writes=[t1])
        for h in range(nh):
            P.op("act", ACT(outb.ap[:, h, :], t1.ap[:, h * 128:(h + 1) * 128], AF.Copy, scale=hr.ap[:, h:h + 1]),
                 reads=[t1, hr], writes=[outb], add=(h > 0))

    def phase1():
        reset(BASE)
        wkv = sb([16, 512], BF16)
        gT = sb([16], F32)
        gk = sb([256], F32)
        xt = [sb([D], F32) for _ in range(3)]
        xg = [sb([D], BF16) for _ in range(3)]
        xnT = [sb([16, 128], BF16) for _ in range(2)]
        junk = sb([D], BF16)
        cs = [sb([128], F32) for _ in range(3)]
        sn = [sb([128], F32) for _ in range(3)]
        ss = [sb([1], F32) for _ in range(3)]
        r = [sb([1], F32) for _ in range(3)]
        hss = [sb([2], F32) for _ in range(2)]
        hr = [sb([2], F32) for _ in range(2)]
        kg = sb([256], F32)
        t1 = sb([256], F32)
        t2 = sb([256], F32)
        kb = [sb([2, 128], BF16) for _ in range(2)]
        kst = [sb([2, 512], BF16) for _ in range(2)]
        vst = [sb([2, 4, 128], BF16) for _ in range(2)]
        KA_TK = [Tk() for _ in range(32)]
        VA_TK = [Tk() for _ in range(32)]

        P.dma("pool", wkv.ap, w_in[:, 1024:1536].rearrange("(c p) n -> p c n", p=128), writes=[wkv])
        P.dma("sp", gT.ap, gainsT[0], writes=[gT])
        P.dma("sp", gk.ap, gk_in[:, :], writes=[gk])
        fold_gain(wkv, gT)

        def front_a(i):
            b3 = i % 3
            P.dma("sp", xt[b3].ap, x_all[i * 128:(i + 1) * 128, :], writes=[xt[b3]])
            P.dma("sp", cs[b3].ap, cos_all[i * 128:(i + 1) * 128, :], writes=[cs[b3]])
            P.dma("sp", sn[b3].ap, sin_all[i * 128:(i + 1) * 128, :], writes=[sn[b3]])
            P.op("dve", CP(xg[b3].ap, xt[b3].ap), reads=[xt[b3]], writes=[xg[b3]])
            P.op("act", ACT(junk.ap, xt[b3].ap, AF.Square, accum_out=ss[b3].ap[:, 0:1]), reads=[xt[b3]], writes=[ss[b3]])
            rstd_ops(ss[b3], r[b3], D)

        def front_b(i):
            b = i % 2
            b3 = i % 3
            transpose16(xg[b3], lambda c0, c1, b=b: xnT[b].ap[:, c0:c1, :], [xnT[b]], banks=(6, 7))
            bank = i % 2
            for c in range(16):
                P.op("pe", MM(ps[:, bank, :], xnT[b].ap[:, c, :], wkv.ap[:, c, :], c == 0, c == 15),
                     reads=[xnT[b], wkv], writes=[PSB[bank]], signal=(c == 15))

        def back(i):
            b = i % 2
            b3 = i % 3
            g4 = i // 4
            gb = g4 % 2
            bank = i % 2
            P.op("act", ACT(vst[gb].ap[:, :, i % 4, :], ps[:, bank, 256:512].rearrange("p (k d) -> p k d", k=2), AF.Copy, scale=r[b3].ap[:, 0:1]),
                 reads=[PSB[bank], r[b3]], writes=[vst[gb]], add=(i % 4 != 0))
            norm_rope(ps[:, bank, 0:256], PSB[bank], 2, gk, cs[b3], sn[b3], kg, t1, t2, hss[b], hr[b], junk, kb[b], rs=r[b3])

        def back_pe(i):
            b = i % 2
            g4 = i // 4
            gb = g4 % 2
            tb = 4 + (i % 2)
            for h in range(2):
                P.op("pe", TR(psb[:, tb, h * 128:(h + 1) * 128], kb[b].ap[:, h, :], ident.ap),
                     reads=[kb[b], ident], writes=[PSB[tb]], signal=(h == 1))
            P.op("dve", CP(kst[gb].ap[:, :, (i % 4) * 128:(i % 4 + 1) * 128], psb[:, tb, 0:256].rearrange("p (k t) -> p k t", k=2)),
                 reads=[PSB[tb]], writes=[kst[gb]], add=(i % 4 != 0))
            if i % 4 == 3:
                P.dma("sp", kaT_s[:, :, g4 * 512:(g4 + 1) * 512].rearrange("k d t -> d k t"), kst[gb].ap,
                      reads=[kst[gb]], writes=[KA_TK[g4]])
                P.dma("sp", va_s[:, :, g4 * 4:(g4 + 1) * 4, :], vst[gb].ap, reads=[vst[gb]], writes=[VA_TK[g4]])

        front_a(0)
        front_a(1)
        front_b(0)
        for i in range(128):
            if i + 1 < 128:
                front_b(i + 1)
            if i + 2 < 128:
                front_a(i + 2)
            back(i)
            if i >= 1:
                back_pe(i - 1)
        back_pe(127)
        return KA_TK, VA_TK

    def phase2():
        reset(BASE)
        xnT = sb([16, ETOK], BF16)
        XN_TK = [Tk() for _ in range(NE)]
        gT = sb([16], F32)
        gq = sb([512], F32)
        mark = st["off"]
        xt = [sb([D], F32) for _ in range(2)]
        xg = [sb([D], BF16) for _ in range(2)]
        junk = sb([D], BF16)
        ss = [sb([1], F32) for _ in range(2)]
        r = [sb([1], F32) for _ in range(2)]

        P.dma("sp", gT.ap, gainsT[0], writes=[gT])
        P.dma("sp", gq.ap, gq_in[:, :], writes=[gq])
        for e in range(NE):
            b = e % 2
            P.dma("sp", xt[b].ap, x_ext[e * 128:(e + 1) * 128, :], writes=[xt[b]])
            norm_to_bf16(xt[b], None, junk, ss[b], r[b], xg[b], mode="fold")
            transpose16(xg[b], lambda c0, c1, e=e: xnT.ap[:, c0:c1, e * 128:(e + 1) * 128], [XN_TK[e]],
                        banks=(6, 7) if e % 2 == 0 else (4, 5))
        P.barrier()
        reset(mark)
        wblk = [sb([16, 512], BF16) for _ in range(2)]
        stage = [sb([4 * ETOK], BF16) for _ in range(2)]
        junk = sb([D], BF16)
        cs = [sb([128], F32) for _ in range(2)]
        sn = [sb([128], F32) for _ in range(2)]
        hss = [sb([4], F32) for _ in range(2)]
        hr = [sb([4], F32) for _ in range(2)]
        kg = sb([512], F32)
        t1 = sb([512], F32)
        t2 = sb([512], F32)
        qb_ = [sb([4, 128], BF16) for _ in range(2)]

        blocks = [("qa", 0, 0), ("qa", 1, 512), ("qb", 0, 1536), ("qb", 1, 2048),
                  ("kb", 0, 2560), ("kb", 1, 3072), ("vb", 0, 3584), ("vb", 1, 4096)]
        bankrot = [0]

        def nbank():
            bk = bankrot[0] % 4
            bankrot[0] += 1
            return bk

        OUT_TK = {"qa": [Tk(), Tk()], "qb": [Tk(), Tk()], "kb": [Tk(), Tk()], "vb": [Tk(), Tk()]}
        pend = [None]

        def load_block(bi_):
            wb_ = wblk[bi_ % 2]
            col_ = blocks[bi_][2]
            P.dma("pool", wb_.ap, w_in[:, col_:col_ + 512].rearrange("(c p) n -> p c n", p=128), writes=[wb_])
            pend[0] = wb_

        def maybe_fold():
            if pend[0] is not None:
                fold_gain(pend[0], gT)
                pend[0] = None

        load_block(0)
        maybe_fold()
        for bi, (kind, half, col) in enumerate(blocks):
            wb = wblk[bi % 2]
            sg = stage[bi % 2]
            if bi + 1 < len(blocks):
                load_block(bi + 1)
            if kind == "qa":
                sgv = sg.ap[:, 0:4 * TOK].rearrange("p (h t) -> p h t", h=4)
                def qa_mm(t, wb=wb):
                    e = t + 2
                    b = t % 2
                    P.dma("sp", cs[b].ap, cos_own[t * 128:(t + 1) * 128, :], writes=[cs[b]])
                    P.dma("sp", sn[b].ap, sin_own[t * 128:(t + 1) * 128, :], writes=[sn[b]])
                    bk = nbank()
                    for c in range(16):
                        P.op("pe", MM(ps[:, bk, :], xnT.ap[:, c, e * 128:(e + 1) * 128], wb.ap[:, c, :], c == 0, c == 15),
                             reads=[XN_TK[e], wb], writes=[PSB[bk]], signal=(c == 15))
                    return bk

                def qa_rest(t, bk, sg=sg, sgv=sgv):
                    b = t % 2
                    norm_rope(ps[:, bk, :], PSB[bk], 4, gq, cs[b], sn[b], kg, t1, t2, hss[b], hr[b], junk, qb_[b])
                    tb = 4 + (t % 2)
                    for h in range(4):
                        P.op("pe", TR(psb[:, tb, h * 128:(h + 1) * 128], qb_[b].ap[:, h, :], ident.ap),
                             reads=[qb_[b], ident], writes=[PSB[tb]], signal=(h == 3))
                    P.op("dve", CP(sgv[:, :, t * 128:(t + 1) * 128], psb[:, tb, 0:512].rearrange("p (h t) -> p h t", h=4)),
                         reads=[PSB[tb]], writes=[sg], add=(t > 0))

                bks = {0: qa_mm(0)}
                for t in range(NT):
                    if t == 5:
                        maybe_fold()
                    if t + 1 < NT:
                        bks[t + 1] = qa_mm(t + 1)
                    qa_rest(t, bks[t])
                P.dma("sp", qaT_s[half * 4:(half + 1) * 4].rearrange("h d t -> d h t"), sgv, reads=[sg], writes=[OUT_TK[kind][half]])
            elif kind in ("qb", "kb"):
                ntok = TOK if kind == "qb" else ETOK
                e0 = 2 if kind == "qb" else 0
                sgv = sg.ap[:, 0:4 * ntok].rearrange("p (h t) -> p h t", h=4)
                first = True
                for j in range(4):
                    if j == 1:
                        maybe_fold()
                    for sbk in range(ntok // 512):
                        tk0 = e0 * 128 + sbk * 512
                        bk = nbank()
                        for c in range(16):
                            P.op("pe", MM(ps[:, bk, :], wb.ap[:, c, j * 128:(j + 1) * 128], xnT.ap[:, c, tk0:tk0 + 512], c == 0, c == 15),
                                 reads=[wb] + XN_TK[tk0 // 128:tk0 // 128 + 4], writes=[PSB[bk]], signal=(c == 15))
                        eng = "act" if (sbk % 2 == 0) else "dve"
                        dst = sgv[:, j, sbk * 512:(sbk + 1) * 512]
                        fn = ACT(dst, ps[:, bk, :], AF.Copy) if eng == "act" else CP(dst, ps[:, bk, :])
                        P.op(eng, fn, reads=[PSB[bk]], writes=[sg], add=(not first))
                        first = False
                dst_s = qbT_s if kind == "qb" else kbT_s
                P.dma("sp", dst_s[half * 4:(half + 1) * 4].rearrange("h d t -> d h t"), sgv, reads=[sg], writes=[OUT_TK[kind][half]])
            else:
                sgv = sg.ap[:, 0:NE * 512].rearrange("p (h e d) -> p h e d", h=4, e=NE)
                for e in range(NE):
                    if e == 6:
                        maybe_fold()
                    bk = nbank()
                    for c in range(16):
                        P.op("pe", MM(ps[:, bk, :], xnT.ap[:, c, e * 128:(e + 1) * 128], wb.ap[:, c, :], c == 0, c == 15),
                             reads=[XN_TK[e], wb], writes=[PSB[bk]], signal=(c == 15))
                    eng = "act" if (e % 2 == 0) else "dve"
                    dst = sgv[:, :, e, :]
                    src = ps[:, bk, :].rearrange("p (h d) -> p h d", h=4)
                    fn = ACT(dst, src, AF.Copy) if eng == "act" else CP(dst, src)
                    P.op(eng, fn, reads=[PSB[bk]], writes=[sg], add=(e > 0))
                P.dma("sp", vb_s[:, half * 4:(half + 1) * 4, :, :], sgv, reads=[sg], writes=[OUT_TK[kind][half]])
        return OUT_TK

    def attn_iter(it, kT_fn, v_fn, k_reads, q_ap, q_reads, ngroups, bufs, out_dst, out_tk, bias=None, pek=4):
        pT, accD, ofp, sbias, rl, ost = bufs["pT"], bufs["accD"], bufs["ofp"], bufs["sbias"], bufs["rl"], bufs["ost"]
        ob = 6
        ostb = ost[it % 2]

        def qk(j):
            s0 = (j % 3) * 2
            for u in range(2):
                P.op("pe", MM(ps[:, s0 + u, :], kT_fn(2 * j + u), q_ap, True, True),
                     reads=k_reads + q_reads, writes=[PSB[s0 + u]], signal=(u == 1))

        def softmax_pv(j):
            s0 = (j % 3) * 2
            pt = pT[j % 3]
            src = ps[:, s0:s0 + 2, :]
            if bias is not None:
                sbt = sbias[j % 2]
                P.op("dve", STT(sbt.ap, src, float(SCALE), bias[0][:, 2 * j:2 * j + 2, :], ALU.mult, ALU.add),
                     reads=[PSB[s0], PSB[s0 + 1], bias[1]], writes=[sbt])
                P.op("act", ACT(pt.ap, sbt.ap, AF.Exp), reads=[sbt], writes=[pt])
            else:
                P.op("act", ACT(pt.ap, src, AF.Exp, scale=float(SCALE)), reads=[PSB[s0], PSB[s0 + 1]], writes=[pt])
            for u in range(2):
                P.op("pe", MM(ps[:, ob, :], v_fn(2 * j + u), pt.ap[:, u, :], (j == 0 and u == 0), (j == ngroups - 1 and u == 1)),
                     reads=k_reads + [pt], writes=[PSB[ob]], signal=(u == 1))
            if j in pe_groups:
                for u in range(2):
                    P.op("pe", MM(ps[:, 7, :], onesb.ap, pt.ap[:, u, :], (j == pe_groups[0] and u == 0),
                                  (not dve_groups) and j == pe_groups[-1] and u == 1),
                         reads=[pt, onesb], writes=[PSB[7]], signal=(u == 1))
            elif j == dve_groups[0]:
                P.op("dve", CP(accD.ap, pt.ap), reads=[pt], writes=[accD])
            else:
                P.op("dve", TT(accD.ap, accD.ap, pt.ap, ALU.add), reads=[pt, accD], writes=[accD])

        pe_groups = [j for j in range(ngroups) if j % pek == pek - 1]
        dve_groups = [j for j in range(ngroups) if j % pek != pek - 1]
        qk(0)
        if ngroups > 1:
            qk(1)
        for j in range(ngroups):
            if j + 2 < ngroups:
                qk(j + 2)
            softmax_pv(j)
        P.op("dve", CP(ofp.ap, ps[:, ob, :]), reads=[PSB[ob]], writes=[ofp])
        if dve_groups:
            P.op("dve", TT(accD.ap[:, 0, :], accD.ap[:, 0, :], accD.ap[:, 1, :], ALU.add), reads=[accD], writes=[accD])
            P.op("pe", MM(ps[:, 7, :], ones.ap, accD.ap[:, 0, :], not pe_groups, True), reads=[accD, ones], writes=[PSB[7]])
        P.op("dve", RCP(rl.ap, ps[:, 7, :]), reads=[PSB[7]], writes=[rl])
        P.op("dve", TT(ostb.ap, ofp.ap, rl.ap, ALU.mult), reads=[ofp, rl], writes=[ostb])
        P.dma("sp", out_dst, ostb.ap, reads=[ostb], writes=[out_tk], add=True)

    def attn_bufs():
        return {
            "pT": [sb([2, 512], BF16) for _ in range(3)],
            "accD": sb([2, 512], F32), "ofp": sb([512], F32),
            "sbias": [sb([2, 512], F32) for _ in range(2)],
            "rl": sb([512], F32),
            "ost": [sb([512], BF16) for _ in range(2)],
        }

    MIX_TK = [Tk() for _ in range(16)]

    def phase3(KA_TK, VA_TK, QA_TK):
        reset(BASE)
        kT = sb([S], BF16)
        vv = sb([128, 128], BF16)
        qT = sb([4, TOK], BF16)
        bufs = attn_bufs()
        it = 0
        for kv in range(2):
            for c4 in range(4):
                P.dma("sp", kT.ap[:, c4 * 4096:(c4 + 1) * 4096], kaT_s[kv, :, c4 * 4096:(c4 + 1) * 4096],
                      reads=KA_TK[c4 * 8:(c4 + 1) * 8], writes=[kT], add=(c4 > 0))
                P.dma("sp", vv.ap[:, c4 * 32:(c4 + 1) * 32, :], va_s[:, kv, c4 * 32:(c4 + 1) * 32, :],
                      reads=VA_TK[c4 * 8:(c4 + 1) * 8], writes=[vv], add=(c4 > 0))
            P.dma("sp", qT.ap, qaT_s[kv * 4:(kv + 1) * 4].rearrange("h d t -> d h t"), reads=[QA_TK[kv]], writes=[qT])
            for g in range(4):
                h = kv * 4 + g
                for qb in range(4):
                    issue_weight_casts(3, extra=[("act", P.cnt["act"])] if P.cnt["act"] else ())
                    attn_iter(it, lambda kt: kT.ap[:, kt * 128:(kt + 1) * 128], lambda kt: vv.ap[:, kt, :], [kT, vv],
                              qT.ap[:, g, qb * 512:(qb + 1) * 512], [qT], 64, bufs,
                              mixT_s[h, :, qb * 512:(qb + 1) * 512], MIX_TK[h])
                    it += 1

    def phase4(OUT_TK):
        reset(BASE)
        issue_weight_casts(100)
        kT = [sb([ETOK], BF16) for _ in range(2)]
        vv = [sb([NE, 128], BF16) for _ in range(2)]
        qT = [sb([TOK], BF16) for _ in range(2)]
        bt = [sb([8, 512], F32) for _ in range(2)]
        bufs = attn_bufs()
        it = 0

        def load_bias(it_):
            h_, qb_ = it_ // 4, it_ % 4
            typ = 0 if qb_ == 0 else (2 if qb_ == 3 else 1)
            P.dma("sp", bt[it_ % 2].ap, biasT[typ, h_], writes=[bt[it_ % 2]])

        def load_head(h_):
            hb_ = h_ % 2
            P.dma("sp", kT[hb_].ap, kbT_s[h_], reads=[OUT_TK["kb"][h_ // 4]], writes=[kT[hb_]])
            P.dma("sp", vv[hb_].ap, vb_s[:, h_, :, :], reads=[OUT_TK["vb"][h_ // 4]], writes=[vv[hb_]])
            P.dma("sp", qT[hb_].ap, qbT_s[h_], reads=[OUT_TK["qb"][h_ // 4]], writes=[qT[hb_]])

        load_head(0)
        load_bias(0)
        for h in range(8):
            hb = h % 2
            for qb in range(4):
                if it + 1 < 32:
                    if qb == 3:
                        load_head(h + 1)
                    load_bias(it + 1)
                btt = bt[it % 2]
                attn_iter(it, lambda kt, hb=hb, qb=qb: kT[hb].ap[:, (4 * qb + kt) * 128:(4 * qb + kt + 1) * 128],
                          lambda kt, hb=hb, qb=qb: vv[hb].ap[:, 4 * qb + kt, :], [kT[hb], vv[hb]],
                          qT[hb].ap[:, qb * 512:(qb + 1) * 512], [qT[hb]], 4, bufs,
                          mixT_s[8 + h, :, qb * 512:(qb + 1) * 512], MIX_TK[8 + h], bias=(btt.ap, btt), pek=1)
                it += 1

    H1_TK = [Tk() for _ in range(NT)]
    H1N_TK = [Tk() for _ in range(4)]

    def phase5():
        reset(BASE)
        wo = sb([16, D], BF16)
        mixT = [sb([16, 512], BF16) for _ in range(1)]
        gpo = sb([D], F32)
        gpl = sb([D], F32)
        xt = [sb([D], F32) for _ in range(2)]
        tmp = sb([D], F32)
        h1 = [sb([D], F32) for _ in range(2)]
        xg = sb([D], BF16)
        junk = sb([D], BF16)
        hst = [sb([16, 512], BF16) for _ in range(2)]
        ss = [sb([1], F32) for _ in range(2)]
        r = [sb([1], F32) for _ in range(2)]
        ss1 = [sb([1], F32) for _ in range(2)]
        r1 = [sb([1], F32) for _ in range(2)]
        for n in range(4):
            P.dma("pool", wo.ap[:, :, n * 512:(n + 1) * 512], w_o[:, n * 512:(n + 1) * 512].rearrange("(c p) n -> p c n", p=128),
                  writes=[wo], add=(n > 0))
        P.dma("sp", gpo.ap, gains[1], writes=[gpo])
        P.dma("sp", gpl.ap, gains[2], writes=[gpl])

        def A(t):
            sbk, tt = t // 4, t % 4
            mx = mixT[0]
            b = t % 2
            g0 = 0 if t % 2 == 0 else 4
            if tt == 0:
                P.dma("sp", mx.ap, mixT_s[:, :, sbk * 512:(sbk + 1) * 512].rearrange("c d t -> d c t"), reads=MIX_TK, writes=[mx])
            P.dma("sp", xt[b].ap, x_ext[(t + 2) * 128:(t + 3) * 128, :], writes=[xt[b]])
            for n in range(4):
                for c in range(16):
                    P.op("pe", MM(ps[:, g0 + n, :], mx.ap[:, c, tt * 128:(tt + 1) * 128], wo.ap[:, c, n * 512:(n + 1) * 512], c == 0, c == 15),
                         reads=[mx, wo], writes=[PSB[g0 + n]], signal=(c == 15))

        def B(t):
            sbk, tt = t // 4, t % 4
            b = t % 2
            g0 = 0 if t % 2 == 0 else 4
            pall = ps[:, g0:g0 + 4, :]
            P.op("act", ACT(junk.ap.rearrange("p (a b) -> p a b", a=4), pall, AF.Square, accum_out=ss[b].ap[:, 0:1]),
                 reads=PSB[g0:g0 + 4], writes=[ss[b]])
            rstd_ops(ss[b], r[b], D)
            P.op("dve", STT(tmp.ap.rearrange("p (a b) -> p a b", a=4), pall, r[b].ap[:, 0:1], gpo.ap.rearrange("p (a b) -> p a b", a=4), ALU.mult, ALU.mult),
                 reads=PSB[g0:g0 + 4] + [r[b], gpo], writes=[tmp])
            P.op("dve", TT(h1[b].ap, tmp.ap, xt[b].ap, ALU.add), reads=[tmp, xt[b]], writes=[h1[b]])
            P.dma("sp", h1_s[t * 128:(t + 1) * 128, :], h1[b].ap, reads=[h1[b]], writes=[H1_TK[t]])
            norm_to_bf16(h1[b], gpl, junk, ss1[b], r1[b], xg)
            hs = hst[sbk % 2]
            transpose16(xg, lambda c0, c1, hs=hs, tt=tt: hs.ap[:, c0:c1, tt * 128:(tt + 1) * 128], [hs], banks=(g0, g0 + 1))
            if tt == 3:
                P.dma("sp", h1nT_s[sbk], hs.ap, reads=[hs], writes=[H1N_TK[sbk]])

        A(0)
        for t in range(NT):
            if t + 1 < NT:
                A(t + 1)
            B(t)

    M_TK = [Tk() for _ in range(4)]

    def phase6():
        reset(BASE)
        aT = sb([64, 512], BF16)
        AT_TK = [Tk() for _ in range(64)]
        hnT = [sb([16, 512], BF16) for _ in range(2)]
        wu = [sb([16, 512], BF16) for _ in range(2)]
        wd = [sb([4, 1024], BF16) for _ in range(3)]
        rt = [sb([512], F32) for _ in range(2)]
        mst = [sb([1024], F32) for _ in range(2)]
        k_up = 0
        k_dn = 0
        k_ms = 0
        for sbk in range(4):
            hn = hnT[sbk % 2]
            P.dma("sp", hn.ap, h1nT_s[sbk], reads=[H1N_TK[sbk]], writes=[hn])
            for fg in range(16):
                w = wu[k_up % 2]
                k_up += 1
                P.dma("sp", w.ap, wup_s[:, fg * 512:(fg + 1) * 512].rearrange("(c p) n -> p c n", p=128), reads=WUP_TK, writes=[w])
                for j in range(4):
                    fc = fg * 4 + j
                    bk = fc % 4
                    for c in range(16):
                        P.op("pe", MM(ps[:, bk, :], w.ap[:, c, j * 128:(j + 1) * 128], hn.ap[:, c, :], c == 0, c == 15),
                             reads=[w, hn], writes=[PSB[bk]], signal=(c == 15))
                    rtt = rt[fc % 2]
                    P.op("act", ACT(rtt.ap, ps[:, bk, :], AF.Relu), reads=[PSB[bk]], writes=[rtt])
                    P.op("dve", TT(aT.ap[:, fc, :], rtt.ap, rtt.ap, ALU.mult), reads=[rtt], writes=[AT_TK[fc]])
            for hf in range(2):
                for fgd in range(16):
                    w = wd[k_dn % 3]
                    k_dn += 1
                    P.dma("sp", w.ap, wdn_s[fgd * 512:(fgd + 1) * 512, hf * 1024:(hf + 1) * 1024].rearrange("(j p) n -> p j n", p=128),
                          reads=WDN_TK[fgd * 4:(fgd + 1) * 4], writes=[w])
                    for j in range(4):
                        fc = fgd * 4 + j
                        for tt in range(4):
                            for n in range(2):
                                bk = tt * 2 + n
                                last = (fc == 63)
                                P.op("pe", MM(ps[:, bk, :], aT.ap[:, fc, tt * 128:(tt + 1) * 128], w.ap[:, j, n * 512:(n + 1) * 512], fc == 0, last),
                                     reads=[AT_TK[fc], w], writes=[PSB[bk]], signal=(last or (j == 3 and tt == 3 and n == 1)))
                for tt in range(4):
                    ms = mst[k_ms % 2]
                    k_ms += 1
                    src = ps[:, tt * 2:tt * 2 + 2, :]
                    dst = ms.ap.rearrange("p (a b) -> p a b", a=2)
                    fn = ACT(dst, src, AF.Copy) if tt % 2 == 0 else CP(dst, src)
                    P.op("act" if tt % 2 == 0 else "dve", fn, reads=[PSB[tt * 2], PSB[tt * 2 + 1]], writes=[ms])
                    t = sbk * 4 + tt
                    P.dma("sp", m_s[t * 128:(t + 1) * 128, hf * 1024:(hf + 1) * 1024], ms.ap, reads=[ms], writes=[M_TK[sbk]], add=True)

    def phase7():
        reset(BASE)
        wg = sb([16, D], BF16)
        wp = sb([2, D], BF16)
        g3 = sb([D], F32)
        g4 = sb([D], F32)
        g5 = sb([D], F32)
        mt = [sb([D], F32) for _ in range(2)]
        h1 = [sb([D], F32) for _ in range(3)]
        pt = [sb([2, 128], BF16) for _ in range(3)]
        tmp = sb([D], F32)
        xg = sb([D], BF16)
        junk = sb([D], BF16)
        hnT = [sb([16, 128], BF16) for _ in range(2)]
        sg = [sb([1024], F32) for _ in range(2)]
        et = sb([D], F32)
        yt = [sb([D], F32) for _ in range(1)]
        ssA = [sb([1], F32) for _ in range(2)]
        rA = [sb([1], F32) for _ in range(2)]
        ssB = [sb([1], F32) for _ in range(2)]
        rB = [sb([1], F32) for _ in range(2)]
        ssC = [sb([1], F32) for _ in range(2)]
        rC = [sb([1], F32) for _ in range(2)]
        for n in range(4):
            P.dma("pool", wg.ap[:, :, n * 512:(n + 1) * 512], w_g[:, n * 512:(n + 1) * 512].rearrange("(c p) n -> p c n", p=128),
                  writes=[wg], add=(n > 0))
        P.dma("pool", wp.ap, w_p.rearrange("(c p) n -> p c n", p=128), writes=[wp])
        P.dma("sp", g3.ap, gains[3], writes=[g3])
        P.dma("sp", g4.ap, gains[4], writes=[g4])
        P.dma("sp", g5.ap, gains[5], writes=[g5])
        Y_TK = []

        def L(t):
            b = t % 2
            b3 = t % 3
            P.dma("sp", mt[b].ap, m_s[t * 128:(t + 1) * 128, :], reads=[M_TK[t // 4]], writes=[mt[b]])
            P.dma("sp", h1[b3].ap, h1_s[t * 128:(t + 1) * 128, :], reads=[H1_TK[t]], writes=[h1[b3]])
            P.dma("pool", pt[b3].ap, pT_in[:, t * 128:(t + 1) * 128].rearrange("(c p) t -> p c t", p=128), writes=[pt[b3]])

        def B1(t):
            b = t % 2
            b3 = t % 3
            P.op("act", ACT(junk.ap, mt[b].ap, AF.Square, accum_out=ssA[b].ap[:, 0:1]), reads=[mt[b]], writes=[ssA[b]])
            rstd_ops(ssA[b], rA[b], D)
            P.op("dve", STT(tmp.ap, mt[b].ap, rA[b].ap[:, 0:1], g3.ap, ALU.mult, ALU.mult), reads=[mt[b], rA[b], g3], writes=[tmp])
            P.op("dve", TT(h1[b3].ap, tmp.ap, h1[b3].ap, ALU.add), reads=[tmp, h1[b3]], writes=[h1[b3]])
            norm_to_bf16(h1[b3], g4, junk, ssB[b], rB[b], xg)

        def C(t):
            b = t % 2
            transpose16(xg, lambda c0, c1, b=b: hnT[b].ap[:, c0:c1, :], [hnT[b]], banks=(6, 7))

        def A(t):
            b = t % 2
            for hf in range(2):
                gb = hf * 2
                for n in range(2):
                    col = hf * 1024 + n * 512
                    for c in range(16):
                        P.op("pe", MM(ps[:, gb + n, :], hnT[b].ap[:, c, :], wg.ap[:, c, col:col + 512], c == 0, c == 15),
                             reads=[hnT[b], wg], writes=[PSB[gb + n]], signal=(c == 15))
                for n in range(2):
                    col = hf * 1024 + n * 512
                    for c in range(2):
                        P.op("pe", MM(ps[:, 4 + n, :], pt[t % 3].ap[:, c, :], wp.ap[:, c, col:col + 512], c == 0, c == 1),
                             reads=[pt[t % 3], wp], writes=[PSB[4 + n]], signal=(c == 1))
                sgt = sg[hf]
                P.op("act", ACT(sgt.ap.rearrange("p (a b) -> p a b", a=2), ps[:, gb:gb + 2, :], AF.Sigmoid),
                     reads=[PSB[gb], PSB[gb + 1]], writes=[sgt])
                P.op("dve", TT(et.ap[:, hf * 1024:(hf + 1) * 1024].rearrange("p (a b) -> p a b", a=2), ps[:, 4:6, :],
                               sgt.ap.rearrange("p (a b) -> p a b", a=2), ALU.mult),
                     reads=[PSB[4], PSB[5], sgt], writes=[et], add=(hf == 1))

        def B2(t):
            b = t % 2
            b3 = t % 3
            P.op("act", ACT(junk.ap, et.ap, AF.Square, accum_out=ssC[b].ap[:, 0:1]), reads=[et], writes=[ssC[b]])
            rstd_ops(ssC[b], rC[b], D)
            P.op("dve", STT(tmp.ap, et.ap, rC[b].ap[:, 0:1], g5.ap, ALU.mult, ALU.mult), reads=[et, rC[b], g5], writes=[tmp])
            P.op("dve", TT(yt[0].ap, tmp.ap, h1[b3].ap, ALU.add), reads=[tmp, h1[b3]], writes=[yt[0]])
            ytk = Tk()
            P.dma("pool", y_out[t * 128:(t + 1) * 128, :], yt[0].ap, reads=[yt[0]], writes=[ytk])
            Y_TK.append(ytk)

        L(0)
        L(1)
        B1(0)
        C(0)
        for t in range(NT):
            if t + 2 < NT:
                L(t + 2)
            if t + 1 < NT:
                B1(t + 1)
            A(t)
            if t + 1 < NT:
                C(t + 1)
            B2(t)
        return Y_TK

    KA_TK, VA_TK = phase1()
    P.barrier()
    OUT_TK = phase2()
    P.barrier()
    phase3(KA_TK, VA_TK, OUT_TK["qa"])
    P.barrier()
    phase4(OUT_TK)
    P.barrier()
    phase5()
    P.barrier()
    phase6()
    P.barrier()
    Y_TK = phase7()
    P.barrier()
    P.op("sp", lambda e: e.nop(), signal=False)
    for e in ("pe", "act", "dve", "pool"):
        P.bar[e] = []

    sem_names = ["pe", "act", "dve", "pool"] + ["d_sp_%d" % i for i in range(P.ring["sp"])] + ["d_pool_%d" % i for i in range(P.ring["pool"])]
    sem_cms = [nc.semaphore(n) for n in sem_names]
    sems = {n: cm.__enter__() for n, cm in zip(sem_names, sem_cms)}

    ref = {k: set() for k in ("pe", "act", "dve", "pool")}
    for name in ENGS:
        for ws, fn, sig in P.q[name]:
            for k, v in ws:
                if k in ref:
                    ref[k].add(v)
    remap = {k: {v: i + 1 for i, v in enumerate(sorted(vs))} for k, vs in ref.items()}

    def replay(name, e):
        c = 0
        for ws, fn, sig in P.q[name]:
            for k, v in ws:
                e.wait_ge(sems[k], remap[k][v] if k in remap else v)
            ins = fn(e)
            if sig is not None:
                if sig[0] in remap:
                    c += 1
                    if c in ref[sig[0]]:
                        ins.then_inc(sems[sig[0]], 1)
                else:
                    ins.then_inc(sems[sig[0]], sig[1])

    with nc.Block() as block:
        @block.tensor
        def _(e):
            replay("pe", e)

        @block.scalar
        def _(e):
            replay("act", e)

        @block.vector
        def _(e):
            replay("dve", e)

        @block.gpsimd
        def _(e):
            replay("pool", e)

        @block.sync
        def _(e):
            replay("sp", e)

    for cm in reversed(sem_cms):
        cm.__exit__(None, None, None)
    psum_cm.__exit__(None, None, None)
    arena_cm.__exit__(None, None, None)
    return nc


DEBUG_OUTS = ()


def _rope_tables():
    t = np.arange(S)
    row = (t // 64).astype(np.float32)
    col = (t % 64).astype(np.float32)
    freqs = (np.float32(10000.0) ** (-np.arange(32, dtype=np.float32) / np.float32(32))).astype(np.float32)
    ar = row[:, None] * freqs[None, :]
    ac = col[:, None] * freqs[None, :]
    cos = np.concatenate([np.cos(ar), np.cos(ar), np.cos(ac), np.cos(ac)], axis=1).astype(np.float32)
    sin = np.concatenate([-np.sin(ar), np.sin(ar), -np.sin(ac), np.sin(ac)], axis=1).astype(np.float32)
    return cos, sin


def _bias_tiles(rpb, core):
    out = np.full((3, 8, 8 * 128, 512), NEG, dtype=np.float32)
    for typ, qb in ((0, 0), (1, 1), (2, 3)):
        R = 32 * core + 8 * qb
        q = np.arange(512)
        r = R + q // 64
        c = q % 64
        r0 = np.clip(r - 4, 0, 256 - 8)
        c0 = np.clip(c - 8, 0, 64 - 16)
        k = np.arange(1024)
        kr = (R - 4) + k // 64
        kc = k % 64
        inr = (kr[:, None] >= r0[None, :]) & (kr[:, None] < r0[None, :] + 8)
        inc = (kc[:, None] >= c0[None, :]) & (kc[:, None] < c0[None, :] + 16)
        ok = inr & inc
        dr = np.clip(kr[:, None] - r[None, :] + 7, 0, 14)
        dc = np.clip(kc[:, None] - c[None, :] + 15, 0, 30)
        for h in range(8):
            g = rpb[h][dr, dc]
            out[typ, h] = np.where(ok, g, np.float32(NEG))
    return np.ascontiguousarray(out.reshape(3, 8, 8, 128, 512).transpose(0, 1, 3, 2, 4))


def _prepare_inputs(x, p, pre_mix_norm, w_in, q_norm, k_norm, rel_pos_bias, w_o, post_mix_norm, pre_mlp_norm,
                    w_up, w_down, post_mlp_norm, pre_ple_norm, w_ple_gate, w_ple_proj, post_ple_norm):
    f = lambda a: np.ascontiguousarray(np.asarray(a, dtype=np.float32))
    x2 = f(x)[0]
    p2 = f(p)[0, 0]
    gains = np.stack([np.broadcast_to(f(g)[0][None, :], (128, D)) for g in
                      (pre_mix_norm, post_mix_norm, pre_mlp_norm, post_mlp_norm, pre_ple_norm, post_ple_norm)])
    gains = np.ascontiguousarray(gains)
    gainsT = np.ascontiguousarray(np.stack([f(g)[0].reshape(16, 128).T for g in
                                            (pre_mix_norm, post_mix_norm, pre_mlp_norm, post_mlp_norm, pre_ple_norm, post_ple_norm)]))
    gq = np.ascontiguousarray(np.broadcast_to(np.tile(f(q_norm)[0], 4)[None, :], (128, 512)))
    gk = np.ascontiguousarray(np.broadcast_to(np.tile(f(k_norm)[0], 2)[None, :], (128, 256)))
    cos, sin = _rope_tables()
    rpb = f(rel_pos_bias)[0]
    ident = np.eye(128, dtype=np.float32)
    shared = {
        "x_all": x2, "w_in": f(w_in)[0], "w_o": f(w_o)[0], "w_up": f(w_up)[0], "w_down": f(w_down)[0],
        "w_gate": f(w_ple_gate)[0], "w_proj": f(w_ple_proj)[0], "gains": gains, "gainsT": gainsT, "gq": gq, "gk": gk,
        "cos_all": cos, "sin_all": sin, "ident": ident,
    }
    in_maps = []
    xpad = np.concatenate([np.zeros((256, D), np.float32), x2, np.zeros((256, D), np.float32)], axis=0)
    for c in range(NCORES):
        m = dict(shared)
        m["x_ext"] = np.ascontiguousarray(xpad[c * TOK:c * TOK + ETOK])
        m["pT"] = np.ascontiguousarray(p2[c * TOK:(c + 1) * TOK].T)
        m["cos_own"] = np.ascontiguousarray(cos[c * TOK:(c + 1) * TOK])
        m["sin_own"] = np.ascontiguousarray(sin[c * TOK:(c + 1) * TOK])
        m["biasT"] = _bias_tiles(rpb, c)
        in_maps.append(m)
    return in_maps


_NC_CACHE = {}


def kernel(**inputs):
    in_maps = _prepare_inputs(**inputs)
    if "nc" not in _NC_CACHE:
        _NC_CACHE["nc"] = build_program()
    nc = _NC_CACHE["nc"]
    res = run_bass_kernel_spmd(nc, in_maps, core_ids=list(range(NCORES)))
    out = np.concatenate([np.asarray(r["y"], dtype=np.float32) for r in res.results], axis=0)
    if DEBUG:
        kernel.last_results = res.results
    return out.reshape(1, S, D)
```
